# Optimizing a Trainium2 kernel written in Bass

```python
import math
import jax, jax.numpy as jnp
from jax import lax
import numpy as np

D_MODEL = 1024
BATCH = 32
SEQ = 2048
DEPTH = 2

N_MIXERS = 2
N_A_LAYERS = (DEPTH + 1) // 2
N_B_LAYERS = DEPTH // 2
N_META = 16
D_RNN = D_MODEL
LRU_BLOCKS = 8
LRU_BLOCK_W = D_RNN // LRU_BLOCKS
LRU_CONV_W = 4
LRU_C = 8.0
CONF_KERNEL = 31
PEER_HEADS = 8
PEER_N_KEYS = 128
PEER_N_EXPERTS = PEER_N_KEYS * PEER_N_KEYS
PEER_QUERY_DIM = 256
PEER_HALF = PEER_QUERY_DIM // 2
PEER_TOPK = 16
PEER_CHUNK = 256
DEEPNORM_ALPHA = (2.0 * DEPTH) ** 0.25
DEEPNORM_BETA = (8.0 * DEPTH) ** -0.25
LN_EPS = 1e-5

kernel_name = "hybrid_rglru_conformer_peer_encoder"


def layer_norm(x, g, b):
    xf = x.astype(jnp.float32)
    mu = jnp.mean(xf, axis=-1, keepdims=True)
    var = jnp.mean(jnp.square(xf - mu), axis=-1, keepdims=True)
    y = (xf - mu) * lax.rsqrt(var + LN_EPS)
    return (y * g.astype(jnp.float32) + b.astype(jnp.float32)).astype(x.dtype)


def depthwise_conv(x, w, b, pad_left, pad_right):
    k, c = w.shape
    y = lax.conv_general_dilated(
        x, w.reshape(k, 1, c).astype(x.dtype), window_strides=(1,),
        padding=[(pad_left, pad_right)], dimension_numbers=('NWC', 'WIO', 'NWC'),
        feature_group_count=c)
    return y + b.astype(x.dtype)


def rglru_coeffs(u, w_a, b_a, w_x, b_x, lam):
    bsz, s, _ = u.shape
    ub = u.reshape(bsz, s, LRU_BLOCKS, LRU_BLOCK_W)
    r = jax.nn.sigmoid(jnp.einsum('bsnc,ncd->bsnd', ub, w_a).reshape(bsz, s, D_RNN) + b_a)
    i = jax.nn.sigmoid(jnp.einsum('bsnc,ncd->bsnd', ub, w_x).reshape(bsz, s, D_RNN) + b_x)
    log_a = (-LRU_C * r.astype(jnp.float32)) * jax.nn.softplus(-lam.astype(jnp.float32))
    a = jnp.exp(log_a)
    mult = jnp.sqrt(-jnp.expm1(2.0 * log_a))
    return a, mult * (i * u).astype(jnp.float32)


def linear_scan(a, b):
    def combine(c1, c2):
        a1, b1 = c1
        a2, b2 = c2
        return a1 * a2, a2 * b1 + b2
    _, h = lax.associative_scan(combine, (a, b), axis=1)
    return h


def rglru_mixer(x, w_in, conv_w, conv_b, w_a, b_a, w_x, b_x, lam, w_out):
    gate, u = jnp.split(x @ w_in, 2, axis=-1)
    pl = LRU_CONV_W // 2
    u = depthwise_conv(u, conv_w, conv_b, pl, LRU_CONV_W - 1 - pl)
    a_f, b_f = rglru_coeffs(u, w_a[0], b_a[0], w_x[0], b_x[0], lam[0])
    a_r, b_r = rglru_coeffs(u, w_a[1], b_a[1], w_x[1], b_x[1], lam[1])
    h_f = linear_scan(a_f, b_f)
    h_r = jnp.flip(linear_scan(jnp.flip(a_r, axis=1), jnp.flip(b_r, axis=1)), axis=1)
    h = (h_f + h_r).astype(x.dtype)
    return (h * jax.nn.gelu(gate)) @ w_out


def conformer_conv_mixer(x, w_pw1, b_pw1, dw_w, dw_b, ln_g, ln_b, w_pw2, b_pw2):
    h = jax.nn.glu(x @ w_pw1 + b_pw1, axis=-1)
    h = depthwise_conv(h, dw_w, dw_b, CONF_KERNEL // 2, CONF_KERNEL // 2)
    h = jax.nn.silu(layer_norm(h, ln_g, ln_b))
    return h @ w_pw2 + b_pw2


def peer_block(xc, w_query, sub_keys, expert_u, expert_v):
    c = xc.shape[0]
    q = (xc @ w_query).reshape(c, PEER_HEADS, 2, PEER_HALF)
    scores = jnp.einsum('chpk,pnk->chpn', q, sub_keys.astype(q.dtype))
    s1, i1 = lax.top_k(scores[:, :, 0], PEER_TOPK)
    s2, i2 = lax.top_k(scores[:, :, 1], PEER_TOPK)
    cand = (s1[..., :, None] + s2[..., None, :]).reshape(c, PEER_HEADS, PEER_TOPK * PEER_TOPK)
    s, flat = lax.top_k(cand, PEER_TOPK)
    idx1 = jnp.take_along_axis(i1, flat // PEER_TOPK, axis=-1)
    idx2 = jnp.take_along_axis(i2, flat % PEER_TOPK, axis=-1)
    expert = idx1 * PEER_N_KEYS + idx2
    g = jax.nn.softmax(s.astype(jnp.float32), axis=-1).astype(xc.dtype)
    u = expert_u[expert]
    act = jax.nn.gelu(jnp.einsum('chkd,cd->chk', u, xc))
    v = expert_v[expert]
    return jnp.einsum('chk,chkd->cd', g * act, v)


def peer_ffn(x, w_query, sub_keys, expert_u, expert_v):
    bsz, s, d = x.shape
    t = bsz * s
    n_blocks = -(-t // PEER_CHUNK)
    xt = jnp.pad(x.reshape(t, d), ((0, n_blocks * PEER_CHUNK - t), (0, 0)))
    xt = xt.reshape(n_blocks, PEER_CHUNK, d)
    y = lax.map(lambda xc: peer_block(xc, w_query, sub_keys, expert_u, expert_v), xt)
    return y.reshape(n_blocks * PEER_CHUNK, d)[:t].reshape(bsz, s, d)


def setup_inputs(seed: int = 0) -> dict:
    key = jax.random.key(seed)
    ks = jax.random.split(key, 32)
    f32 = jnp.float32
    nrm = lambda k, shape, scale: jax.random.normal(k, shape, f32) * scale
    a_init = jax.random.uniform(ks[8], (N_A_LAYERS, 2, D_RNN), f32, 0.9, 0.999)
    base = a_init ** (1.0 / LRU_C)
    lam = jnp.log(base) - jnp.log1p(-base)
    return {
        "x": nrm(ks[0], (BATCH, SEQ, D_MODEL), 1.0),
        "meta_tokens": nrm(ks[1], (N_META, D_MODEL), 1.0),
        "lru_w_in": nrm(ks[2], (N_A_LAYERS, D_MODEL, 2 * D_RNN), D_MODEL ** -0.5),
        "lru_conv_w": nrm(ks[3], (N_A_LAYERS, LRU_CONV_W, D_RNN), LRU_CONV_W ** -0.5),
        "lru_conv_b": nrm(ks[4], (N_A_LAYERS, D_RNN), 0.01),
        "lru_w_a": nrm(ks[5], (N_A_LAYERS, 2, LRU_BLOCKS, LRU_BLOCK_W, LRU_BLOCK_W), LRU_BLOCK_W ** -0.5),
        "lru_b_a": nrm(ks[6], (N_A_LAYERS, 2, D_RNN), 0.01),
        "lru_w_x": nrm(ks[7], (N_A_LAYERS, 2, LRU_BLOCKS, LRU_BLOCK_W, LRU_BLOCK_W), LRU_BLOCK_W ** -0.5),
        "lru_b_x": nrm(ks[9], (N_A_LAYERS, 2, D_RNN), 0.01),
        "lru_lambda": lam,
        "lru_w_out": nrm(ks[10], (N_A_LAYERS, D_RNN, D_MODEL), DEEPNORM_BETA * D_RNN ** -0.5),
        "conf_w_pw1": nrm(ks[11], (N_B_LAYERS, D_MODEL, 2 * D_MODEL), D_MODEL ** -0.5),
        "conf_b_pw1": nrm(ks[12], (N_B_LAYERS, 2 * D_MODEL), 0.01),
        "conf_dw_w": nrm(ks[13], (N_B_LAYERS, CONF_KERNEL, D_MODEL), CONF_KERNEL ** -0.5),
        "conf_dw_b": nrm(ks[14], (N_B_LAYERS, D_MODEL), 0.01),
        "conf_ln_g": 1.0 + nrm(ks[15], (N_B_LAYERS, D_MODEL), 0.01),
        "conf_ln_b": nrm(ks[16], (N_B_LAYERS, D_MODEL), 0.01),
        "conf_w_pw2": nrm(ks[17], (N_B_LAYERS, D_MODEL, D_MODEL), DEEPNORM_BETA * D_MODEL ** -0.5),
        "conf_b_pw2": nrm(ks[18], (N_B_LAYERS, D_MODEL), 0.01),
        "peer_w_query": nrm(ks[19], (DEPTH, D_MODEL, PEER_HEADS * PEER_QUERY_DIM), D_MODEL ** -0.5),
        "peer_sub_keys": nrm(ks[20], (DEPTH, 2, PEER_N_KEYS, PEER_HALF), PEER_HALF ** -0.5),
        "peer_u": nrm(ks[21], (DEPTH, PEER_N_EXPERTS, D_MODEL), D_MODEL ** -0.5),
        "peer_v": nrm(ks[22], (DEPTH, PEER_N_EXPERTS, D_MODEL), DEEPNORM_BETA * PEER_HEADS ** -0.5),
        "ln_mix_g": 1.0 + nrm(ks[23], (DEPTH, D_MODEL), 0.01),
        "ln_mix_b": nrm(ks[24], (DEPTH, D_MODEL), 0.01),
        "ln_ffn_g": 1.0 + nrm(ks[25], (DEPTH, D_MODEL), 0.01),
        "ln_ffn_b": nrm(ks[26], (DEPTH, D_MODEL), 0.01),
    }


def reference(x, meta_tokens, lru_w_in, lru_conv_w, lru_conv_b, lru_w_a, lru_b_a, lru_w_x,
              lru_b_x, lru_lambda, lru_w_out, conf_w_pw1, conf_b_pw1, conf_dw_w, conf_dw_b,
              conf_ln_g, conf_ln_b, conf_w_pw2, conf_b_pw2, peer_w_query, peer_sub_keys,
              peer_u, peer_v, ln_mix_g, ln_mix_b, ln_ffn_g, ln_ffn_b):
    bsz = x.shape[0]
    meta = jnp.broadcast_to(meta_tokens.astype(x.dtype)[None], (bsz, N_META, D_MODEL))
    h = jnp.concatenate([meta, x], axis=1)
    for i in range(DEPTH):
        j = i // N_MIXERS
        if i % N_MIXERS == 0:
            m = rglru_mixer(h, lru_w_in[j], lru_conv_w[j], lru_conv_b[j], lru_w_a[j], lru_b_a[j],
                            lru_w_x[j], lru_b_x[j], lru_lambda[j], lru_w_out[j])
        else:
            m = conformer_conv_mixer(h, conf_w_pw1[j], conf_b_pw1[j], conf_dw_w[j], conf_dw_b[j],
                                     conf_ln_g[j], conf_ln_b[j], conf_w_pw2[j], conf_b_pw2[j])
        h = layer_norm(DEEPNORM_ALPHA * h + m, ln_mix_g[i], ln_mix_b[i])
        f = peer_ffn(h, peer_w_query[i], peer_sub_keys[i], peer_u[i], peer_v[i])
        h = layer_norm(DEEPNORM_ALPHA * h + f, ln_ffn_g[i], ln_ffn_b[i])
    return h[:, N_META:]
```

```python
import contextlib
import types
import numpy as np
import concourse.bass as bass
import concourse.mybir as mybir
from concourse.bass_utils import run_bass_kernel_spmd

F32 = mybir.dt.float32
BF16 = mybir.dt.bfloat16
ALU = mybir.AluOpType
AF = mybir.ActivationFunctionType
AX = mybir.AxisListType

D = 1024
SEQ = 2048
NMETA = 16
L = SEQ + NMETA
NCORES = 8
ALPHA = float(4.0 ** 0.25)
EPS = 1e-5
NKEY = 128
NEXP = NKEY * NKEY
GELU_K = 1.5957691216057308

PF = {}
_c = 0
for _n, _w in (("conv_w", 32), ("conv_b", 8), ("b_a", 16), ("b_x", 16), ("lam", 16), ("b_pw1", 16),
               ("dw_w", 248), ("dw_b", 8), ("cln_g", 8), ("cln_b", 8)):
    PF[_n] = _c
    _c += _w
PF_COLS = _c
PT = {"mix_g0": 0, "mix_b0": 1, "ffn_g0": 2, "ffn_b0": 3, "mix_g1": 4, "mix_b1": 5, "ffn_g1": 6, "ffn_b1": 7, "b_pw2": 8}


def freeze(fn):
    if fn.__closure__ is None:
        return fn
    cells = []
    for c in fn.__closure__:
        try:
            cells.append(types.CellType(c.cell_contents))
        except ValueError:
            cells.append(c)
    g = types.FunctionType(fn.__code__, fn.__globals__, fn.__name__, fn.__defaults__, tuple(cells))
    g.__kwdefaults__ = fn.__kwdefaults__
    return g


class Buf:
    __slots__ = ("name", "w", "r")

    def __init__(self, name=""):
        self.name = name
        self.w = None
        self.r = []


class DSem:
    __slots__ = ("h", "count")

    def __init__(self, h):
        self.h = h
        self.count = 0


class Eng:
    def __init__(self, name, sem):
        self.name = name
        self.sem = sem
        self.ops = []
        self.seen = {}


class Prog:
    def __init__(self, nc, stack):
        self.nc = nc
        self.stack = stack
        self.engs = {}
        self.nblk = 0
        for n in ("pe", "act", "dve", "pool", "sp"):
            self.engs[n] = Eng(n, None)
        self._new_sems()
        self.nops = 0

    def _new_sems(self):
        for n, e in self.engs.items():
            e.sem = DSem(self.stack.enter_context(self.nc.semaphore("s_%s_%d" % (n, self.nblk))))
            e.seen = {}

    def dsem(self, name):
        return DSem(self.stack.enter_context(self.nc.semaphore(name)))

    def emit(self, eng, fn, reads=(), writes=(), dsem=None):
        e = self.engs[eng]
        deps = {}

        def dep(sig):
            s, v = sig
            if deps.get(s, 0) < v:
                deps[s] = v

        for b in reads:
            if b.w is not None:
                dep(b.w)
        for b in writes:
            if b.w is not None:
                dep(b.w)
            for r in b.r:
                dep(r)
        if dsem is not None and dsem.count > 0:
            dep((dsem, dsem.count))
        waits = []
        for s, v in deps.items():
            if e.seen.get(s, 0) < v:
                e.seen[s] = v
                waits.append((s.h, v))
        if dsem is not None:
            dsem.count += 16
            sig = (dsem, dsem.count)
            inc = 16
        else:
            e.sem.count += 1
            sig = (e.sem, e.sem.count)
            inc = 1
        for b in reads:
            b.r.append(sig)
        for b in writes:
            b.w = sig
            b.r = []
        e.ops.append((waits, freeze(fn), sig[0].h, inc))
        self.nops += 1
        return sig

    def wait_all(self, eng, sigs):
        e = self.engs[eng]
        for s, v in sigs:
            if e.seen.get(s, 0) < v:
                e.seen[s] = v
                e.ops.append(([(s.h, v)], None, None, 0))

    def flush_block(self, bufs=()):
        nc = self.nc
        engs = self.engs

        def replay(e, h):
            for waits, fn, sh, inc in e.ops:
                for s, v in waits:
                    h.wait_ge(s, v)
                if fn is not None:
                    fn(h).then_inc(sh, inc)
            e.ops = []

        with nc.Block() as block:
            @block.tensor
            def _(h):
                replay(engs["pe"], h)

            @block.scalar
            def _(h):
                replay(engs["act"], h)

            @block.vector
            def _(h):
                replay(engs["dve"], h)

            @block.gpsimd
            def _(h):
                replay(engs["pool"], h)

            @block.sync
            def _(h):
                replay(engs["sp"], h)
        self.nblk += 1
        self._new_sems()
        for b in bufs:
            b.w = None
            b.r = []


class Ctx:
    pass


def pieces(n, step=512):
    return [(s, min(step, n - s)) for s in range(0, n, step)]


def pos_tiles():
    return [(s, min(128, L - s)) for s in range(0, L, 128)]


class PsumPool:
    def __init__(self, nc, st, n, shape, dtype, name):
        self.t = [st.enter_context(nc.psum_tensor("%s%d" % (name, i), shape, dtype)) for i in range(n)]
        self.b = [Buf("%s%d" % (name, i)) for i in range(n)]
        self.i = 0

    def get(self):
        i = self.i
        self.i = (i + 1) % len(self.t)
        return self.t[i], self.b[i]


def load_seq_tile(P, C, eng, dst, dbuf, dsem, s, p0, n):
    sigs = []
    if p0 < NMETA:
        m = min(NMETA - p0, n)
        P.emit(eng, lambda h: h.dma_start(out=dst[0:m, :], in_=C.meta[p0:p0 + m, :]), writes=[dbuf], dsem=dsem)
        if n > m:
            P.emit(eng, lambda h: h.dma_start(out=dst[m:n, :], in_=C.x[s, 0:n - m, :]), writes=[dbuf], dsem=dsem)
    else:
        P.emit(eng, lambda h: h.dma_start(out=dst[0:n, :], in_=C.x[s, p0 - NMETA:p0 - NMETA + n, :]), writes=[dbuf], dsem=dsem)


def layer_norm_tile(P, C, gbB, z, zb, n, g_ap, b_ap, out, outb, tmp):
    stats, sb = tmp["stats"], tmp["statsb"]
    for k in range(2):
        P.emit("dve", lambda h, k=k: h.bn_stats(out=stats[0:n, k, :], in_=z[0:n, k * 512:(k + 1) * 512]), reads=[zb], writes=[sb])
    mv, mvb = tmp["mv"], tmp["mvb"]
    P.emit("dve", lambda h: h.bn_aggr(out=mv[0:n, :], in_=stats[0:n, :, :]), reads=[sb], writes=[mvb])
    rs, rsb = tmp["rs"], tmp["rsb"]
    P.emit("act", lambda h: h.activation(out=rs[0:n, :], in_=mv[0:n, 1:2], func=AF.Sqrt, bias=tmp["eps"][0:n, :], scale=1.0), reads=[mvb], writes=[rsb])
    P.emit("dve", lambda h: h.reciprocal(out=rs[0:n, :], in_=rs[0:n, :]), reads=[rsb], writes=[rsb])
    P.emit("dve", lambda h: h.tensor_scalar(out=out[0:n, :], in0=z[0:n, :], scalar1=mv[0:n, 0:1], scalar2=rs[0:n, 0:1],
                                            op0=ALU.subtract, op1=ALU.mult), reads=[zb, mvb, rsb], writes=[outb])
    P.emit("pool", lambda h: h.tensor_tensor(out=out[0:n, :], in0=out[0:n, :], in1=g_ap[0:n, :], op=ALU.mult), reads=[outb, gbB], writes=[outb])
    P.emit("pool", lambda h: h.tensor_tensor(out=out[0:n, :], in0=out[0:n, :], in1=b_ap[0:n, :], op=ALU.add), reads=[outb, gbB], writes=[outb])


def alloc_ln_tmp(nc, st, P):
    t = {}
    t["stats"] = st.enter_context(nc.sbuf_tensor("ln_stats", [128, 2, 6], F32))
    t["statsb"] = Buf("ln_stats")
    t["mv"] = st.enter_context(nc.sbuf_tensor("ln_mv", [128, 2], F32))
    t["mvb"] = Buf("ln_mv")
    t["rs"] = st.enter_context(nc.sbuf_tensor("ln_rs", [128, 1], F32))
    t["rsb"] = Buf("ln_rs")
    t["eps"] = st.enter_context(nc.sbuf_tensor("ln_eps", [128, 1], F32))
    P.emit("pool", lambda h: h.memset(t["eps"][:], EPS), writes=[Buf()])
    return t


def make_ident(P, nc, st):
    idf = st.enter_context(nc.sbuf_tensor("identf", [128, 128], F32))
    idb = st.enter_context(nc.sbuf_tensor("identb", [128, 128], BF16))
    B = Buf("ident")
    P.emit("pool", lambda h: h.memset(idf[:], 1.0), writes=[B])
    P.emit("pool", lambda h: h.affine_select(out=idf[:], in_=idf[:], pattern=[[-1, 128]], base=0, channel_multiplier=1,
                                              compare_op=ALU.is_equal, fill=0.0), reads=[B], writes=[B])
    P.emit("pool", lambda h: h.tensor_copy(out=idb[:], in_=idf[:]), reads=[B], writes=[B])
    return idf, idb, B


def load_tokens_T(P, C, nc, hT, hTb, xt, xtb, xsem, idf, idB, pp, loader, tiles):
    for (c0, n, args) in tiles:
        loader(xt, xtb, xsem, n, *args)
        for half in range(2):
            pt, pb = pp.get()
            for j in range(4):
                kc = half * 4 + j
                P.emit("pe", lambda h, kc=kc, j=j, pt=pt, n=n: h.transpose(out=pt[:, j * 128:j * 128 + n], in_=xt[0:n, kc * 128:(kc + 1) * 128],
                                                                         identity=idf[0:n, 0:n]), reads=[xtb, idB], writes=[pb])
            e = "act" if half == 0 else "dve"
            if e == "act":
                P.emit("act", lambda h, half=half, pt=pt, n=n, c0=c0: h.activation(
                    out=hT[:, half * 4:half * 4 + 4, c0:c0 + n], in_=pt[:, :].rearrange("p (j t) -> p j t", j=4)[:, :, 0:n], func=AF.Copy),
                    reads=[pb], writes=[hTb])
            else:
                P.emit("dve", lambda h, half=half, pt=pt, n=n, c0=c0: h.tensor_copy(
                    out=hT[:, half * 4:half * 4 + 4, c0:c0 + n], in_=pt[:, :].rearrange("p (j t) -> p j t", j=4)[:, :, 0:n]),
                    reads=[pb], writes=[hTb])


def phase_rglru(P, C, nc, nseq):
    with contextlib.ExitStack() as st:
        sb = lambda name, shape, dt: st.enter_context(nc.sbuf_tensor("a_" + name, shape, dt))
        idf, idb, idB = make_ident(P, nc, st)
        lnt = alloc_ln_tmp(nc, st, P)
        pp = PsumPool(nc, st, 8, [128, 512], F32, "psA")
        pf = sb("pf", [128, PF_COLS], F32)
        pfB = Buf("pf")
        sem_c = P.dsem("semc_a")
        P.emit("sp", lambda h: h.dma_start(out=pf[:], in_=C.pf), writes=[pfB], dsem=sem_c)
        wout = sb("wout", [128, 8, D], BF16)
        woutB = Buf("wout")
        sem_wo = P.dsem("sem_wo")
        for kc in range(8):
            P.emit("pool", lambda h, kc=kc: h.dma_start(out=wout[:, kc, :], in_=C.w_out[kc * 128:(kc + 1) * 128, :]), writes=[woutB], dsem=sem_wo)
        wga = sb("wga", [128, 16, 128], BF16)
        wgx = sb("wgx", [128, 16, 128], BF16)
        wgB = Buf("wg")
        sem_wg = P.dsem("sem_wg")
        P.emit("pool", lambda h: h.dma_start(out=wga[:], in_=C.w_a.rearrange("r n c d -> c (r n) d")), writes=[wgB], dsem=sem_wg)
        P.emit("pool", lambda h: h.dma_start(out=wgx[:], in_=C.w_x.rearrange("r n c d -> c (r n) d")), writes=[wgB], dsem=sem_wg)
        gt = sb("lng", [128, 2, D], F32)
        gtB = Buf("lng")
        sem_g = P.dsem("sem_lng")
        P.emit("sp", lambda h: h.dma_start(out=gt[:, 0, :], in_=C.pt[PT["mix_g0"]:PT["mix_g0"] + 1, :].partition_broadcast(128)), writes=[gtB], dsem=sem_g)
        P.emit("sp", lambda h: h.dma_start(out=gt[:, 1, :], in_=C.pt[PT["mix_b0"]:PT["mix_b0"] + 1, :].partition_broadcast(128)), writes=[gtB], dsem=sem_g)
        cl = sb("cl", [128, 16], F32)
        clB = Buf("cl")
        lam = pf[:, PF["lam"]:PF["lam"] + 16]
        P.emit("act", lambda h: h.activation(out=cl[:], in_=lam, func=AF.Exp, scale=-1.0), reads=[pfB], writes=[clB])
        P.emit("act", lambda h: h.activation(out=cl[:], in_=cl[:], func=AF.Ln, bias=1.0, scale=1.0), reads=[clB], writes=[clB])
        P.emit("dve", lambda h: h.tensor_scalar(out=cl[:], in0=cl[:], scalar1=-8.0, scalar2=None, op0=ALU.mult), reads=[clB], writes=[clB])

        hT = sb("hT", [128, 8, L], BF16)
        hTB = Buf("hT")
        Y = sb("Y", [128, 8, L], BF16)
        YB = Buf("Y")
        xt = sb("xt", [128, D], F32)
        xtB = Buf("xt")
        sem_x = P.dsem("sem_x")
        win = [sb("win%d" % i, [128, 8, 256], BF16) for i in range(2)]
        winB = [Buf("win%d" % i) for i in range(2)]
        sem_win = [P.dsem("sem_win%d" % i) for i in range(2)]
        gg = sb("gg", [128, L], F32); ggB = Buf("gg")
        upad = sb("upad", [128, L + 3], F32); upB = Buf("upad")
        uc = sb("uc", [128, L], F32); ucB = Buf("uc")
        ucb = sb("ucb", [128, L], BF16); ucbB = Buf("ucb")
        ab = sb("ab", [128, L], F32); abB = Buf("ab")
        bb = sb("bb", [128, L], F32); bbB = Buf("bb")
        tm = sb("tm", [128, L], F32); tmB = Buf("tm")
        hf = sb("hf", [128, L], F32); hfB = Buf("hf")
        z = sb("z", [128, D], F32); zB = Buf("z")
        zo = sb("zo", [128, D], F32); zoB = Buf("zo")
        sem_o = P.dsem("sem_oa")
        P.emit("pool", lambda h: h.memset(upad[:], 0.0), writes=[upB])
        pcs = pieces(L)
        wcount = 0
        for s in range(nseq):
            def loader(xt_, xtb_, sem_, n, p0, s=s):
                load_seq_tile(P, C, "sp", xt_, xtb_, sem_, s, p0, n)
            load_tokens_T(P, C, nc, hT, hTB, xt, xtB, sem_x, idf, idB, pp, loader, [(p0, n, (p0,)) for (p0, n) in pos_tiles()])
            for c in range(8):
                wi = wcount % 2
                wcount += 1
                w = win[wi]
                P.emit("pool", lambda h, w=w, c=c: h.dma_start(out=w[:, :, 0:128], in_=C.w_in[:, c * 128:(c + 1) * 128].rearrange("(kc p) n -> p kc n", p=128)),
                       writes=[winB[wi]], dsem=sem_win[wi])
                P.emit("pool", lambda h, w=w, c=c: h.dma_start(out=w[:, :, 128:256], in_=C.w_in[:, D + c * 128:D + (c + 1) * 128].rearrange("(kc p) n -> p kc n", p=128)),
                       writes=[winB[wi]], dsem=sem_win[wi])
                for (t0, tn) in pcs:
                    pt, pb = pp.get()
                    for kc in range(8):
                        P.emit("pe", lambda h, pt=pt, kc=kc, t0=t0, tn=tn, w=w: h.matmul(pt[:, 0:tn], lhsT=w[:, kc, 0:128], rhs=hT[:, kc, t0:t0 + tn],
                                                                                      start=(kc == 0), stop=(kc == 7)), reads=[winB[wi], hTB], writes=[pb])
                    P.emit("act", lambda h, pt=pt, t0=t0, tn=tn: h.activation(out=gg[:, t0:t0 + tn], in_=pt[:, 0:tn], func=AF.Gelu_apprx_tanh),
                           reads=[pb], writes=[ggB])
                for (t0, tn) in pcs:
                    pt, pb = pp.get()
                    for kc in range(8):
                        P.emit("pe", lambda h, pt=pt, kc=kc, t0=t0, tn=tn, w=w: h.matmul(pt[:, 0:tn], lhsT=w[:, kc, 128:256], rhs=hT[:, kc, t0:t0 + tn],
                                                                                      start=(kc == 0), stop=(kc == 7)), reads=[winB[wi], hTB], writes=[pb])
                    P.emit("act", lambda h, pt=pt, t0=t0, tn=tn: h.activation(out=upad[:, 2 + t0:2 + t0 + tn], in_=pt[:, 0:tn], func=AF.Copy),
                           reads=[pb], writes=[upB])
                cw = PF["conv_w"]
                P.emit("dve", lambda h, c=c: h.tensor_scalar(out=uc[:], in0=upad[:, 0:L], scalar1=pf[:, cw + c:cw + c + 1],
                                                             scalar2=pf[:, PF["conv_b"] + c:PF["conv_b"] + c + 1], op0=ALU.mult, op1=ALU.add),
                       reads=[upB, pfB], writes=[ucB])
                for k in range(1, 4):
                    P.emit("dve", lambda h, c=c, k=k: h.scalar_tensor_tensor(out=uc[:], in0=upad[:, k:k + L], scalar=pf[:, cw + k * 8 + c:cw + k * 8 + c + 1],
                                                                            in1=uc[:], op0=ALU.mult, op1=ALU.add), reads=[upB, pfB, ucB], writes=[ucB])
                P.emit("pool", lambda h: h.tensor_copy(out=ucb[:], in_=uc[:]), reads=[ucB], writes=[ucbB])
                for r in range(2):
                    gi = r * 8 + c
                    for (t0, tn) in pcs:
                        pt, pb = pp.get()
                        P.emit("pe", lambda h, pt=pt, t0=t0, tn=tn, gi=gi: h.matmul(pt[:, 0:tn], lhsT=wga[:, gi, :], rhs=ucb[:, t0:t0 + tn], start=True, stop=True),
                               reads=[wgB, ucbB], writes=[pb])
                        P.emit("act", lambda h, pt=pt, t0=t0, tn=tn, gi=gi: h.activation(out=ab[:, t0:t0 + tn], in_=pt[:, 0:tn], func=AF.Sigmoid,
                                                                                       bias=pf[:, PF["b_a"] + gi:PF["b_a"] + gi + 1], scale=1.0), reads=[pb, pfB], writes=[abB])
                    for (t0, tn) in pcs:
                        pt, pb = pp.get()
                        P.emit("pe", lambda h, pt=pt, t0=t0, tn=tn, gi=gi: h.matmul(pt[:, 0:tn], lhsT=wgx[:, gi, :], rhs=ucb[:, t0:t0 + tn], start=True, stop=True),
                               reads=[wgB, ucbB], writes=[pb])
                        P.emit("act", lambda h, pt=pt, t0=t0, tn=tn, gi=gi: h.activation(out=bb[:, t0:t0 + tn], in_=pt[:, 0:tn], func=AF.Sigmoid,
                                                                                       bias=pf[:, PF["b_x"] + gi:PF["b_x"] + gi + 1], scale=1.0), reads=[pb, pfB], writes=[bbB])
                    P.emit("act", lambda h, gi=gi: h.activation(out=ab[:], in_=ab[:], func=AF.Exp, scale=cl[:, gi:gi + 1]), reads=[abB, clB], writes=[abB])
                    P.emit("pool", lambda h: h.tensor_tensor(out=bb[:], in0=bb[:], in1=uc[:], op=ALU.mult), reads=[bbB, ucB], writes=[bbB])
                    P.emit("act", lambda h: h.activation(out=tm[:], in_=ab[:], func=AF.Square), reads=[abB], writes=[tmB])
                    P.emit("act", lambda h: h.activation(out=tm[:], in_=tm[:], func=AF.Sqrt, bias=1.0, scale=-1.0), reads=[tmB], writes=[tmB])
                    P.emit("pool", lambda h: h.tensor_tensor(out=bb[:], in0=bb[:], in1=tm[:], op=ALU.mult), reads=[bbB, tmB], writes=[bbB])
                    if r == 0:
                        P.emit("dve", lambda h: h.tensor_tensor_scan(out=hf[:], data0=ab[:], data1=bb[:], initial=0.0, op0=ALU.mult, op1=ALU.add),
                               reads=[abB, bbB], writes=[hfB])
                    else:
                        P.emit("dve", lambda h: h.tensor_tensor_scan(out=tm[:, ::-1], data0=ab[:, ::-1], data1=bb[:, ::-1], initial=0.0, op0=ALU.mult, op1=ALU.add),
                               reads=[abB, bbB], writes=[tmB])
                P.emit("pool", lambda h: h.tensor_tensor(out=hf[:], in0=hf[:], in1=tm[:], op=ALU.add), reads=[hfB, tmB], writes=[hfB])
                P.emit("dve", lambda h, c=c: h.tensor_tensor(out=Y[:, c, :], in0=hf[:], in1=gg[:], op=ALU.mult), reads=[hfB, ggB], writes=[YB])
            for (p0, n) in pos_tiles():
                loader(xt, xtB, sem_x, n, p0)
                for half in range(2):
                    pt, pb = pp.get()
                    for kc in range(8):
                        P.emit("pe", lambda h, pt=pt, kc=kc, p0=p0, n=n, half=half: h.matmul(pt[0:n, :], lhsT=Y[:, kc, p0:p0 + n], rhs=wout[:, kc, half * 512:(half + 1) * 512],
                                                                                          start=(kc == 0), stop=(kc == 7)), reads=[YB, woutB], writes=[pb])
                    P.emit("dve", lambda h, pt=pt, n=n, half=half: h.scalar_tensor_tensor(out=z[0:n, half * 512:(half + 1) * 512], in0=xt[0:n, half * 512:(half + 1) * 512],
                                                                                         scalar=ALPHA, in1=pt[0:n, :], op0=ALU.mult, op1=ALU.add),
                           reads=[xtB, pb], writes=[zB])
                layer_norm_tile(P, C, gtB, z, zB, n, gt[:, 0, :], gt[:, 1, :], zo, zoB, lnt)
                r0 = s * L + p0
                P.emit("sp", lambda h, r0=r0, n=n: h.dma_start(out=C.H1[r0:r0 + n, :], in_=zo[0:n, :]), reads=[zoB], writes=[C.H1B], dsem=sem_o)
        P.wait_all("sp", [(sem_o, sem_o.count)])
        P.flush_block([C.H1B])


def build_program(nseq, stop_after=None):
    nc = bass.Bass("TRN2", target_bir_lowering=False)
    C = Ctx()
    T = nseq * L
    di = lambda name, shape: nc.dram_tensor(name, shape, F32, kind="ExternalInput").ap()
    C.x = di("x", [nseq, SEQ, D])
    C.meta = di("meta", [NMETA, D])
    C.w_in = di("w_in", [D, 2 * D])
    C.w_a = di("w_a", [2, 8, 128, 128])
    C.w_x = di("w_x", [2, 8, 128, 128])
    C.w_out = di("w_out", [D, D])
    C.pw1 = di("pw1", [D, 2 * D])
    C.pw2 = di("pw2", [D, D])
    C.wq = di("wq", [2, D, 2 * D])
    C.kt = di("kt", [2, 2, 128, 128])
    C.ut = di("ut", [2, D, NEXP])
    C.v = di("v", [2, NEXP, D])
    C.pf = di("pf", [128, PF_COLS])
    C.pt = di("pt", [len(PT), D])
    dbg = stop_after is not None
    mk = lambda name, shape, out: nc.dram_tensor(name, shape, F32, kind=("ExternalOutput" if out else "Internal")).ap()
    C.H1 = mk("H1", [T, D], dbg and stop_after == 1)
    C.H1B = Buf("H1")
    C.H2 = mk("H2", [T, D], dbg and stop_after == 2)
    C.H2B = Buf("H2")
    C.H3 = mk("H3", [T, D], dbg and stop_after == 3)
    C.H3B = Buf("H3")
    C.out = nc.dram_tensor("out", [nseq, SEQ, D], F32, kind="ExternalOutput").ap() if (stop_after is None or stop_after == 4) else None
    C.outB = Buf("out")
    with contextlib.ExitStack() as st:
        P = Prog(nc, st)
        phase_rglru(P, C, nc, nseq)
        if stop_after == 1:
            return nc
        phase_peer(P, C, nc, T, 0, C.H1, C.H1B, C.H2, C.H2B, False, "b_")
        if stop_after == 2:
            return nc
        phase_conf(P, C, nc, nseq, C.H2, C.H2B, C.H3, C.H3B, "c_")
        if stop_after == 3:
            return nc
        phase_peer(P, C, nc, T, 1, C.H3, C.H3B, C.out, C.outB, True, "d_")
    return nc


def pack_fm(v):
    v = np.asarray(v, np.float32).reshape(-1, 128)
    return np.ascontiguousarray(v.T)


def make_shared_inputs(inp):
    f = lambda a: np.ascontiguousarray(np.asarray(a, np.float32))
    pfm = np.zeros((128, PF_COLS), np.float32)
    cw = inp["lru_conv_w"][0]
    for k in range(4):
        pfm[:, PF["conv_w"] + k * 8:PF["conv_w"] + k * 8 + 8] = pack_fm(cw[k])
    pfm[:, PF["conv_b"]:PF["conv_b"] + 8] = pack_fm(inp["lru_conv_b"][0])
    pfm[:, PF["b_a"]:PF["b_a"] + 16] = pack_fm(inp["lru_b_a"][0].reshape(-1))
    pfm[:, PF["b_x"]:PF["b_x"] + 16] = pack_fm(inp["lru_b_x"][0].reshape(-1))
    pfm[:, PF["lam"]:PF["lam"] + 16] = pack_fm(inp["lru_lambda"][0].reshape(-1))
    pfm[:, PF["b_pw1"]:PF["b_pw1"] + 16] = pack_fm(inp["conf_b_pw1"][0])
    dw = inp["conf_dw_w"][0]
    for k in range(31):
        pfm[:, PF["dw_w"] + k * 8:PF["dw_w"] + k * 8 + 8] = pack_fm(dw[k])
    pfm[:, PF["dw_b"]:PF["dw_b"] + 8] = pack_fm(inp["conf_dw_b"][0])
    pfm[:, PF["cln_g"]:PF["cln_g"] + 8] = pack_fm(inp["conf_ln_g"][0])
    pfm[:, PF["cln_b"]:PF["cln_b"] + 8] = pack_fm(inp["conf_ln_b"][0])
    ptm = np.zeros((len(PT), D), np.float32)
    for i in range(2):
        ptm[PT["mix_g%d" % i]] = inp["ln_mix_g"][i]
        ptm[PT["mix_b%d" % i]] = inp["ln_mix_b"][i]
        ptm[PT["ffn_g%d" % i]] = inp["ln_ffn_g"][i]
        ptm[PT["ffn_b%d" % i]] = inp["ln_ffn_b"][i]
    ptm[PT["b_pw2"]] = inp["conf_b_pw2"][0]
    sh = {
        "meta": f(inp["meta_tokens"]),
        "w_in": f(inp["lru_w_in"][0]),
        "w_a": f(inp["lru_w_a"][0]),
        "w_x": f(inp["lru_w_x"][0]),
        "w_out": f(inp["lru_w_out"][0]),
        "pw1": f(inp["conf_w_pw1"][0]),
        "pw2": f(inp["conf_w_pw2"][0]),
        "wq": f(inp["peer_w_query"]),
        "kt": f(np.transpose(np.asarray(inp["peer_sub_keys"], np.float32), (0, 1, 3, 2))),
        "ut": f(np.transpose(np.asarray(inp["peer_u"], np.float32), (0, 2, 1))),
        "v": f(inp["peer_v"]),
        "pf": pfm,
        "pt": ptm,
    }
    return sh


def kernel(**inputs):
    x = np.asarray(inputs["x"], np.float32)
    nseq = x.shape[0] // NCORES
    sh = make_shared_inputs(inputs)
    nc = build_program(nseq)
    in_maps = []
    for c in range(NCORES):
        m = dict(sh)
        m["x"] = np.ascontiguousarray(x[c * nseq:(c + 1) * nseq])
        in_maps.append(m)
    res = run_bass_kernel_spmd(nc, in_maps, core_ids=list(range(NCORES)))
    return np.concatenate([r["out"] for r in res.results], axis=0)


def out_segments(r0, n):
    segs = []
    r = r0
    while r < r0 + n:
        s, pos = divmod(r, L)
        if pos < NMETA:
            r += min(NMETA - pos, r0 + n - r)
            continue
        cnt = min(L - pos, r0 + n - r)
        segs.append((r - r0, cnt, s, pos - NMETA))
        r += cnt
    return segs


def phase_peer(P, C, nc, T, layer, Hin, HinB, Hout, HoutB, final, pfx):
    GN = 4
    NG = NKEY // GN
    GE = GN * NKEY
    with contextlib.ExitStack() as st:
        sb = lambda name, shape, dt: st.enter_context(nc.sbuf_tensor(pfx + name, shape, dt))
        idf, idb, idB = make_ident_named(P, nc, st, pfx)
        lnt = alloc_ln_tmp_named(nc, st, P, pfx)
        ppA = PsumPool(nc, st, 2, [128, 512], F32, pfx + "psA")
        ppT = PsumPool(nc, st, 2, [128, 4, 128], BF16, pfx + "psT")
        ppO = PsumPool(nc, st, 4, [128, 512], F32, pfx + "psO")
        wq = sb("wq", [128, 8, 2 * D], BF16); wqB = Buf("wq")
        sem_wq = P.dsem(pfx + "sem_wq")
        for kc in range(8):
            P.emit("pool", lambda h, kc=kc: h.dma_start(out=wq[:, kc, :], in_=C.wq[layer, kc * 128:(kc + 1) * 128, :], max_dma_last_dim=4096), writes=[wqB], dsem=sem_wq)
        kt = sb("kt", [128, 2, 128], BF16); ktB = Buf("kt")
        sem_kt = P.dsem(pfx + "sem_kt")
        P.emit("pool", lambda h: h.dma_start(out=kt[:], in_=C.kt[layer].rearrange("p k n -> k p n")), writes=[ktB], dsem=sem_kt)
        gt = sb("lng", [128, 2, D], F32); gtB = Buf("lng")
        sem_g = P.dsem(pfx + "sem_lng")
        gi, bi = PT["ffn_g%d" % layer], PT["ffn_b%d" % layer]
        P.emit("sp", lambda h: h.dma_start(out=gt[:, 0, :], in_=C.pt[gi:gi + 1, :].partition_broadcast(128)), writes=[gtB], dsem=sem_g)
        P.emit("sp", lambda h: h.dma_start(out=gt[:, 1, :], in_=C.pt[bi:bi + 1, :].partition_broadcast(128)), writes=[gtB], dsem=sem_g)
        xt = sb("xt", [128, D], F32); xtB = Buf("xt"); sem_x = P.dsem(pfx + "sem_x")
        hT = sb("hT", [128, 8, 512], BF16); hTB = Buf("hT")
        P.emit("pool", lambda h: h.memset(hT[:], 0.0), writes=[hTB])
        P.emit("pool", lambda h: h.memset(xt[:], 0.0), writes=[xtB])
        qTs = [sb("qT%d" % i, [128, 512], BF16) for i in range(2)]; qTB = [Buf("qT%d" % i) for i in range(2)]
        S = sb("S", [128, 4, 16, 128], F32); SB = [Buf("S%d" % i) for i in range(4)]
        TAU = sb("TAU", [128, 4, 8], F32); TAUB = [Buf("TAU%d" % i) for i in range(4)]
        O = sb("O", [128, 4, D], F32); OB = [Buf("O%d" % i) for i in range(4)]
        T16 = sb("T16", [128, 16, 16], F32); T16B = Buf("T16")
        tmpS = sb("tmpS", [128, 128], F32); tmpSB = Buf("tmpS")
        cand2 = sb("cand2", [128, 256], F32); cand2B = Buf("cand2")
        c24 = sb("c24", [128, 8, 24], F32); c24B = Buf("c24")
        e16 = sb("e16", [128, 8, 16], F32); e16B = Buf("e16")
        zs = sb("zs", [128, 8], F32); zsB = Buf("zs")
        mb = sb("mb", [128, 8], F32); mbB = Buf("mb")
        UT = [sb("UT%d" % i, [128, 8, GE], BF16) for i in range(2)]; UTB = [Buf("UT%d" % i) for i in range(2)]
        VG = [sb("VG%d" % i, [128, GN, D], BF16) for i in range(2)]; VGB = [Buf("VG%d" % i) for i in range(2)]
        sem_u = [P.dsem(pfx + "sem_u%d" % i) for i in range(2)]
        sem_v = [P.dsem(pfx + "sem_v%d" % i) for i in range(2)]
        G = sb("G", [128, 8, GN, 128], F32); GB = [Buf("G0"), Buf("G1")]
        cand = G[:, 0:4].rearrange("p h a n -> p (h a n)").rearrange("p (h c) -> p h c", h=8); candB = GB[0]
        E = [sb("E%d" % i, [128, 8, GN, 128], F32) for i in range(2)]
        EB = [[Buf("E%d_%d" % (i, j)) for j in range(2)] for i in range(2)]
        A = [sb("A%d" % i, [128, GE], F32) for i in range(2)]; AB = [Buf("A%d" % i) for i in range(2)]
        WA = [sb("WA%d" % i, [128, GE], BF16) for i in range(2)]; WAB = [Buf("WA%d" % i) for i in range(2)]
        WT = [sb("WT%d" % i, [128, GN, 128], BF16) for i in range(2)]; WTB = [Buf("WT%d" % i) for i in range(2)]
        z = sb("z", [128, D], F32); zB = Buf("z")
        zo = sb("zo", [128, D], F32); zoB = Buf("zo")
        sem_o = P.dsem(pfx + "sem_o")

        tiles = [(r0, min(128, T - r0)) for r0 in range(0, T, 128)]
        cnt = 0
        for s0 in range(0, len(tiles), 4):
            tl = tiles[s0:s0 + 4]
            ntl = len(tl)
            ncol = ntl * 128

            def loader(xt_, xtb_, sem_, n, r0):
                P.emit("sp", lambda h: h.dma_start(out=xt_[0:n, :], in_=Hin[r0:r0 + n, :]), reads=[HinB], writes=[xtb_], dsem=sem_)
            for i, (r0, n) in enumerate(tl):
                loader(xt, xtB, sem_x, n, r0)
                for half in range(2):
                    pt, pb = ppA.get()
                    for j in range(4):
                        kc = half * 4 + j
                        P.emit("pe", lambda h, kc=kc, j=j, pt=pt: h.transpose(out=pt[:, j * 128:(j + 1) * 128], in_=xt[:, kc * 128:(kc + 1) * 128], identity=idf[:]),
                               reads=[xtB, idB], writes=[pb])
                    if half == 0:
                        P.emit("act", lambda h, pt=pt, i=i: h.activation(out=hT[:, 0:4, i * 128:(i + 1) * 128], in_=pt[:, :].rearrange("p (j t) -> p j t", j=4), func=AF.Copy),
                               reads=[pb], writes=[hTB])
                    else:
                        P.emit("dve", lambda h, pt=pt, i=i: h.tensor_copy(out=hT[:, 4:8, i * 128:(i + 1) * 128], in_=pt[:, :].rearrange("p (j t) -> p j t", j=4)),
                               reads=[pb], writes=[hTB])
            for j in range(16):
                pt, pb = ppA.get()
                for kc in range(8):
                    P.emit("pe", lambda h, pt=pt, kc=kc, j=j: h.matmul(pt[:, 0:ncol], lhsT=wq[:, kc, j * 128:(j + 1) * 128], rhs=hT[:, kc, 0:ncol],
                                                                     start=(kc == 0), stop=(kc == 7)), reads=[wqB, hTB], writes=[pb])
                q = qTs[j % 2]; qb = qTB[j % 2]
                if j % 2 == 0:
                    P.emit("act", lambda h, pt=pt, q=q: h.activation(out=q[:, 0:ncol], in_=pt[:, 0:ncol], func=AF.Copy), reads=[pb], writes=[qb])
                else:
                    P.emit("dve", lambda h, pt=pt, q=q: h.tensor_copy(out=q[:, 0:ncol], in_=pt[:, 0:ncol]), reads=[pb], writes=[qb])
                po, pob = ppO.get()
                for i in range(ntl):
                    P.emit("pe", lambda h, po=po, i=i, q=q, j=j: h.matmul(po[:, i * 128:(i + 1) * 128], lhsT=q[:, i * 128:(i + 1) * 128], rhs=kt[:, j % 2, :], start=True, stop=True),
                           reads=[qb, ktB], writes=[pob])
                eng = "act" if j % 2 == 1 else "dve"
                if eng == "act":
                    P.emit("act", lambda h, po=po, j=j: h.activation(out=S[:, 0:ntl, j, :], in_=po[:, 0:ncol].rearrange("p (t n) -> p t n", n=128), func=AF.Copy),
                           reads=[pob], writes=SB[0:ntl])
                else:
                    P.emit("dve", lambda h, po=po, j=j: h.tensor_copy(out=S[:, 0:ntl, j, :], in_=po[:, 0:ncol].rearrange("p (t n) -> p t n", n=128)),
                           reads=[pob], writes=SB[0:ntl])
            for i in range(ntl):
                for j in range(16):
                    P.emit("dve", lambda h, i=i, j=j: h.max(out=T16[:, j, 0:8], in_=S[:, i, j, :]), reads=[SB[i]], writes=[T16B])
                    P.emit("dve", lambda h, i=i, j=j: h.match_replace(out=tmpS[:], in_to_replace=T16[:, j, 0:8], in_values=S[:, i, j, :], imm_value=-1e30),
                           reads=[SB[i], T16B], writes=[tmpSB])
                    P.emit("dve", lambda h, j=j: h.max(out=T16[:, j, 8:16], in_=tmpS[:]), reads=[tmpSB], writes=[T16B])
                P.emit("dve", lambda h: h.tensor_tensor(out=cand[:].rearrange("p h (a b) -> p h a b", a=16),
                                                        in0=T16[:, 0::2, :].unsqueeze(3).to_broadcast([128, 8, 16, 16]),
                                                        in1=T16[:, 1::2, :].unsqueeze(2).to_broadcast([128, 8, 16, 16]), op=ALU.add), reads=[T16B], writes=[candB])
                for hh in range(8):
                    P.emit("dve", lambda h, hh=hh: h.max(out=c24[:, hh, 0:8], in_=cand[:, hh, :]), reads=[candB], writes=[c24B])
                    P.emit("dve", lambda h, hh=hh: h.match_replace(out=cand2[:], in_to_replace=c24[:, hh, 0:8], in_values=cand[:, hh, :], imm_value=-1e30),
                           reads=[candB, c24B], writes=[cand2B])
                    P.emit("dve", lambda h, hh=hh: h.max(out=c24[:, hh, 8:16], in_=cand2[:]), reads=[cand2B], writes=[c24B])
                    P.emit("dve", lambda h, hh=hh: h.match_replace(out=cand2[:], in_to_replace=c24[:, hh, 8:16], in_values=cand2[:], imm_value=-1e30),
                           reads=[cand2B, c24B], writes=[cand2B])
                    P.emit("dve", lambda h, hh=hh: h.max(out=c24[:, hh, 16:24], in_=cand2[:]), reads=[cand2B], writes=[c24B])
                P.emit("dve", lambda h: h.tensor_tensor(out=e16[:], in0=c24[:, :, 0:16], in1=c24[:, :, 0:1].to_broadcast([128, 8, 16]), op=ALU.subtract),
                       reads=[c24B], writes=[e16B])
                P.emit("act", lambda h: h.activation(out=e16[:], in_=e16[:], func=AF.Exp), reads=[e16B], writes=[e16B])
                P.emit("dve", lambda h: h.tensor_reduce(out=zs[:], in_=e16[:], axis=AX.X, op=ALU.add), reads=[e16B], writes=[zsB])
                P.emit("act", lambda h: h.activation(out=zs[:], in_=zs[:], func=AF.Ln), reads=[zsB], writes=[zsB])
                P.emit("dve", lambda h: h.tensor_tensor(out=mb[:], in0=zs[:], in1=c24[:, :, 0], op=ALU.add), reads=[zsB, c24B], writes=[mbB])
                P.emit("dve", lambda h: h.tensor_tensor(out=zs[:], in0=c24[:, :, 15], in1=c24[:, :, 16], op=ALU.add), reads=[c24B, zsB], writes=[zsB])
                P.emit("dve", lambda h, i=i: h.scalar_tensor_tensor(out=TAU[:, i, :], in0=zs[:], scalar=0.5, in1=mb[:], op0=ALU.mult, op1=ALU.subtract),
                       reads=[zsB, mbB], writes=[TAUB[i]])
                P.emit("dve", lambda h, i=i: h.tensor_tensor(out=S[:, i, 0::2, :], in0=S[:, i, 0::2, :], in1=mb[:].unsqueeze(2).to_broadcast([128, 8, 128]), op=ALU.subtract),
                       reads=[SB[i], mbB], writes=[SB[i]])

            def load_group(g):
                b = g % 2
                P.emit("pool", lambda h: h.dma_start(out=UT[b][:], in_=C.ut[layer, :, g * GE:(g + 1) * GE].rearrange("(kc p) e -> p kc e", p=128)),
                       writes=[UTB[b]], dsem=sem_u[b])
                P.emit("pool", lambda h: h.dma_start(out=VG[b][:], in_=C.v[layer, g * GE:(g + 1) * GE, :].rearrange("(ec p) d -> p ec d", p=128)),
                       writes=[VGB[b]], dsem=sem_v[b])

            items = [(g, i) for g in range(NG) for i in range(ntl)]
            pa_of = {}

            def st_front(k):
                g, i = items[k]
                b = g % 2
                eb = k % 2
                Ei, EiB = E[eb], EB[eb]
                pa, pab = ppA.get()
                pa_of[k] = (pa, pab)
                for kc in range(8):
                    P.emit("pe", lambda h: h.matmul(pa[:, :], lhsT=hT[:, kc, i * 128:(i + 1) * 128], rhs=UT[b][:, kc, :], start=(kc == 0), stop=(kc == 7)),
                           reads=[hTB, UTB[b]], writes=[pab])
                for hf in range(2):
                    hs = slice(hf * 4, hf * 4 + 4)
                    P.emit("dve", lambda h: h.tensor_tensor(out=G[:, hs], in0=S[:, i, 1::2, :][:, hs].unsqueeze(2).to_broadcast([128, 4, GN, 128]),
                                                            in1=S[:, i, 0::2, g * GN:(g + 1) * GN][:, hs].unsqueeze(3).to_broadcast([128, 4, GN, 128]), op=ALU.add),
                           reads=[SB[i]], writes=[GB[hf]])
                    P.emit("act", lambda h: h.activation(out=Ei[:, hs], in_=G[:, hs], func=AF.Exp), reads=[GB[hf]], writes=[EiB[hf]])
                for hh in range(8):
                    hf = hh // 4
                    P.emit("dve", lambda h: h.scalar_tensor_tensor(out=Ei[:, hh], in0=G[:, hh], scalar=TAU[:, i, hh:hh + 1], in1=Ei[:, hh],
                                                                   op0=ALU.is_ge, op1=ALU.mult), reads=[GB[hf], TAUB[i], EiB[hf]], writes=[EiB[hf]])
                P.emit("act", lambda h: h.activation(out=A[eb][:], in_=pa[:, :], func=AF.Gelu_apprx_tanh), reads=[pab], writes=[AB[eb]])

            def st_adds(k):
                eb = k % 2
                Ei, EiB = E[eb], EB[eb]
                P.emit("pool", lambda h: h.tensor_tensor(out=Ei[:, 0:4], in0=Ei[:, 0:4], in1=Ei[:, 4:8], op=ALU.add), reads=[EiB[0], EiB[1]], writes=[EiB[0]])
                P.emit("pool", lambda h: h.tensor_tensor(out=Ei[:, 0:2], in0=Ei[:, 0:2], in1=Ei[:, 2:4], op=ALU.add), reads=[EiB[0]], writes=[EiB[0]])
                P.emit("pool", lambda h: h.tensor_tensor(out=Ei[:, 0], in0=Ei[:, 0], in1=Ei[:, 1], op=ALU.add), reads=[EiB[0]], writes=[EiB[0]])

            po_of = {}

            def st_back(k):
                g, i = items[k]
                b = g % 2
                eb = k % 2
                Ei, EiB = E[eb], EB[eb]
                P.emit("pool", lambda h: h.tensor_tensor(out=WA[eb][:], in0=A[eb][:], in1=Ei[:, 0].rearrange("p a n -> p (a n)"), op=ALU.mult),
                       reads=[AB[eb], EiB[0]], writes=[WAB[eb]])
                ptt, pttb = ppT.get()
                for ec in range(GN):
                    P.emit("pe", lambda h: h.transpose(out=ptt[:, ec, :], in_=WA[eb][:, ec * 128:(ec + 1) * 128], identity=idb[:]),
                           reads=[WAB[eb], idB], writes=[pttb])
                P.emit("act", lambda h: h.activation(out=WT[eb][:], in_=ptt[:], func=AF.Copy), reads=[pttb], writes=[WTB[eb]])
                pos = []
                for dh in range(2):
                    po, pob = ppO.get()
                    pos.append((po, pob))
                    for ec in range(GN):
                        P.emit("pe", lambda h: h.matmul(po[:, :], lhsT=WT[eb][:, ec, :], rhs=VG[b][:, ec, dh * 512:(dh + 1) * 512], start=(ec == 0), stop=(ec == GN - 1)),
                               reads=[WTB[eb], VGB[b]], writes=[pob])
                po_of[k] = pos

            def st_acc(k):
                g, i = items[k]
                for dh in range(2):
                    po, pob = po_of.pop(k)[dh] if dh == 1 else po_of[k][dh]
                    if g == 0:
                        P.emit("act", lambda h: h.activation(out=O[:, i, dh * 512:(dh + 1) * 512], in_=po[:, :], func=AF.Copy), reads=[pob], writes=[OB[i]])
                    else:
                        P.emit("dve", lambda h: h.tensor_tensor(out=O[:, i, dh * 512:(dh + 1) * 512], in0=O[:, i, dh * 512:(dh + 1) * 512], in1=po[:, :], op=ALU.add),
                               reads=[pob, OB[i]], writes=[OB[i]])

            load_group(0)
            if NG > 1:
                load_group(1)
            nit = len(items)
            for k in range(nit + 2):
                if k - 2 >= 0:
                    st_acc(k - 2)
                if k < nit:
                    st_front(k)
                if 0 <= k - 1 < nit:
                    st_back(k - 1)
                    gk, ik = items[k - 1]
                    if ik == ntl - 1 and gk + 2 < NG:
                        load_group(gk + 2)
                if k < nit:
                    st_adds(k)
            for i, (r0, n) in enumerate(tl):
                loader(xt, xtB, sem_x, n, r0)
                P.emit("dve", lambda h, i=i: h.scalar_tensor_tensor(out=z[:], in0=xt[:], scalar=ALPHA, in1=O[:, i, :], op0=ALU.mult, op1=ALU.add),
                       reads=[xtB, OB[i]], writes=[zB])
                layer_norm_tile(P, C, gtB, z, zB, 128, gt[:, 0, :], gt[:, 1, :], zo, zoB, lnt)
                if not final:
                    P.emit("sp", lambda h, r0=r0, n=n: h.dma_start(out=Hout[r0:r0 + n, :], in_=zo[0:n, :]), reads=[zoB], writes=[HoutB], dsem=sem_o)
                else:
                    for (ro, c, sq, ps) in out_segments(r0, n):
                        P.emit("sp", lambda h, ro=ro, c=c, sq=sq, ps=ps: h.dma_start(out=Hout[sq, ps:ps + c, :], in_=zo[ro:ro + c, :]), reads=[zoB], writes=[HoutB], dsem=sem_o)
        P.wait_all("sp", [(sem_o, sem_o.count)])
        P.flush_block([HinB, HoutB])


def make_ident_named(P, nc, st, pfx):
    idf = st.enter_context(nc.sbuf_tensor(pfx + "identf", [128, 128], F32))
    idb = st.enter_context(nc.sbuf_tensor(pfx + "identb", [128, 128], BF16))
    B = Buf("ident")
    P.emit("pool", lambda h: h.memset(idf[:], 1.0), writes=[B])
    P.emit("pool", lambda h: h.affine_select(out=idf[:], in_=idf[:], pattern=[[-1, 128]], base=0, channel_multiplier=1,
                                              compare_op=ALU.is_equal, fill=0.0), reads=[B], writes=[B])
    P.emit("pool", lambda h: h.tensor_copy(out=idb[:], in_=idf[:]), reads=[B], writes=[B])
    return idf, idb, B


def alloc_ln_tmp_named(nc, st, P, pfx):
    t = {}
    t["stats"] = st.enter_context(nc.sbuf_tensor(pfx + "ln_stats", [128, 2, 6], F32))
    t["statsb"] = Buf("ln_stats")
    t["mv"] = st.enter_context(nc.sbuf_tensor(pfx + "ln_mv", [128, 2], F32))
    t["mvb"] = Buf("ln_mv")
    t["rs"] = st.enter_context(nc.sbuf_tensor(pfx + "ln_rs", [128, 1], F32))
    t["rsb"] = Buf("ln_rs")
    t["eps"] = st.enter_context(nc.sbuf_tensor(pfx + "ln_eps", [128, 1], F32))
    P.emit("pool", lambda h: h.memset(t["eps"][:], EPS), writes=[Buf()])
    return t


def phase_conf(P, C, nc, nseq, Hin, HinB, Hout, HoutB, pfx):
    KW = 31
    PADW = KW // 2
    with contextlib.ExitStack() as st:
        sb = lambda name, shape, dt: st.enter_context(nc.sbuf_tensor(pfx + name, shape, dt))
        idf, idb, idB = make_ident_named(P, nc, st, pfx)
        lnt = alloc_ln_tmp_named(nc, st, P, pfx)
        pp = PsumPool(nc, st, 8, [128, 512], F32, pfx + "ps")
        pf = sb("pf", [128, PF_COLS], F32); pfB = Buf("pf")
        sem_c = P.dsem(pfx + "semc")
        P.emit("sp", lambda h: h.dma_start(out=pf[:], in_=C.pf), writes=[pfB], dsem=sem_c)
        ones = sb("ones", [128, 128], F32); onesB = Buf("ones")
        P.emit("pool", lambda h: h.memset(ones[:], 1.0 / D), writes=[onesB])
        pw2 = sb("pw2", [128, 8, D], BF16); pw2B = Buf("pw2")
        sem_p2 = P.dsem(pfx + "sem_p2")
        for kc in range(8):
            P.emit("pool", lambda h, kc=kc: h.dma_start(out=pw2[:, kc, :], in_=C.pw2[kc * 128:(kc + 1) * 128, :]), writes=[pw2B], dsem=sem_p2)
        gt = sb("lng", [128, 3, D], F32); gtB = Buf("lng")
        sem_g = P.dsem(pfx + "sem_lng")
        for k, nm in enumerate(("mix_g1", "mix_b1", "b_pw2")):
            P.emit("sp", lambda h, k=k, nm=nm: h.dma_start(out=gt[:, k, :], in_=C.pt[PT[nm]:PT[nm] + 1, :].partition_broadcast(128)), writes=[gtB], dsem=sem_g)
        hT = sb("hT", [128, 8, L], BF16); hTB = Buf("hT")
        CV = sb("CV", [128, 8, L], F32); CVB = [Buf("CV%d" % i) for i in range(8)]
        xt = sb("xt", [128, D], F32); xtB = Buf("xt"); sem_x = P.dsem(pfx + "sem_x")
        win = [sb("win%d" % i, [128, 8, 256], BF16) for i in range(2)]
        winB = [Buf("win%d" % i) for i in range(2)]
        sem_win = [P.dsem(pfx + "sem_win%d" % i) for i in range(2)]
        sig = sb("sig", [128, L], F32); sigB = Buf("sig")
        gpad = sb("gpad", [128, L + 2 * PADW], F32); gpB = Buf("gpad")
        P.emit("pool", lambda h: h.memset(gpad[:], 0.0), writes=[gpB])
        sq = [sb("sq%d" % i, [128, 512], F32) for i in range(2)]; sqB = [Buf("sq%d" % i) for i in range(2)]
        mean = sb("mean", [128, 512], F32); meanB = Buf("mean")
        rstd = sb("rstd", [128, 512], F32); rstdB = Buf("rstd")
        tq = [sb("tq%d" % i, [128, 512], F32) for i in range(2)]; tqB = [Buf("tq%d" % i) for i in range(2)]
        z = sb("z", [128, D], F32); zB = Buf("z")
        zo = sb("zo", [128, D], F32); zoB = Buf("zo")
        sem_o = P.dsem(pfx + "sem_o")
        pcs = pieces(L)
        wcount = 0
        sqc = 0
        for s in range(nseq):
            def loader(xt_, xtb_, sem_, n, p0, s=s):
                r0 = s * L + p0
                P.emit("sp", lambda h: h.dma_start(out=xt_[0:n, :], in_=Hin[r0:r0 + n, :]), reads=[HinB], writes=[xtb_], dsem=sem_)
            load_tokens_T(P, C, nc, hT, hTB, xt, xtB, sem_x, idf, idB, pp, loader, [(p0, n, (p0,)) for (p0, n) in pos_tiles()])
            for c in range(8):
                wi = wcount % 2
                wcount += 1
                w = win[wi]
                P.emit("pool", lambda h: h.dma_start(out=w[:, :, 0:128], in_=C.pw1[:, c * 128:(c + 1) * 128].rearrange("(kc p) n -> p kc n", p=128)),
                       writes=[winB[wi]], dsem=sem_win[wi])
                P.emit("pool", lambda h: h.dma_start(out=w[:, :, 128:256], in_=C.pw1[:, D + c * 128:D + (c + 1) * 128].rearrange("(kc p) n -> p kc n", p=128)),
                       writes=[winB[wi]], dsem=sem_win[wi])
                for (t0, tn) in pcs:
                    pa, pab = pp.get()
                    pg, pgb = pp.get()
                    for kc in range(8):
                        P.emit("pe", lambda h: h.matmul(pg[:, 0:tn], lhsT=w[:, kc, 128:256], rhs=hT[:, kc, t0:t0 + tn], start=(kc == 0), stop=(kc == 7)),
                               reads=[winB[wi], hTB], writes=[pgb])
                    for kc in range(8):
                        P.emit("pe", lambda h: h.matmul(pa[:, 0:tn], lhsT=w[:, kc, 0:128], rhs=hT[:, kc, t0:t0 + tn], start=(kc == 0), stop=(kc == 7)),
                               reads=[winB[wi], hTB], writes=[pab])
                    bg = PF["b_pw1"] + 8 + c
                    ba = PF["b_pw1"] + c
                    P.emit("act", lambda h: h.activation(out=sig[:, t0:t0 + tn], in_=pg[:, 0:tn], func=AF.Sigmoid, bias=pf[:, bg:bg + 1], scale=1.0),
                           reads=[pgb, pfB], writes=[sigB])
                    P.emit("dve", lambda h: h.scalar_tensor_tensor(out=gpad[:, PADW + t0:PADW + t0 + tn], in0=pa[:, 0:tn], scalar=pf[:, ba:ba + 1], in1=sig[:, t0:t0 + tn],
                                                                   op0=ALU.add, op1=ALU.mult), reads=[pab, pfB, sigB], writes=[gpB])
                dw = PF["dw_w"]
                db = PF["dw_b"] + c
                P.emit("dve", lambda h: h.tensor_scalar(out=CV[:, c, :], in0=gpad[:, 0:L], scalar1=pf[:, dw + c:dw + c + 1], scalar2=pf[:, db:db + 1],
                                                        op0=ALU.mult, op1=ALU.add), reads=[gpB, pfB], writes=[CVB[c]])
                for k in range(1, KW):
                    P.emit("dve", lambda h: h.scalar_tensor_tensor(out=CV[:, c, :], in0=gpad[:, k:k + L], scalar=pf[:, dw + k * 8 + c:dw + k * 8 + c + 1],
                                                                   in1=CV[:, c, :], op0=ALU.mult, op1=ALU.add), reads=[gpB, pfB, CVB[c]], writes=[CVB[c]])
            Yc, YcB = hT, hTB
            for (t0, tn) in pcs:
                pm, pmb = pp.get()
                pq, pqb = pp.get()
                for c in range(8):
                    P.emit("pe", lambda h: h.matmul(pm[:, 0:tn], lhsT=ones[:], rhs=CV[:, c, t0:t0 + tn], start=(c == 0), stop=(c == 7)),
                           reads=[onesB, CVB[c]], writes=[pmb])
                for c in range(8):
                    si = sqc % 2
                    sqc += 1
                    P.emit("act", lambda h: h.activation(out=sq[si][:, 0:tn], in_=CV[:, c, t0:t0 + tn], func=AF.Square), reads=[CVB[c]], writes=[sqB[si]])
                    P.emit("pe", lambda h: h.matmul(pq[:, 0:tn], lhsT=ones[:], rhs=sq[si][:, 0:tn], start=(c == 0), stop=(c == 7)),
                           reads=[onesB, sqB[si]], writes=[pqb])
                P.emit("act", lambda h: h.activation(out=mean[:, 0:tn], in_=pm[:, 0:tn], func=AF.Copy), reads=[pmb], writes=[meanB])
                P.emit("dve", lambda h: h.tensor_tensor(out=rstd[:, 0:tn], in0=mean[:, 0:tn], in1=mean[:, 0:tn], op=ALU.mult), reads=[meanB], writes=[rstdB])
                P.emit("dve", lambda h: h.tensor_tensor(out=rstd[:, 0:tn], in0=pq[:, 0:tn], in1=rstd[:, 0:tn], op=ALU.subtract), reads=[pqb, rstdB], writes=[rstdB])
                P.emit("act", lambda h: h.activation(out=rstd[:, 0:tn], in_=rstd[:, 0:tn], func=AF.Sqrt, bias=lnt["eps"][:, :], scale=1.0), reads=[rstdB], writes=[rstdB])
                P.emit("dve", lambda h: h.reciprocal(out=rstd[:, 0:tn], in_=rstd[:, 0:tn]), reads=[rstdB], writes=[rstdB])
                for c in range(8):
                    ti = c % 2
                    P.emit("dve", lambda h: h.tensor_tensor(out=tq[ti][:, 0:tn], in0=CV[:, c, t0:t0 + tn], in1=mean[:, 0:tn], op=ALU.subtract),
                           reads=[CVB[c], meanB], writes=[tqB[ti]])
                    P.emit("pool", lambda h: h.tensor_tensor(out=tq[ti][:, 0:tn], in0=tq[ti][:, 0:tn], in1=rstd[:, 0:tn], op=ALU.mult),
                           reads=[tqB[ti], rstdB], writes=[tqB[ti]])
                    gcol = PF["cln_g"] + c
                    bcol = PF["cln_b"] + c
                    P.emit("act", lambda h: h.activation(out=Yc[:, c, t0:t0 + tn], in_=tq[ti][:, 0:tn], func=AF.Silu, bias=pf[:, bcol:bcol + 1], scale=pf[:, gcol:gcol + 1]),
                           reads=[tqB[ti], pfB], writes=[YcB])
            for (p0, n) in pos_tiles():
                loader(xt, xtB, sem_x, n, p0)
                for half in range(2):
                    pt, pb = pp.get()
                    for kc in range(8):
                        P.emit("pe", lambda h: h.matmul(pt[0:n, :], lhsT=Yc[:, kc, p0:p0 + n], rhs=pw2[:, kc, half * 512:(half + 1) * 512], start=(kc == 0), stop=(kc == 7)),
                               reads=[YcB, pw2B], writes=[pb])
                    P.emit("dve", lambda h: h.scalar_tensor_tensor(out=z[0:n, half * 512:(half + 1) * 512], in0=xt[0:n, half * 512:(half + 1) * 512],
                                                                   scalar=ALPHA, in1=pt[0:n, :], op0=ALU.mult, op1=ALU.add), reads=[xtB, pb], writes=[zB])
                P.emit("pool", lambda h: h.tensor_tensor(out=z[0:n, :], in0=z[0:n, :], in1=gt[0:n, 2, :], op=ALU.add), reads=[zB, gtB], writes=[zB])
                layer_norm_tile(P, C, gtB, z, zB, n, gt[:, 0, :], gt[:, 1, :], zo, zoB, lnt)
                r0 = s * L + p0
                P.emit("sp", lambda h: h.dma_start(out=Hout[r0:r0 + n, :], in_=zo[0:n, :]), reads=[zoB], writes=[HoutB], dsem=sem_o)
        P.wait_all("sp", [(sem_o, sem_o.count)])
        P.flush_block([HinB, HoutB])
```

```python
import contextlib
import types
import numpy as np
import concourse.bass as bass
import concourse.mybir as mybir
from concourse.bass_utils import run_bass_kernel_spmd

F32 = mybir.dt.float32
BF16 = mybir.dt.bfloat16
ALU = mybir.AluOpType
AF = mybir.ActivationFunctionType
AX = mybir.AxisListType

D = 1024
SEQ = 2048
NMETA = 16
L = SEQ + NMETA
NCORES = 8
ALPHA = float(4.0 ** 0.25)
EPS = 1e-5
NKEY = 128
NEXP = NKEY * NKEY
GELU_K = 1.5957691216057308

PF = {}
_c = 0
for _n, _w in (("conv_w", 32), ("conv_b", 8), ("b_a", 16), ("b_x", 16), ("lam", 16), ("b_pw1", 16),
               ("dw_w", 248), ("dw_b", 8), ("cln_g", 8), ("cln_b", 8)):
    PF[_n] = _c
    _c += _w
PF_COLS = _c
PT = {"mix_g0": 0, "mix_b0": 1, "ffn_g0": 2, "ffn_b0": 3, "mix_g1": 4, "mix_b1": 5, "ffn_g1": 6, "ffn_b1": 7, "b_pw2": 8}


def freeze(fn):
    if fn.__closure__ is None:
        return fn
    cells = []
    for c in fn.__closure__:
        try:
            cells.append(types.CellType(c.cell_contents))
        except ValueError:
            cells.append(c)
    g = types.FunctionType(fn.__code__, fn.__globals__, fn.__name__, fn.__defaults__, tuple(cells))
    g.__kwdefaults__ = fn.__kwdefaults__
    return g


class Buf:
    __slots__ = ("name", "w", "r")

    def __init__(self, name=""):
        self.name = name
        self.w = None
        self.r = []


class DSem:
    __slots__ = ("h", "count")

    def __init__(self, h):
        self.h = h
        self.count = 0


class Eng:
    def __init__(self, name, sem):
        self.name = name
        self.sem = sem
        self.ops = []
        self.seen = {}


class Prog:
    def __init__(self, nc, stack):
        self.nc = nc
        self.stack = stack
        self.engs = {}
        self.nblk = 0
        for n in ("pe", "act", "dve", "pool", "sp"):
            self.engs[n] = Eng(n, None)
        self._new_sems()
        self.nops = 0

    def _new_sems(self):
        for n, e in self.engs.items():
            e.sem = DSem(self.stack.enter_context(self.nc.semaphore("s_%s_%d" % (n, self.nblk))))
            e.seen = {}

    def dsem(self, name):
        return DSem(self.stack.enter_context(self.nc.semaphore(name)))

    def emit(self, eng, fn, reads=(), writes=(), dsem=None):
        e = self.engs[eng]
        deps = {}

        def dep(sig):
            s, v = sig
            if deps.get(s, 0) < v:
                deps[s] = v

        for b in reads:
            if b.w is not None:
                dep(b.w)
        for b in writes:
            if b.w is not None:
                dep(b.w)
            for r in b.r:
                dep(r)
        if dsem is not None and dsem.count > 0:
            dep((dsem, dsem.count))
        waits = []
        for s, v in deps.items():
            if e.seen.get(s, 0) < v:
                e.seen[s] = v
                waits.append((s.h, v))
        if dsem is not None:
            dsem.count += 16
            sig = (dsem, dsem.count)
            inc = 16
        else:
            e.sem.count += 1
            sig = (e.sem, e.sem.count)
            inc = 1
        for b in reads:
            b.r.append(sig)
        for b in writes:
            b.w = sig
            b.r = []
        e.ops.append((waits, freeze(fn), sig[0].h, inc))
        self.nops += 1
        return sig

    def wait_all(self, eng, sigs):
        e = self.engs[eng]
        for s, v in sigs:
            if e.seen.get(s, 0) < v:
                e.seen[s] = v
                e.ops.append(([(s.h, v)], None, None, 0))

    def flush_block(self, bufs=()):
        nc = self.nc
        engs = self.engs

        def replay(e, h):
            for waits, fn, sh, inc in e.ops:
                for s, v in waits:
                    h.wait_ge(s, v)
                if fn is not None:
                    fn(h).then_inc(sh, inc)
            e.ops = []

        with nc.Block() as block:
            @block.tensor
            def _(h):
                replay(engs["pe"], h)

            @block.scalar
            def _(h):
                replay(engs["act"], h)

            @block.vector
            def _(h):
                replay(engs["dve"], h)

            @block.gpsimd
            def _(h):
                replay(engs["pool"], h)

            @block.sync
            def _(h):
                replay(engs["sp"], h)
        self.nblk += 1
        self._new_sems()
        for b in bufs:
            b.w = None
            b.r = []


class Ctx:
    pass


def pieces(n, step=512):
    return [(s, min(step, n - s)) for s in range(0, n, step)]


def pos_tiles():
    return [(s, min(128, L - s)) for s in range(0, L, 128)]


class PsumPool:
    def __init__(self, nc, st, n, shape, dtype, name):
        self.t = [st.enter_context(nc.psum_tensor("%s%d" % (name, i), shape, dtype)) for i in range(n)]
        self.b = [Buf("%s%d" % (name, i)) for i in range(n)]
        self.i = 0

    def get(self):
        i = self.i
        self.i = (i + 1) % len(self.t)
        return self.t[i], self.b[i]


def load_seq_tile(P, C, eng, dst, dbuf, dsem, s, p0, n):
    sigs = []
    if p0 < NMETA:
        m = min(NMETA - p0, n)
        P.emit(eng, lambda h: h.dma_start(out=dst[0:m, :], in_=C.meta[p0:p0 + m, :]), writes=[dbuf], dsem=dsem)
        if n > m:
            P.emit(eng, lambda h: h.dma_start(out=dst[m:n, :], in_=C.x[s, 0:n - m, :]), writes=[dbuf], dsem=dsem)
    else:
        P.emit(eng, lambda h: h.dma_start(out=dst[0:n, :], in_=C.x[s, p0 - NMETA:p0 - NMETA + n, :]), writes=[dbuf], dsem=dsem)


def layer_norm_tile(P, C, gbB, z, zb, n, g_ap, b_ap, out, outb, tmp):
    stats, sb = tmp["stats"], tmp["statsb"]
    for k in range(2):
        P.emit("dve", lambda h, k=k: h.bn_stats(out=stats[0:n, k, :], in_=z[0:n, k * 512:(k + 1) * 512]), reads=[zb], writes=[sb])
    mv, mvb = tmp["mv"], tmp["mvb"]
    P.emit("dve", lambda h: h.bn_aggr(out=mv[0:n, :], in_=stats[0:n, :, :]), reads=[sb], writes=[mvb])
    rs, rsb = tmp["rs"], tmp["rsb"]
    P.emit("act", lambda h: h.activation(out=rs[0:n, :], in_=mv[0:n, 1:2], func=AF.Sqrt, bias=tmp["eps"][0:n, :], scale=1.0), reads=[mvb], writes=[rsb])
    P.emit("dve", lambda h: h.reciprocal(out=rs[0:n, :], in_=rs[0:n, :]), reads=[rsb], writes=[rsb])
    P.emit("dve", lambda h: h.tensor_scalar(out=out[0:n, :], in0=z[0:n, :], scalar1=mv[0:n, 0:1], scalar2=rs[0:n, 0:1],
                                            op0=ALU.subtract, op1=ALU.mult), reads=[zb, mvb, rsb], writes=[outb])
    P.emit("pool", lambda h: h.tensor_tensor(out=out[0:n, :], in0=out[0:n, :], in1=g_ap[0:n, :], op=ALU.mult), reads=[outb, gbB], writes=[outb])
    P.emit("pool", lambda h: h.tensor_tensor(out=out[0:n, :], in0=out[0:n, :], in1=b_ap[0:n, :], op=ALU.add), reads=[outb, gbB], writes=[outb])


def alloc_ln_tmp(nc, st, P):
    t = {}
    t["stats"] = st.enter_context(nc.sbuf_tensor("ln_stats", [128, 2, 6], F32))
    t["statsb"] = Buf("ln_stats")
    t["mv"] = st.enter_context(nc.sbuf_tensor("ln_mv", [128, 2], F32))
    t["mvb"] = Buf("ln_mv")
    t["rs"] = st.enter_context(nc.sbuf_tensor("ln_rs", [128, 1], F32))
    t["rsb"] = Buf("ln_rs")
    t["eps"] = st.enter_context(nc.sbuf_tensor("ln_eps", [128, 1], F32))
    P.emit("pool", lambda h: h.memset(t["eps"][:], EPS), writes=[Buf()])
    return t


def make_ident(P, nc, st):
    idf = st.enter_context(nc.sbuf_tensor("identf", [128, 128], F32))
    idb = st.enter_context(nc.sbuf_tensor("identb", [128, 128], BF16))
    B = Buf("ident")
    P.emit("pool", lambda h: h.memset(idf[:], 1.0), writes=[B])
    P.emit("pool", lambda h: h.affine_select(out=idf[:], in_=idf[:], pattern=[[-1, 128]], base=0, channel_multiplier=1,
                                              compare_op=ALU.is_equal, fill=0.0), reads=[B], writes=[B])
    P.emit("pool", lambda h: h.tensor_copy(out=idb[:], in_=idf[:]), reads=[B], writes=[B])
    return idf, idb, B


def load_tokens_T(P, C, nc, hT, hTb, xt, xtb, xsem, idf, idB, pp, loader, tiles):
    for (c0, n, args) in tiles:
        loader(xt, xtb, xsem, n, *args)
        for half in range(2):
            pt, pb = pp.get()
            for j in range(4):
                kc = half * 4 + j
                P.emit("pe", lambda h, kc=kc, j=j, pt=pt, n=n: h.transpose(out=pt[:, j * 128:j * 128 + n], in_=xt[0:n, kc * 128:(kc + 1) * 128],
                                                                         identity=idf[0:n, 0:n]), reads=[xtb, idB], writes=[pb])
            e = "act" if half == 0 else "dve"
            if e == "act":
                P.emit("act", lambda h, half=half, pt=pt, n=n, c0=c0: h.activation(
                    out=hT[:, half * 4:half * 4 + 4, c0:c0 + n], in_=pt[:, :].rearrange("p (j t) -> p j t", j=4)[:, :, 0:n], func=AF.Copy),
                    reads=[pb], writes=[hTb])
            else:
                P.emit("dve", lambda h, half=half, pt=pt, n=n, c0=c0: h.tensor_copy(
                    out=hT[:, half * 4:half * 4 + 4, c0:c0 + n], in_=pt[:, :].rearrange("p (j t) -> p j t", j=4)[:, :, 0:n]),
                    reads=[pb], writes=[hTb])


def phase_rglru(P, C, nc, nseq):
    with contextlib.ExitStack() as st:
        sb = lambda name, shape, dt: st.enter_context(nc.sbuf_tensor("a_" + name, shape, dt))
        idf, idb, idB = make_ident(P, nc, st)
        lnt = alloc_ln_tmp(nc, st, P)
        pp = PsumPool(nc, st, 8, [128, 512], F32, "psA")
        pf = sb("pf", [128, PF_COLS], F32)
        pfB = Buf("pf")
        sem_c = P.dsem("semc_a")
        P.emit("sp", lambda h: h.dma_start(out=pf[:], in_=C.pf), writes=[pfB], dsem=sem_c)
        wout = sb("wout", [128, 8, D], BF16)
        woutB = Buf("wout")
        sem_wo = P.dsem("sem_wo")
        for kc in range(8):
            P.emit("pool", lambda h, kc=kc: h.dma_start(out=wout[:, kc, :], in_=C.w_out[kc * 128:(kc + 1) * 128, :]), writes=[woutB], dsem=sem_wo)
        wga = sb("wga", [128, 16, 128], BF16)
        wgx = sb("wgx", [128, 16, 128], BF16)
        wgB = Buf("wg")
        sem_wg = P.dsem("sem_wg")
        P.emit("pool", lambda h: h.dma_start(out=wga[:], in_=C.w_a.rearrange("r n c d -> c (r n) d")), writes=[wgB], dsem=sem_wg)
        P.emit("pool", lambda h: h.dma_start(out=wgx[:], in_=C.w_x.rearrange("r n c d -> c (r n) d")), writes=[wgB], dsem=sem_wg)
        gt = sb("lng", [128, 2, D], F32)
        gtB = Buf("lng")
        sem_g = P.dsem("sem_lng")
        P.emit("sp", lambda h: h.dma_start(out=gt[:, 0, :], in_=C.pt[PT["mix_g0"]:PT["mix_g0"] + 1, :].partition_broadcast(128)), writes=[gtB], dsem=sem_g)
        P.emit("sp", lambda h: h.dma_start(out=gt[:, 1, :], in_=C.pt[PT["mix_b0"]:PT["mix_b0"] + 1, :].partition_broadcast(128)), writes=[gtB], dsem=sem_g)
        cl = sb("cl", [128, 16], F32)
        clB = Buf("cl")
        lam = pf[:, PF["lam"]:PF["lam"] + 16]
        P.emit("act", lambda h: h.activation(out=cl[:], in_=lam, func=AF.Exp, scale=-1.0), reads=[pfB], writes=[clB])
        P.emit("act", lambda h: h.activation(out=cl[:], in_=cl[:], func=AF.Ln, bias=1.0, scale=1.0), reads=[clB], writes=[clB])
        P.emit("dve", lambda h: h.tensor_scalar(out=cl[:], in0=cl[:], scalar1=-8.0, scalar2=None, op0=ALU.mult), reads=[clB], writes=[clB])

        hT = sb("hT", [128, 8, L], BF16)
        hTB = Buf("hT")
        Y = sb("Y", [128, 8, L], BF16)
        YB = Buf("Y")
        xt = sb("xt", [128, D], F32)
        xtB = Buf("xt")
        sem_x = P.dsem("sem_x")
        win = [sb("win%d" % i, [128, 8, 256], BF16) for i in range(2)]
        winB = [Buf("win%d" % i) for i in range(2)]
        sem_win = [P.dsem("sem_win%d" % i) for i in range(2)]
        gg = sb("gg", [128, L], F32); ggB = Buf("gg")
        upad = sb("upad", [128, L + 3], F32); upB = Buf("upad")
        uc = sb("uc", [128, L], F32); ucB = Buf("uc")
        ucb = sb("ucb", [128, L], BF16); ucbB = Buf("ucb")
        ab = sb("ab", [128, L], F32); abB = Buf("ab")
        bb = sb("bb", [128, L], F32); bbB = Buf("bb")
        tm = sb("tm", [128, L], F32); tmB = Buf("tm")
        hf = sb("hf", [128, L], F32); hfB = Buf("hf")
        z = sb("z", [128, D], F32); zB = Buf("z")
        zo = sb("zo", [128, D], F32); zoB = Buf("zo")
        sem_o = P.dsem("sem_oa")
        P.emit("pool", lambda h: h.memset(upad[:], 0.0), writes=[upB])
        pcs = pieces(L)
        wcount = 0
        for s in range(nseq):
            def loader(xt_, xtb_, sem_, n, p0, s=s):
                load_seq_tile(P, C, "sp", xt_, xtb_, sem_, s, p0, n)
            load_tokens_T(P, C, nc, hT, hTB, xt, xtB, sem_x, idf, idB, pp, loader, [(p0, n, (p0,)) for (p0, n) in pos_tiles()])
            for c in range(8):
                wi = wcount % 2
                wcount += 1
                w = win[wi]
                P.emit("pool", lambda h, w=w, c=c: h.dma_start(out=w[:, :, 0:128], in_=C.w_in[:, c * 128:(c + 1) * 128].rearrange("(kc p) n -> p kc n", p=128)),
                       writes=[winB[wi]], dsem=sem_win[wi])
                P.emit("pool", lambda h, w=w, c=c: h.dma_start(out=w[:, :, 128:256], in_=C.w_in[:, D + c * 128:D + (c + 1) * 128].rearrange("(kc p) n -> p kc n", p=128)),
                       writes=[winB[wi]], dsem=sem_win[wi])
                for (t0, tn) in pcs:
                    pt, pb = pp.get()
                    for kc in range(8):
                        P.emit("pe", lambda h, pt=pt, kc=kc, t0=t0, tn=tn, w=w: h.matmul(pt[:, 0:tn], lhsT=w[:, kc, 0:128], rhs=hT[:, kc, t0:t0 + tn],
                                                                                      start=(kc == 0), stop=(kc == 7)), reads=[winB[wi], hTB], writes=[pb])
                    P.emit("act", lambda h, pt=pt, t0=t0, tn=tn: h.activation(out=gg[:, t0:t0 + tn], in_=pt[:, 0:tn], func=AF.Gelu_apprx_tanh),
                           reads=[pb], writes=[ggB])
                for (t0, tn) in pcs:
                    pt, pb = pp.get()
                    for kc in range(8):
                        P.emit("pe", lambda h, pt=pt, kc=kc, t0=t0, tn=tn, w=w: h.matmul(pt[:, 0:tn], lhsT=w[:, kc, 128:256], rhs=hT[:, kc, t0:t0 + tn],
                                                                                      start=(kc == 0), stop=(kc == 7)), reads=[winB[wi], hTB], writes=[pb])
                    P.emit("act", lambda h, pt=pt, t0=t0, tn=tn: h.activation(out=upad[:, 2 + t0:2 + t0 + tn], in_=pt[:, 0:tn], func=AF.Copy),
                           reads=[pb], writes=[upB])
                cw = PF["conv_w"]
                P.emit("dve", lambda h, c=c: h.tensor_scalar(out=uc[:], in0=upad[:, 0:L], scalar1=pf[:, cw + c:cw + c + 1],
                                                             scalar2=pf[:, PF["conv_b"] + c:PF["conv_b"] + c + 1], op0=ALU.mult, op1=ALU.add),
                       reads=[upB, pfB], writes=[ucB])
                for k in range(1, 4):
                    P.emit("dve", lambda h, c=c, k=k: h.scalar_tensor_tensor(out=uc[:], in0=upad[:, k:k + L], scalar=pf[:, cw + k * 8 + c:cw + k * 8 + c + 1],
                                                                            in1=uc[:], op0=ALU.mult, op1=ALU.add), reads=[upB, pfB, ucB], writes=[ucB])
                P.emit("pool", lambda h: h.tensor_copy(out=ucb[:], in_=uc[:]), reads=[ucB], writes=[ucbB])
                for r in range(2):
                    gi = r * 8 + c
                    for (t0, tn) in pcs:
                        pt, pb = pp.get()
                        P.emit("pe", lambda h, pt=pt, t0=t0, tn=tn, gi=gi: h.matmul(pt[:, 0:tn], lhsT=wga[:, gi, :], rhs=ucb[:, t0:t0 + tn], start=True, stop=True),
                               reads=[wgB, ucbB], writes=[pb])
                        P.emit("act", lambda h, pt=pt, t0=t0, tn=tn, gi=gi: h.activation(out=ab[:, t0:t0 + tn], in_=pt[:, 0:tn], func=AF.Sigmoid,
                                                                                       bias=pf[:, PF["b_a"] + gi:PF["b_a"] + gi + 1], scale=1.0), reads=[pb, pfB], writes=[abB])
                    for (t0, tn) in pcs:
                        pt, pb = pp.get()
                        P.emit("pe", lambda h, pt=pt, t0=t0, tn=tn, gi=gi: h.matmul(pt[:, 0:tn], lhsT=wgx[:, gi, :], rhs=ucb[:, t0:t0 + tn], start=True, stop=True),
                               reads=[wgB, ucbB], writes=[pb])
                        P.emit("act", lambda h, pt=pt, t0=t0, tn=tn, gi=gi: h.activation(out=bb[:, t0:t0 + tn], in_=pt[:, 0:tn], func=AF.Sigmoid,
                                                                                       bias=pf[:, PF["b_x"] + gi:PF["b_x"] + gi + 1], scale=1.0), reads=[pb, pfB], writes=[bbB])
                    P.emit("act", lambda h, gi=gi: h.activation(out=ab[:], in_=ab[:], func=AF.Exp, scale=cl[:, gi:gi + 1]), reads=[abB, clB], writes=[abB])
                    P.emit("pool", lambda h: h.tensor_tensor(out=bb[:], in0=bb[:], in1=uc[:], op=ALU.mult), reads=[bbB, ucB], writes=[bbB])
                    P.emit("act", lambda h: h.activation(out=tm[:], in_=ab[:], func=AF.Square), reads=[abB], writes=[tmB])
                    P.emit("act", lambda h: h.activation(out=tm[:], in_=tm[:], func=AF.Sqrt, bias=1.0, scale=-1.0), reads=[tmB], writes=[tmB])
                    P.emit("pool", lambda h: h.tensor_tensor(out=bb[:], in0=bb[:], in1=tm[:], op=ALU.mult), reads=[bbB, tmB], writes=[bbB])
                    if r == 0:
                        P.emit("dve", lambda h: h.tensor_tensor_scan(out=hf[:], data0=ab[:], data1=bb[:], initial=0.0, op0=ALU.mult, op1=ALU.add),
                               reads=[abB, bbB], writes=[hfB])
                    else:
                        P.emit("dve", lambda h: h.tensor_tensor_scan(out=tm[:, ::-1], data0=ab[:, ::-1], data1=bb[:, ::-1], initial=0.0, op0=ALU.mult, op1=ALU.add),
                               reads=[abB, bbB], writes=[tmB])
                P.emit("pool", lambda h: h.tensor_tensor(out=hf[:], in0=hf[:], in1=tm[:], op=ALU.add), reads=[hfB, tmB], writes=[hfB])
                P.emit("dve", lambda h, c=c: h.tensor_tensor(out=Y[:, c, :], in0=hf[:], in1=gg[:], op=ALU.mult), reads=[hfB, ggB], writes=[YB])
            for (p0, n) in pos_tiles():
                loader(xt, xtB, sem_x, n, p0)
                for half in range(2):
                    pt, pb = pp.get()
                    for kc in range(8):
                        P.emit("pe", lambda h, pt=pt, kc=kc, p0=p0, n=n, half=half: h.matmul(pt[0:n, :], lhsT=Y[:, kc, p0:p0 + n], rhs=wout[:, kc, half * 512:(half + 1) * 512],
                                                                                          start=(kc == 0), stop=(kc == 7)), reads=[YB, woutB], writes=[pb])
                    P.emit("dve", lambda h, pt=pt, n=n, half=half: h.scalar_tensor_tensor(out=z[0:n, half * 512:(half + 1) * 512], in0=xt[0:n, half * 512:(half + 1) * 512],
                                                                                         scalar=ALPHA, in1=pt[0:n, :], op0=ALU.mult, op1=ALU.add),
                           reads=[xtB, pb], writes=[zB])
                layer_norm_tile(P, C, gtB, z, zB, n, gt[:, 0, :], gt[:, 1, :], zo, zoB, lnt)
                r0 = s * L + p0
                P.emit("sp", lambda h, r0=r0, n=n: h.dma_start(out=C.H1[r0:r0 + n, :], in_=zo[0:n, :]), reads=[zoB], writes=[C.H1B], dsem=sem_o)
        P.wait_all("sp", [(sem_o, sem_o.count)])
        P.flush_block([C.H1B])


def build_program(nseq, stop_after=None):
    nc = bass.Bass("TRN2", target_bir_lowering=False)
    C = Ctx()
    T = nseq * L
    di = lambda name, shape: nc.dram_tensor(name, shape, F32, kind="ExternalInput").ap()
    C.x = di("x", [nseq, SEQ, D])
    C.meta = di("meta", [NMETA, D])
    C.w_in = di("w_in", [D, 2 * D])
    C.w_a = di("w_a", [2, 8, 128, 128])
    C.w_x = di("w_x", [2, 8, 128, 128])
    C.w_out = di("w_out", [D, D])
    C.pw1 = di("pw1", [D, 2 * D])
    C.pw2 = di("pw2", [D, D])
    C.wq = di("wq", [2, D, 2 * D])
    C.kt = di("kt", [2, 2, 128, 128])
    C.ut = di("ut", [2, D, NEXP])
    C.v = di("v", [2, NEXP, D])
    C.pf = di("pf", [128, PF_COLS])
    C.pt = di("pt", [len(PT), D])
    dbg = stop_after is not None
    mk = lambda name, shape, out: nc.dram_tensor(name, shape, F32, kind=("ExternalOutput" if out else "Internal")).ap()
    C.H1 = mk("H1", [T, D], dbg and stop_after == 1)
    C.H1B = Buf("H1")
    C.H2 = mk("H2", [T, D], dbg and stop_after == 2)
    C.H2B = Buf("H2")
    C.H3 = mk("H3", [T, D], dbg and stop_after == 3)
    C.H3B = Buf("H3")
    C.out = nc.dram_tensor("out", [nseq, SEQ, D], F32, kind="ExternalOutput").ap() if (stop_after is None or stop_after == 4) else None
    C.outB = Buf("out")
    with contextlib.ExitStack() as st:
        P = Prog(nc, st)
        phase_rglru(P, C, nc, nseq)
        if stop_after == 1:
            return nc
        phase_peer(P, C, nc, T, 0, C.H1, C.H1B, C.H2, C.H2B, False, "b_")
        if stop_after == 2:
            return nc
        phase_conf(P, C, nc, nseq, C.H2, C.H2B, C.H3, C.H3B, "c_")
        if stop_after == 3:
            return nc
        phase_peer(P, C, nc, T, 1, C.H3, C.H3B, C.out, C.outB, True, "d_")
    return nc


def pack_fm(v):
    v = np.asarray(v, np.float32).reshape(-1, 128)
    return np.ascontiguousarray(v.T)


def make_shared_inputs(inp):
    f = lambda a: np.ascontiguousarray(np.asarray(a, np.float32))
    pfm = np.zeros((128, PF_COLS), np.float32)
    cw = inp["lru_conv_w"][0]
    for k in range(4):
        pfm[:, PF["conv_w"] + k * 8:PF["conv_w"] + k * 8 + 8] = pack_fm(cw[k])
    pfm[:, PF["conv_b"]:PF["conv_b"] + 8] = pack_fm(inp["lru_conv_b"][0])
    pfm[:, PF["b_a"]:PF["b_a"] + 16] = pack_fm(inp["lru_b_a"][0].reshape(-1))
    pfm[:, PF["b_x"]:PF["b_x"] + 16] = pack_fm(inp["lru_b_x"][0].reshape(-1))
    pfm[:, PF["lam"]:PF["lam"] + 16] = pack_fm(inp["lru_lambda"][0].reshape(-1))
    pfm[:, PF["b_pw1"]:PF["b_pw1"] + 16] = pack_fm(inp["conf_b_pw1"][0])
    dw = inp["conf_dw_w"][0]
    for k in range(31):
        pfm[:, PF["dw_w"] + k * 8:PF["dw_w"] + k * 8 + 8] = pack_fm(dw[k])
    pfm[:, PF["dw_b"]:PF["dw_b"] + 8] = pack_fm(inp["conf_dw_b"][0])
    pfm[:, PF["cln_g"]:PF["cln_g"] + 8] = pack_fm(inp["conf_ln_g"][0])
    pfm[:, PF["cln_b"]:PF["cln_b"] + 8] = pack_fm(inp["conf_ln_b"][0])
    ptm = np.zeros((len(PT), D), np.float32)
    for i in range(2):
        ptm[PT["mix_g%d" % i]] = inp["ln_mix_g"][i]
        ptm[PT["mix_b%d" % i]] = inp["ln_mix_b"][i]
        ptm[PT["ffn_g%d" % i]] = inp["ln_ffn_g"][i]
        ptm[PT["ffn_b%d" % i]] = inp["ln_ffn_b"][i]
    ptm[PT["b_pw2"]] = inp["conf_b_pw2"][0]
    sh = {
        "meta": f(inp["meta_tokens"]),
        "w_in": f(inp["lru_w_in"][0]),
        "w_a": f(inp["lru_w_a"][0]),
        "w_x": f(inp["lru_w_x"][0]),
        "w_out": f(inp["lru_w_out"][0]),
        "pw1": f(inp["conf_w_pw1"][0]),
        "pw2": f(inp["conf_w_pw2"][0]),
        "wq": f(inp["peer_w_query"]),
        "kt": f(np.transpose(np.asarray(inp["peer_sub_keys"], np.float32), (0, 1, 3, 2))),
        "ut": f(np.transpose(np.asarray(inp["peer_u"], np.float32), (0, 2, 1))),
        "v": f(inp["peer_v"]),
        "pf": pfm,
        "pt": ptm,
    }
    return sh


def kernel(**inputs):
    x = np.asarray(inputs["x"], np.float32)
    nseq = x.shape[0] // NCORES
    sh = make_shared_inputs(inputs)
    nc = build_program(nseq)
    in_maps = []
    for c in range(NCORES):
        m = dict(sh)
        m["x"] = np.ascontiguousarray(x[c * nseq:(c + 1) * nseq])
        in_maps.append(m)
    res = run_bass_kernel_spmd(nc, in_maps, core_ids=list(range(NCORES)))
    return np.concatenate([r["out"] for r in res.results], axis=0)


def out_segments(r0, n):
    segs = []
    r = r0
    while r < r0 + n:
        s, pos = divmod(r, L)
        if pos < NMETA:
            r += min(NMETA - pos, r0 + n - r)
            continue
        cnt = min(L - pos, r0 + n - r)
        segs.append((r - r0, cnt, s, pos - NMETA))
        r += cnt
    return segs


def phase_peer(P, C, nc, T, layer, Hin, HinB, Hout, HoutB, final, pfx):
    GN = 4
    NG = NKEY // GN
    GE = GN * NKEY
    with contextlib.ExitStack() as st:
        sb = lambda name, shape, dt: st.enter_context(nc.sbuf_tensor(pfx + name, shape, dt))
        idf, idb, idB = make_ident_named(P, nc, st, pfx)
        lnt = alloc_ln_tmp_named(nc, st, P, pfx)
        ppA = PsumPool(nc, st, 2, [128, 512], F32, pfx + "psA")
        _ptt = st.enter_context(nc.psum_tensor(pfx + "psT", [128, 2, 4, 128], BF16))
        ppT = PsumPool.__new__(PsumPool); ppT.t = [_ptt[:, 0], _ptt[:, 1]]; ppT.b = [Buf("psT0"), Buf("psT1")]; ppT.i = 0
        ppO = PsumPool(nc, st, 2, [128, 1024], F32, pfx + "psO")
        s1g = st.enter_context(nc.psum_tensor(pfx + "s1g", [128, 2, 8, GN], F32)); s1gB = [Buf("s1g0"), Buf("s1g1")]
        wq = sb("wq", [128, 8, 2 * D], BF16); wqB = Buf("wq")
        sem_wq = P.dsem(pfx + "sem_wq")
        for kc in range(8):
            P.emit("pool", lambda h, kc=kc: h.dma_start(out=wq[:, kc, :], in_=C.wq[layer, kc * 128:(kc + 1) * 128, :], max_dma_last_dim=4096), writes=[wqB], dsem=sem_wq)
        kt = sb("kt", [128, 2, 128], BF16); ktB = Buf("kt")
        sem_kt = P.dsem(pfx + "sem_kt")
        P.emit("pool", lambda h: h.dma_start(out=kt[:], in_=C.kt[layer].rearrange("p k n -> k p n")), writes=[ktB], dsem=sem_kt)
        gt = sb("lng", [128, 2, D], F32); gtB = Buf("lng")
        sem_g = P.dsem(pfx + "sem_lng")
        gi, bi = PT["ffn_g%d" % layer], PT["ffn_b%d" % layer]
        P.emit("sp", lambda h: h.dma_start(out=gt[:, 0, :], in_=C.pt[gi:gi + 1, :].partition_broadcast(128)), writes=[gtB], dsem=sem_g)
        P.emit("sp", lambda h: h.dma_start(out=gt[:, 1, :], in_=C.pt[bi:bi + 1, :].partition_broadcast(128)), writes=[gtB], dsem=sem_g)
        xt = sb("xt", [128, D], F32); xtB = Buf("xt"); sem_x = P.dsem(pfx + "sem_x")
        hT = sb("hT", [128, 8, 512], BF16); hTB = Buf("hT")
        P.emit("pool", lambda h: h.memset(hT[:], 0.0), writes=[hTB])
        P.emit("pool", lambda h: h.memset(xt[:], 0.0), writes=[xtB])
        qTs = [sb("qT%d" % i, [128, 512], BF16) for i in range(2)]; qTB = [Buf("qT%d" % i) for i in range(2)]
        S = sb("S", [128, 4, 16, 128], F32); SB = [Buf("S%d" % i) for i in range(4)]
        TAU = sb("TAU", [128, 4, 8], F32); TAUB = [Buf("TAU%d" % i) for i in range(4)]
        O = sb("O", [128, 4, D], F32); OB = [Buf("O%d" % i) for i in range(4)]
        T16 = sb("T16", [128, 16, 16], F32); T16B = Buf("T16")
        tmpS = sb("tmpS", [128, 128], F32); tmpSB = Buf("tmpS")
        cand2 = sb("cand2", [128, 256], F32); cand2B = Buf("cand2")
        c24 = sb("c24", [128, 8, 24], F32); c24B = Buf("c24")
        e16 = sb("e16", [128, 8, 16], F32); e16B = Buf("e16")
        zs = sb("zs", [128, 8], F32); zsB = Buf("zs")
        mb = sb("mb", [128, 8], F32); mbB = Buf("mb")
        UT = [sb("UT%d" % i, [128, 8, GE], BF16) for i in range(2)]; UTB = [Buf("UT%d" % i) for i in range(2)]
        VG = [sb("VG%d" % i, [128, GN, D], BF16) for i in range(2)]; VGB = [Buf("VG%d" % i) for i in range(2)]
        sem_u = [P.dsem(pfx + "sem_u%d" % i) for i in range(2)]
        sem_v = [P.dsem(pfx + "sem_v%d" % i) for i in range(2)]
        G = [sb("G%d" % i, [128, 8, GN, 128], F32) for i in range(2)]; GB = [[Buf("G%d_%d" % (i, j)) for j in range(2)] for i in range(2)]
        cand = G[0][:, 0:4].rearrange("p h a n -> p (h a n)").rearrange("p (h c) -> p h c", h=8); candB = GB[0][0]
        E = [sb("E%d" % i, [128, 8, GN, 128], BF16) for i in range(2)]
        EB = [[Buf("E%d_%d" % (i, j)) for j in range(2)] for i in range(2)]
        A = [sb("A%d" % i, [128, GE], BF16) for i in range(2)]; AB = [Buf("A%d" % i) for i in range(2)]
        WA = [sb("WA%d" % i, [128, GE], BF16) for i in range(2)]; WAB = [Buf("WA%d" % i) for i in range(2)]
        WT = [sb("WT%d" % i, [128, GN, 128], BF16) for i in range(2)]; WTB = [Buf("WT%d" % i) for i in range(2)]
        z = sb("z", [128, D], F32); zB = Buf("z")
        zo = sb("zo", [128, D], F32); zoB = Buf("zo")
        sem_o = P.dsem(pfx + "sem_o")

        tiles = [(r0, min(128, T - r0)) for r0 in range(0, T, 128)]
        cnt = 0
        for s0 in range(0, len(tiles), 4):
            tl = tiles[s0:s0 + 4]
            ntl = len(tl)
            ncol = ntl * 128

            def loader(xt_, xtb_, sem_, n, r0):
                P.emit("sp", lambda h: h.dma_start(out=xt_[0:n, :], in_=Hin[r0:r0 + n, :]), reads=[HinB], writes=[xtb_], dsem=sem_)
            for i, (r0, n) in enumerate(tl):
                loader(xt, xtB, sem_x, n, r0)
                for half in range(2):
                    pt, pb = ppA.get()
                    for j in range(4):
                        kc = half * 4 + j
                        P.emit("pe", lambda h, kc=kc, j=j, pt=pt: h.transpose(out=pt[:, j * 128:(j + 1) * 128], in_=xt[:, kc * 128:(kc + 1) * 128], identity=idf[:]),
                               reads=[xtB, idB], writes=[pb])
                    if half == 0:
                        P.emit("act", lambda h, pt=pt, i=i: h.activation(out=hT[:, 0:4, i * 128:(i + 1) * 128], in_=pt[:, :].rearrange("p (j t) -> p j t", j=4), func=AF.Copy),
                               reads=[pb], writes=[hTB])
                    else:
                        P.emit("dve", lambda h, pt=pt, i=i: h.tensor_copy(out=hT[:, 4:8, i * 128:(i + 1) * 128], in_=pt[:, :].rearrange("p (j t) -> p j t", j=4)),
                               reads=[pb], writes=[hTB])
            for j in range(16):
                pt, pb = ppA.get()
                for kc in range(8):
                    P.emit("pe", lambda h, pt=pt, kc=kc, j=j: h.matmul(pt[:, 0:ncol], lhsT=wq[:, kc, j * 128:(j + 1) * 128], rhs=hT[:, kc, 0:ncol],
                                                                     start=(kc == 0), stop=(kc == 7)), reads=[wqB, hTB], writes=[pb])
                q = qTs[j % 2]; qb = qTB[j % 2]
                if j % 2 == 0:
                    P.emit("act", lambda h, pt=pt, q=q: h.activation(out=q[:, 0:ncol], in_=pt[:, 0:ncol], func=AF.Copy), reads=[pb], writes=[qb])
                else:
                    P.emit("dve", lambda h, pt=pt, q=q: h.tensor_copy(out=q[:, 0:ncol], in_=pt[:, 0:ncol]), reads=[pb], writes=[qb])
                po, pob = ppO.get()
                for i in range(ntl):
                    P.emit("pe", lambda h, po=po, i=i, q=q, j=j: h.matmul(po[:, i * 128:(i + 1) * 128], lhsT=q[:, i * 128:(i + 1) * 128], rhs=kt[:, j % 2, :], start=True, stop=True),
                           reads=[qb, ktB], writes=[pob])
                eng = "act" if j % 2 == 1 else "dve"
                if eng == "act":
                    P.emit("act", lambda h, po=po, j=j: h.activation(out=S[:, 0:ntl, j, :], in_=po[:, 0:ncol].rearrange("p (t n) -> p t n", n=128), func=AF.Copy),
                           reads=[pob], writes=SB[0:ntl])
                else:
                    P.emit("dve", lambda h, po=po, j=j: h.tensor_copy(out=S[:, 0:ntl, j, :], in_=po[:, 0:ncol].rearrange("p (t n) -> p t n", n=128)),
                           reads=[pob], writes=SB[0:ntl])
            for i in range(ntl):
                for j in range(16):
                    P.emit("dve", lambda h, i=i, j=j: h.max(out=T16[:, j, 0:8], in_=S[:, i, j, :]), reads=[SB[i]], writes=[T16B])
                    P.emit("dve", lambda h, i=i, j=j: h.match_replace(out=tmpS[:], in_to_replace=T16[:, j, 0:8], in_values=S[:, i, j, :], imm_value=-1e30),
                           reads=[SB[i], T16B], writes=[tmpSB])
                    P.emit("dve", lambda h, j=j: h.max(out=T16[:, j, 8:16], in_=tmpS[:]), reads=[tmpSB], writes=[T16B])
                P.emit("dve", lambda h: h.tensor_tensor(out=cand[:].rearrange("p h (a b) -> p h a b", a=16),
                                                        in0=T16[:, 0::2, :].unsqueeze(3).to_broadcast([128, 8, 16, 16]),
                                                        in1=T16[:, 1::2, :].unsqueeze(2).to_broadcast([128, 8, 16, 16]), op=ALU.add), reads=[T16B], writes=[candB])
                for hh in range(8):
                    P.emit("dve", lambda h, hh=hh: h.max(out=c24[:, hh, 0:8], in_=cand[:, hh, :]), reads=[candB], writes=[c24B])
                    P.emit("dve", lambda h, hh=hh: h.match_replace(out=cand2[:], in_to_replace=c24[:, hh, 0:8], in_values=cand[:, hh, :], imm_value=-1e30),
                           reads=[candB, c24B], writes=[cand2B])
                    P.emit("dve", lambda h, hh=hh: h.max(out=c24[:, hh, 8:16], in_=cand2[:]), reads=[cand2B], writes=[c24B])
                    P.emit("dve", lambda h, hh=hh: h.match_replace(out=cand2[:], in_to_replace=c24[:, hh, 8:16], in_values=cand2[:], imm_value=-1e30),
                           reads=[cand2B, c24B], writes=[cand2B])
                    P.emit("dve", lambda h, hh=hh: h.max(out=c24[:, hh, 16:24], in_=cand2[:]), reads=[cand2B], writes=[c24B])
                P.emit("dve", lambda h: h.tensor_tensor(out=e16[:], in0=c24[:, :, 0:16], in1=c24[:, :, 0:1].to_broadcast([128, 8, 16]), op=ALU.subtract),
                       reads=[c24B], writes=[e16B])
                P.emit("act", lambda h: h.activation(out=e16[:], in_=e16[:], func=AF.Exp), reads=[e16B], writes=[e16B])
                P.emit("dve", lambda h: h.tensor_reduce(out=zs[:], in_=e16[:], axis=AX.X, op=ALU.add), reads=[e16B], writes=[zsB])
                P.emit("act", lambda h: h.activation(out=zs[:], in_=zs[:], func=AF.Ln), reads=[zsB], writes=[zsB])
                P.emit("dve", lambda h: h.tensor_tensor(out=mb[:], in0=zs[:], in1=c24[:, :, 0], op=ALU.add), reads=[zsB, c24B], writes=[mbB])
                P.emit("dve", lambda h: h.tensor_tensor(out=zs[:], in0=c24[:, :, 15], in1=c24[:, :, 16], op=ALU.add), reads=[c24B, zsB], writes=[zsB])
                P.emit("dve", lambda h, i=i: h.scalar_tensor_tensor(out=TAU[:, i, :], in0=zs[:], scalar=0.5, in1=mb[:], op0=ALU.mult, op1=ALU.subtract),
                       reads=[zsB, mbB], writes=[TAUB[i]])
                P.emit("dve", lambda h, i=i: h.tensor_tensor(out=mb[:], in0=mb[:], in1=TAU[:, i, :], op=ALU.add), reads=[mbB, TAUB[i]], writes=[mbB])
                P.emit("dve", lambda h, i=i: h.tensor_tensor(out=S[:, i, 0::2, :], in0=S[:, i, 0::2, :], in1=mb[:].unsqueeze(2).to_broadcast([128, 8, 128]), op=ALU.subtract),
                       reads=[SB[i], mbB], writes=[SB[i]])

            def load_group(g):
                b = g % 2
                P.emit("pool", lambda h: h.dma_start(out=UT[b][:], in_=C.ut[layer, :, g * GE:(g + 1) * GE].rearrange("(kc p) e -> p kc e", p=128)),
                       writes=[UTB[b]], dsem=sem_u[b])
                P.emit("pool", lambda h: h.dma_start(out=VG[b][:], in_=C.v[layer, g * GE:(g + 1) * GE, :].rearrange("(ec p) d -> p ec d", p=128)),
                       writes=[VGB[b]], dsem=sem_v[b])

            items = [(g, i) for g in range(NG) for i in range(ntl)]
            po_of = {}
            ptt_of = {}

            def st_tr(k):
                eb = k % 2
                ptt, pttb = ppT.get()
                for ec in range(GN):
                    P.emit("pe", lambda h: h.transpose(out=ptt[:, ec, :], in_=WA[eb][:, ec * 128:(ec + 1) * 128], identity=idb[:]),
                           reads=[WAB[eb], idB], writes=[pttb])
                P.emit("act", lambda h: h.activation(out=WT[eb][:], in_=ptt[:], func=AF.Copy), reads=[pttb], writes=[WTB[eb]])

            def st_s1g(k):
                g, i = items[k]
                eb = k % 2
                P.emit("act", lambda h: h.activation(out=s1g[:, eb], in_=S[:, i, 0::2, g * GN:(g + 1) * GN], func=AF.Copy), reads=[SB[i]], writes=[s1gB[eb]])

            def st_front(k):
                g, i = items[k]
                b = g % 2
                eb = k % 2
                Ei, EiB = E[eb], EB[eb]
                Gk, GkB = G[eb], GB[eb]
                pa, pab = ppA.get()
                for hf in range(2):
                    hs = slice(hf * 4, hf * 4 + 4)
                    P.emit("dve", lambda h: h.tensor_tensor(out=Gk[:, hs], in0=S[:, i, 1::2, :][:, hs].unsqueeze(2).to_broadcast([128, 4, GN, 128]),
                                                            in1=s1g[:, eb, hs, :].unsqueeze(3).to_broadcast([128, 4, GN, 128]), op=ALU.add),
                           reads=[SB[i], s1gB[eb]], writes=[GkB[hf]])
                    for hh in range(hf * 4, hf * 4 + 4):
                        P.emit("act", lambda h: h.activation(out=Ei[:, hh], in_=Gk[:, hh], func=AF.Exp, bias=TAU[:, i, hh:hh + 1], scale=1.0),
                               reads=[GkB[hf], TAUB[i]], writes=[EiB[hf]])
                for kc in range(8):
                    P.emit("pe", lambda h: h.matmul(pa[:, :], lhsT=hT[:, kc, i * 128:(i + 1) * 128], rhs=UT[b][:, kc, :], start=(kc == 0), stop=(kc == 7)),
                           reads=[hTB, UTB[b]], writes=[pab])
                P.emit("act", lambda h: h.activation(out=A[eb][:], in_=pa[:, :], func=AF.Gelu_apprx_tanh), reads=[pab], writes=[AB[eb]])

            def st_mid(k):
                g, i = items[k]
                eb = k % 2
                Ei, EiB = E[eb], EB[eb]
                Gk, GkB = G[eb], GB[eb]
                for hf in range(2):
                    hs = slice(hf * 4, hf * 4 + 4)
                    P.emit("dve", lambda h: h.scalar_tensor_tensor(out=Ei[:, hs], in0=Gk[:, hs], scalar=0.0, in1=Ei[:, hs], op0=ALU.is_ge, op1=ALU.mult),
                           reads=[GkB[hf], EiB[hf]], writes=[EiB[hf]])
                P.emit("dve", lambda h: h.tensor_tensor(out=Ei[:, 0:4], in0=Ei[:, 0:4], in1=Ei[:, 4:8], op=ALU.add), reads=[EiB[0], EiB[1]], writes=[EiB[0]])
                P.emit("dve", lambda h: h.tensor_tensor(out=Ei[:, 0:2], in0=Ei[:, 0:2], in1=Ei[:, 2:4], op=ALU.add), reads=[EiB[0]], writes=[EiB[0]])
                P.emit("dve", lambda h: h.tensor_tensor(out=Ei[:, 0], in0=Ei[:, 0], in1=Ei[:, 1], op=ALU.add), reads=[EiB[0]], writes=[EiB[0]])
                P.emit("dve", lambda h: h.tensor_tensor(out=WA[eb][:], in0=A[eb][:], in1=Ei[:, 0].rearrange("p a n -> p (a n)"), op=ALU.mult),
                       reads=[AB[eb], EiB[0]], writes=[WAB[eb]])

            def st_vmm(k):
                g, i = items[k]
                b = g % 2
                eb = k % 2
                po, pob = ppO.get()
                for dh in range(2):
                    for ec in range(GN):
                        P.emit("pe", lambda h: h.matmul(po[:, dh * 512:(dh + 1) * 512], lhsT=WT[eb][:, ec, :], rhs=VG[b][:, ec, dh * 512:(dh + 1) * 512], start=(ec == 0), stop=(ec == GN - 1)),
                               reads=[WTB[eb], VGB[b]], writes=[pob])
                po_of[k] = (po, pob)

            def st_acc(k):
                g, i = items[k]
                po, pob = po_of.pop(k)
                if g == 0:
                    P.emit("act", lambda h: h.activation(out=O[:, i, :], in_=po[:, :], func=AF.Copy), reads=[pob], writes=[OB[i]])
                else:
                    P.emit("dve", lambda h: h.tensor_tensor(out=O[:, i, :], in0=O[:, i, :], in1=po[:, :], op=ALU.add), reads=[pob, OB[i]], writes=[OB[i]])

            load_group(0)
            if NG > 1:
                load_group(1)
            nit = len(items)
            st_s1g(0)
            for k in range(nit + 3):
                if k < nit:
                    st_front(k)
                if k + 1 < nit:
                    st_s1g(k + 1)
                if 0 <= k - 1 < nit:
                    st_mid(k - 1)
                    st_tr(k - 1)
                    st_vmm(k - 1)
                    gk, ik = items[k - 1]
                    if ik == ntl - 1 and gk + 2 < NG:
                        load_group(gk + 2)
                if 0 <= k - 2 < nit:
                    st_acc(k - 2)
            for i, (r0, n) in enumerate(tl):
                loader(xt, xtB, sem_x, n, r0)
                P.emit("dve", lambda h, i=i: h.scalar_tensor_tensor(out=z[:], in0=xt[:], scalar=ALPHA, in1=O[:, i, :], op0=ALU.mult, op1=ALU.add),
                       reads=[xtB, OB[i]], writes=[zB])
                layer_norm_tile(P, C, gtB, z, zB, 128, gt[:, 0, :], gt[:, 1, :], zo, zoB, lnt)
                if not final:
                    P.emit("sp", lambda h, r0=r0, n=n: h.dma_start(out=Hout[r0:r0 + n, :], in_=zo[0:n, :]), reads=[zoB], writes=[HoutB], dsem=sem_o)
                else:
                    for (ro, c, sq, ps) in out_segments(r0, n):
                        P.emit("sp", lambda h, ro=ro, c=c, sq=sq, ps=ps: h.dma_start(out=Hout[sq, ps:ps + c, :], in_=zo[ro:ro + c, :]), reads=[zoB], writes=[HoutB], dsem=sem_o)
        P.wait_all("sp", [(sem_o, sem_o.count)])
        P.flush_block([HinB, HoutB])


def make_ident_named(P, nc, st, pfx):
    idf = st.enter_context(nc.sbuf_tensor(pfx + "identf", [128, 128], F32))
    idb = st.enter_context(nc.sbuf_tensor(pfx + "identb", [128, 128], BF16))
    B = Buf("ident")
    P.emit("pool", lambda h: h.memset(idf[:], 1.0), writes=[B])
    P.emit("pool", lambda h: h.affine_select(out=idf[:], in_=idf[:], pattern=[[-1, 128]], base=0, channel_multiplier=1,
                                              compare_op=ALU.is_equal, fill=0.0), reads=[B], writes=[B])
    P.emit("pool", lambda h: h.tensor_copy(out=idb[:], in_=idf[:]), reads=[B], writes=[B])
    return idf, idb, B


def alloc_ln_tmp_named(nc, st, P, pfx):
    t = {}
    t["stats"] = st.enter_context(nc.sbuf_tensor(pfx + "ln_stats", [128, 2, 6], F32))
    t["statsb"] = Buf("ln_stats")
    t["mv"] = st.enter_context(nc.sbuf_tensor(pfx + "ln_mv", [128, 2], F32))
    t["mvb"] = Buf("ln_mv")
    t["rs"] = st.enter_context(nc.sbuf_tensor(pfx + "ln_rs", [128, 1], F32))
    t["rsb"] = Buf("ln_rs")
    t["eps"] = st.enter_context(nc.sbuf_tensor(pfx + "ln_eps", [128, 1], F32))
    P.emit("pool", lambda h: h.memset(t["eps"][:], EPS), writes=[Buf()])
    return t


def phase_conf(P, C, nc, nseq, Hin, HinB, Hout, HoutB, pfx):
    KW = 31
    PADW = KW // 2
    with contextlib.ExitStack() as st:
        sb = lambda name, shape, dt: st.enter_context(nc.sbuf_tensor(pfx + name, shape, dt))
        idf, idb, idB = make_ident_named(P, nc, st, pfx)
        lnt = alloc_ln_tmp_named(nc, st, P, pfx)
        pp = PsumPool(nc, st, 8, [128, 512], F32, pfx + "ps")
        pf = sb("pf", [128, PF_COLS], F32); pfB = Buf("pf")
        sem_c = P.dsem(pfx + "semc")
        P.emit("sp", lambda h: h.dma_start(out=pf[:], in_=C.pf), writes=[pfB], dsem=sem_c)
        ones = sb("ones", [128, 128], F32); onesB = Buf("ones")
        P.emit("pool", lambda h: h.memset(ones[:], 1.0 / D), writes=[onesB])
        pw2 = sb("pw2", [128, 8, D], BF16); pw2B = Buf("pw2")
        sem_p2 = P.dsem(pfx + "sem_p2")
        for kc in range(8):
            P.emit("pool", lambda h, kc=kc: h.dma_start(out=pw2[:, kc, :], in_=C.pw2[kc * 128:(kc + 1) * 128, :]), writes=[pw2B], dsem=sem_p2)
        gt = sb("lng", [128, 3, D], F32); gtB = Buf("lng")
        sem_g = P.dsem(pfx + "sem_lng")
        for k, nm in enumerate(("mix_g1", "mix_b1", "b_pw2")):
            P.emit("sp", lambda h, k=k, nm=nm: h.dma_start(out=gt[:, k, :], in_=C.pt[PT[nm]:PT[nm] + 1, :].partition_broadcast(128)), writes=[gtB], dsem=sem_g)
        hT = sb("hT", [128, 8, L], BF16); hTB = Buf("hT")
        CV = sb("CV", [128, 8, L], F32); CVB = [Buf("CV%d" % i) for i in range(8)]
        xt = sb("xt", [128, D], F32); xtB = Buf("xt"); sem_x = P.dsem(pfx + "sem_x")
        win = [sb("win%d" % i, [128, 8, 256], BF16) for i in range(2)]
        winB = [Buf("win%d" % i) for i in range(2)]
        sem_win = [P.dsem(pfx + "sem_win%d" % i) for i in range(2)]
        sig = sb("sig", [128, L], F32); sigB = Buf("sig")
        gpad = sb("gpad", [128, L + 2 * PADW], F32); gpB = Buf("gpad")
        P.emit("pool", lambda h: h.memset(gpad[:], 0.0), writes=[gpB])
        sq = [sb("sq%d" % i, [128, 512], F32) for i in range(2)]; sqB = [Buf("sq%d" % i) for i in range(2)]
        mean = sb("mean", [128, 512], F32); meanB = Buf("mean")
        rstd = sb("rstd", [128, 512], F32); rstdB = Buf("rstd")
        tq = [sb("tq%d" % i, [128, 512], F32) for i in range(2)]; tqB = [Buf("tq%d" % i) for i in range(2)]
        z = sb("z", [128, D], F32); zB = Buf("z")
        zo = sb("zo", [128, D], F32); zoB = Buf("zo")
        sem_o = P.dsem(pfx + "sem_o")
        pcs = pieces(L)
        wcount = 0
        sqc = 0
        for s in range(nseq):
            def loader(xt_, xtb_, sem_, n, p0, s=s):
                r0 = s * L + p0
                P.emit("sp", lambda h: h.dma_start(out=xt_[0:n, :], in_=Hin[r0:r0 + n, :]), reads=[HinB], writes=[xtb_], dsem=sem_)
            load_tokens_T(P, C, nc, hT, hTB, xt, xtB, sem_x, idf, idB, pp, loader, [(p0, n, (p0,)) for (p0, n) in pos_tiles()])
            for c in range(8):
                wi = wcount % 2
                wcount += 1
                w = win[wi]
                P.emit("pool", lambda h: h.dma_start(out=w[:, :, 0:128], in_=C.pw1[:, c * 128:(c + 1) * 128].rearrange("(kc p) n -> p kc n", p=128)),
                       writes=[winB[wi]], dsem=sem_win[wi])
                P.emit("pool", lambda h: h.dma_start(out=w[:, :, 128:256], in_=C.pw1[:, D + c * 128:D + (c + 1) * 128].rearrange("(kc p) n -> p kc n", p=128)),
                       writes=[winB[wi]], dsem=sem_win[wi])
                for (t0, tn) in pcs:
                    pa, pab = pp.get()
                    pg, pgb = pp.get()
                    for kc in range(8):
                        P.emit("pe", lambda h: h.matmul(pg[:, 0:tn], lhsT=w[:, kc, 128:256], rhs=hT[:, kc, t0:t0 + tn], start=(kc == 0), stop=(kc == 7)),
                               reads=[winB[wi], hTB], writes=[pgb])
                    for kc in range(8):
                        P.emit("pe", lambda h: h.matmul(pa[:, 0:tn], lhsT=w[:, kc, 0:128], rhs=hT[:, kc, t0:t0 + tn], start=(kc == 0), stop=(kc == 7)),
                               reads=[winB[wi], hTB], writes=[pab])
                    bg = PF["b_pw1"] + 8 + c
                    ba = PF["b_pw1"] + c
                    P.emit("act", lambda h: h.activation(out=sig[:, t0:t0 + tn], in_=pg[:, 0:tn], func=AF.Sigmoid, bias=pf[:, bg:bg + 1], scale=1.0),
                           reads=[pgb, pfB], writes=[sigB])
                    P.emit("dve", lambda h: h.scalar_tensor_tensor(out=gpad[:, PADW + t0:PADW + t0 + tn], in0=pa[:, 0:tn], scalar=pf[:, ba:ba + 1], in1=sig[:, t0:t0 + tn],
                                                                   op0=ALU.add, op1=ALU.mult), reads=[pab, pfB, sigB], writes=[gpB])
                dw = PF["dw_w"]
                db = PF["dw_b"] + c
                P.emit("dve", lambda h: h.tensor_scalar(out=CV[:, c, :], in0=gpad[:, 0:L], scalar1=pf[:, dw + c:dw + c + 1], scalar2=pf[:, db:db + 1],
                                                        op0=ALU.mult, op1=ALU.add), reads=[gpB, pfB], writes=[CVB[c]])
                for k in range(1, KW):
                    P.emit("dve", lambda h: h.scalar_tensor_tensor(out=CV[:, c, :], in0=gpad[:, k:k + L], scalar=pf[:, dw + k * 8 + c:dw + k * 8 + c + 1],
                                                                   in1=CV[:, c, :], op0=ALU.mult, op1=ALU.add), reads=[gpB, pfB, CVB[c]], writes=[CVB[c]])
            Yc, YcB = hT, hTB
            for (t0, tn) in pcs:
                pm, pmb = pp.get()
                pq, pqb = pp.get()
                for c in range(8):
                    P.emit("pe", lambda h: h.matmul(pm[:, 0:tn], lhsT=ones[:], rhs=CV[:, c, t0:t0 + tn], start=(c == 0), stop=(c == 7)),
                           reads=[onesB, CVB[c]], writes=[pmb])
                for c in range(8):
                    si = sqc % 2
                    sqc += 1
                    P.emit("act", lambda h: h.activation(out=sq[si][:, 0:tn], in_=CV[:, c, t0:t0 + tn], func=AF.Square), reads=[CVB[c]], writes=[sqB[si]])
                    P.emit("pe", lambda h: h.matmul(pq[:, 0:tn], lhsT=ones[:], rhs=sq[si][:, 0:tn], start=(c == 0), stop=(c == 7)),
                           reads=[onesB, sqB[si]], writes=[pqb])
                P.emit("act", lambda h: h.activation(out=mean[:, 0:tn], in_=pm[:, 0:tn], func=AF.Copy), reads=[pmb], writes=[meanB])
                P.emit("dve", lambda h: h.tensor_tensor(out=rstd[:, 0:tn], in0=mean[:, 0:tn], in1=mean[:, 0:tn], op=ALU.mult), reads=[meanB], writes=[rstdB])
                P.emit("dve", lambda h: h.tensor_tensor(out=rstd[:, 0:tn], in0=pq[:, 0:tn], in1=rstd[:, 0:tn], op=ALU.subtract), reads=[pqb, rstdB], writes=[rstdB])
                P.emit("act", lambda h: h.activation(out=rstd[:, 0:tn], in_=rstd[:, 0:tn], func=AF.Sqrt, bias=lnt["eps"][:, :], scale=1.0), reads=[rstdB], writes=[rstdB])
                P.emit("dve", lambda h: h.reciprocal(out=rstd[:, 0:tn], in_=rstd[:, 0:tn]), reads=[rstdB], writes=[rstdB])
                for c in range(8):
                    ti = c % 2
                    P.emit("dve", lambda h: h.tensor_tensor(out=tq[ti][:, 0:tn], in0=CV[:, c, t0:t0 + tn], in1=mean[:, 0:tn], op=ALU.subtract),
                           reads=[CVB[c], meanB], writes=[tqB[ti]])
                    P.emit("pool", lambda h: h.tensor_tensor(out=tq[ti][:, 0:tn], in0=tq[ti][:, 0:tn], in1=rstd[:, 0:tn], op=ALU.mult),
                           reads=[tqB[ti], rstdB], writes=[tqB[ti]])
                    gcol = PF["cln_g"] + c
                    bcol = PF["cln_b"] + c
                    P.emit("act", lambda h: h.activation(out=Yc[:, c, t0:t0 + tn], in_=tq[ti][:, 0:tn], func=AF.Silu, bias=pf[:, bcol:bcol + 1], scale=pf[:, gcol:gcol + 1]),
                           reads=[tqB[ti], pfB], writes=[YcB])
            for (p0, n) in pos_tiles():
                loader(xt, xtB, sem_x, n, p0)
                for half in range(2):
                    pt, pb = pp.get()
                    for kc in range(8):
                        P.emit("pe", lambda h: h.matmul(pt[0:n, :], lhsT=Yc[:, kc, p0:p0 + n], rhs=pw2[:, kc, half * 512:(half + 1) * 512], start=(kc == 0), stop=(kc == 7)),
                               reads=[YcB, pw2B], writes=[pb])
                    P.emit("dve", lambda h: h.scalar_tensor_tensor(out=z[0:n, half * 512:(half + 1) * 512], in0=xt[0:n, half * 512:(half + 1) * 512],
                                                                   scalar=ALPHA, in1=pt[0:n, :], op0=ALU.mult, op1=ALU.add), reads=[xtB, pb], writes=[zB])
                P.emit("pool", lambda h: h.tensor_tensor(out=z[0:n, :], in0=z[0:n, :], in1=gt[0:n, 2, :], op=ALU.add), reads=[zB, gtB], writes=[zB])
                layer_norm_tile(P, C, gtB, z, zB, n, gt[:, 0, :], gt[:, 1, :], zo, zoB, lnt)
                r0 = s * L + p0
                P.emit("sp", lambda h: h.dma_start(out=Hout[r0:r0 + n, :], in_=zo[0:n, :]), reads=[zoB], writes=[HoutB], dsem=sem_o)
        P.wait_all("sp", [(sem_o, sem_o.count)])
        P.flush_block([HinB, HoutB])
```

```python
import contextlib
import types
import numpy as np
import concourse.bass as bass
import concourse.mybir as mybir
from concourse.bass_utils import run_bass_kernel_spmd

F32 = mybir.dt.float32
BF16 = mybir.dt.bfloat16
ALU = mybir.AluOpType
AF = mybir.ActivationFunctionType
AX = mybir.AxisListType

D = 1024
SEQ = 2048
NMETA = 16
L = SEQ + NMETA
NCORES = 8
ALPHA = float(4.0 ** 0.25)
EPS = 1e-5
NKEY = 128
NEXP = NKEY * NKEY
GELU_K = 1.5957691216057308

PF = {}
_c = 0
for _n, _w in (("conv_w", 32), ("conv_b", 8), ("b_a", 16), ("b_x", 16), ("lam", 16), ("b_pw1", 16),
               ("dw_w", 248), ("dw_b", 8), ("cln_g", 8), ("cln_b", 8)):
    PF[_n] = _c
    _c += _w
PF_COLS = _c
PT = {"mix_g0": 0, "mix_b0": 1, "ffn_g0": 2, "ffn_b0": 3, "mix_g1": 4, "mix_b1": 5, "ffn_g1": 6, "ffn_b1": 7, "b_pw2": 8}


def freeze(fn):
    if fn.__closure__ is None:
        return fn
    cells = []
    for c in fn.__closure__:
        try:
            cells.append(types.CellType(c.cell_contents))
        except ValueError:
            cells.append(c)
    g = types.FunctionType(fn.__code__, fn.__globals__, fn.__name__, fn.__defaults__, tuple(cells))
    g.__kwdefaults__ = fn.__kwdefaults__
    return g


class Buf:
    __slots__ = ("name", "w", "r")

    def __init__(self, name=""):
        self.name = name
        self.w = None
        self.r = []


class DSem:
    __slots__ = ("h", "count")

    def __init__(self, h):
        self.h = h
        self.count = 0


class Eng:
    def __init__(self, name, sem):
        self.name = name
        self.sem = sem
        self.ops = []
        self.seen = {}


class Prog:
    def __init__(self, nc, stack):
        self.nc = nc
        self.stack = stack
        self.engs = {}
        self.nblk = 0
        for n in ("pe", "act", "dve", "pool", "sp"):
            self.engs[n] = Eng(n, None)
        self._new_sems()
        self.nops = 0

    def _new_sems(self):
        for n, e in self.engs.items():
            e.sem = DSem(self.stack.enter_context(self.nc.semaphore("s_%s_%d" % (n, self.nblk))))
            e.seen = {}

    def dsem(self, name):
        return DSem(self.stack.enter_context(self.nc.semaphore(name)))

    def emit(self, eng, fn, reads=(), writes=(), dsem=None):
        e = self.engs[eng]
        deps = {}

        def dep(sig):
            s, v = sig
            if deps.get(s, 0) < v:
                deps[s] = v

        for b in reads:
            if b.w is not None:
                dep(b.w)
        for b in writes:
            if b.w is not None:
                dep(b.w)
            for r in b.r:
                dep(r)
        if dsem is not None and dsem.count > 0:
            dep((dsem, dsem.count))
        waits = []
        for s, v in deps.items():
            if eng == "pe" and s is e.sem:
                continue
            if e.seen.get(s, 0) < v:
                e.seen[s] = v
                waits.append((s.h, v))
        if dsem is not None:
            dsem.count += 16
            sig = (dsem, dsem.count)
            inc = 16
        else:
            e.sem.count += 1
            sig = (e.sem, e.sem.count)
            inc = 1
        for b in reads:
            b.r.append(sig)
        for b in writes:
            b.w = sig
            b.r = []
        e.ops.append((waits, freeze(fn), sig[0].h, inc))
        self.nops += 1
        return sig

    def wait_all(self, eng, sigs):
        e = self.engs[eng]
        for s, v in sigs:
            if e.seen.get(s, 0) < v:
                e.seen[s] = v
                e.ops.append(([(s.h, v)], None, None, 0))

    def flush_block(self, bufs=()):
        nc = self.nc
        engs = self.engs

        def replay(e, h):
            for waits, fn, sh, inc in e.ops:
                for s, v in waits:
                    h.wait_ge(s, v)
                if fn is not None:
                    fn(h).then_inc(sh, inc)
            e.ops = []

        with nc.Block() as block:
            @block.tensor
            def _(h):
                replay(engs["pe"], h)

            @block.scalar
            def _(h):
                replay(engs["act"], h)

            @block.vector
            def _(h):
                replay(engs["dve"], h)

            @block.gpsimd
            def _(h):
                replay(engs["pool"], h)

            @block.sync
            def _(h):
                replay(engs["sp"], h)
        self.nblk += 1
        self._new_sems()
        for b in bufs:
            b.w = None
            b.r = []


class Ctx:
    pass


def pieces(n, step=512):
    return [(s, min(step, n - s)) for s in range(0, n, step)]


def pos_tiles():
    return [(s, min(128, L - s)) for s in range(0, L, 128)]


class PsumPool:
    def __init__(self, nc, st, n, shape, dtype, name):
        self.t = [st.enter_context(nc.psum_tensor("%s%d" % (name, i), shape, dtype)) for i in range(n)]
        self.b = [Buf("%s%d" % (name, i)) for i in range(n)]
        self.i = 0

    def get(self):
        i = self.i
        self.i = (i + 1) % len(self.t)
        return self.t[i], self.b[i]


def load_seq_tile(P, C, eng, dst, dbuf, dsem, s, p0, n):
    sigs = []
    if p0 < NMETA:
        m = min(NMETA - p0, n)
        P.emit(eng, lambda h: h.dma_start(out=dst[0:m, :], in_=C.meta[p0:p0 + m, :]), writes=[dbuf], dsem=dsem)
        if n > m:
            P.emit(eng, lambda h: h.dma_start(out=dst[m:n, :], in_=C.x[s, 0:n - m, :]), writes=[dbuf], dsem=dsem)
    else:
        P.emit(eng, lambda h: h.dma_start(out=dst[0:n, :], in_=C.x[s, p0 - NMETA:p0 - NMETA + n, :]), writes=[dbuf], dsem=dsem)


def layer_norm_tile(P, C, gbB, z, zb, n, g_ap, b_ap, out, outb, tmp):
    stats, sb = tmp["stats"], tmp["statsb"]
    for k in range(2):
        P.emit("dve", lambda h, k=k: h.bn_stats(out=stats[0:n, k, :], in_=z[0:n, k * 512:(k + 1) * 512]), reads=[zb], writes=[sb])
    mv, mvb = tmp["mv"], tmp["mvb"]
    P.emit("dve", lambda h: h.bn_aggr(out=mv[0:n, :], in_=stats[0:n, :, :]), reads=[sb], writes=[mvb])
    rs, rsb = tmp["rs"], tmp["rsb"]
    P.emit("act", lambda h: h.activation(out=rs[0:n, :], in_=mv[0:n, 1:2], func=AF.Sqrt, bias=tmp["eps"][0:n, :], scale=1.0), reads=[mvb], writes=[rsb])
    P.emit("dve", lambda h: h.reciprocal(out=rs[0:n, :], in_=rs[0:n, :]), reads=[rsb], writes=[rsb])
    P.emit("dve", lambda h: h.tensor_scalar(out=out[0:n, :], in0=z[0:n, :], scalar1=mv[0:n, 0:1], scalar2=rs[0:n, 0:1],
                                            op0=ALU.subtract, op1=ALU.mult), reads=[zb, mvb, rsb], writes=[outb])
    P.emit("pool", lambda h: h.tensor_tensor(out=out[0:n, :], in0=out[0:n, :], in1=g_ap[0:n, :], op=ALU.mult), reads=[outb, gbB], writes=[outb])
    P.emit("pool", lambda h: h.tensor_tensor(out=out[0:n, :], in0=out[0:n, :], in1=b_ap[0:n, :], op=ALU.add), reads=[outb, gbB], writes=[outb])


def alloc_ln_tmp(nc, st, P):
    t = {}
    t["stats"] = st.enter_context(nc.sbuf_tensor("ln_stats", [128, 2, 6], F32))
    t["statsb"] = Buf("ln_stats")
    t["mv"] = st.enter_context(nc.sbuf_tensor("ln_mv", [128, 2], F32))
    t["mvb"] = Buf("ln_mv")
    t["rs"] = st.enter_context(nc.sbuf_tensor("ln_rs", [128, 1], F32))
    t["rsb"] = Buf("ln_rs")
    t["eps"] = st.enter_context(nc.sbuf_tensor("ln_eps", [128, 1], F32))
    P.emit("pool", lambda h: h.memset(t["eps"][:], EPS), writes=[Buf()])
    return t


def make_ident(P, nc, st):
    idf = st.enter_context(nc.sbuf_tensor("identf", [128, 128], F32))
    idb = st.enter_context(nc.sbuf_tensor("identb", [128, 128], BF16))
    B = Buf("ident")
    P.emit("pool", lambda h: h.memset(idf[:], 1.0), writes=[B])
    P.emit("pool", lambda h: h.affine_select(out=idf[:], in_=idf[:], pattern=[[-1, 128]], base=0, channel_multiplier=1,
                                              compare_op=ALU.is_equal, fill=0.0), reads=[B], writes=[B])
    P.emit("pool", lambda h: h.tensor_copy(out=idb[:], in_=idf[:]), reads=[B], writes=[B])
    return idf, idb, B


def load_tokens_T(P, C, nc, hT, hTb, xt, xtb, xsem, idf, idB, pp, loader, tiles):
    for (c0, n, args) in tiles:
        loader(xt, xtb, xsem, n, *args)
        for half in range(2):
            pt, pb = pp.get()
            for j in range(4):
                kc = half * 4 + j
                P.emit("pe", lambda h, kc=kc, j=j, pt=pt, n=n: h.transpose(out=pt[:, j * 128:j * 128 + n], in_=xt[0:n, kc * 128:(kc + 1) * 128],
                                                                         identity=idf[0:n, 0:n]), reads=[xtb, idB], writes=[pb])
            e = "act" if half == 0 else "dve"
            if e == "act":
                P.emit("act", lambda h, half=half, pt=pt, n=n, c0=c0: h.activation(
                    out=hT[:, half * 4:half * 4 + 4, c0:c0 + n], in_=pt[:, :].rearrange("p (j t) -> p j t", j=4)[:, :, 0:n], func=AF.Copy),
                    reads=[pb], writes=[hTb])
            else:
                P.emit("dve", lambda h, half=half, pt=pt, n=n, c0=c0: h.tensor_copy(
                    out=hT[:, half * 4:half * 4 + 4, c0:c0 + n], in_=pt[:, :].rearrange("p (j t) -> p j t", j=4)[:, :, 0:n]),
                    reads=[pb], writes=[hTb])


def phase_rglru(P, C, nc, nseq):
    with contextlib.ExitStack() as st:
        sb = lambda name, shape, dt: st.enter_context(nc.sbuf_tensor("a_" + name, shape, dt))
        idf, idb, idB = make_ident(P, nc, st)
        lnt = alloc_ln_tmp(nc, st, P)
        pp = PsumPool(nc, st, 8, [128, 512], F32, "psA")
        pf = sb("pf", [128, PF_COLS], F32)
        pfB = Buf("pf")
        sem_c = P.dsem("semc_a")
        P.emit("sp", lambda h: h.dma_start(out=pf[:], in_=C.pf), writes=[pfB], dsem=sem_c)
        wout = sb("wout", [128, 8, D], BF16)
        woutB = Buf("wout")
        sem_wo = P.dsem("sem_wo")
        for kc in range(8):
            P.emit("pool", lambda h, kc=kc: h.dma_start(out=wout[:, kc, :], in_=C.w_out[kc * 128:(kc + 1) * 128, :]), writes=[woutB], dsem=sem_wo)
        wga = sb("wga", [128, 16, 128], BF16)
        wgx = sb("wgx", [128, 16, 128], BF16)
        wgB = Buf("wg")
        sem_wg = P.dsem("sem_wg")
        P.emit("pool", lambda h: h.dma_start(out=wga[:], in_=C.w_a.rearrange("r n c d -> c (r n) d")), writes=[wgB], dsem=sem_wg)
        P.emit("pool", lambda h: h.dma_start(out=wgx[:], in_=C.w_x.rearrange("r n c d -> c (r n) d")), writes=[wgB], dsem=sem_wg)
        gt = sb("lng", [128, 2, D], F32)
        gtB = Buf("lng")
        sem_g = P.dsem("sem_lng")
        P.emit("sp", lambda h: h.dma_start(out=gt[:, 0, :], in_=C.pt[PT["mix_g0"]:PT["mix_g0"] + 1, :].partition_broadcast(128)), writes=[gtB], dsem=sem_g)
        P.emit("sp", lambda h: h.dma_start(out=gt[:, 1, :], in_=C.pt[PT["mix_b0"]:PT["mix_b0"] + 1, :].partition_broadcast(128)), writes=[gtB], dsem=sem_g)
        cl = sb("cl", [128, 16], F32)
        clB = Buf("cl")
        lam = pf[:, PF["lam"]:PF["lam"] + 16]
        P.emit("act", lambda h: h.activation(out=cl[:], in_=lam, func=AF.Exp, scale=-1.0), reads=[pfB], writes=[clB])
        P.emit("act", lambda h: h.activation(out=cl[:], in_=cl[:], func=AF.Ln, bias=1.0, scale=1.0), reads=[clB], writes=[clB])
        P.emit("dve", lambda h: h.tensor_scalar(out=cl[:], in0=cl[:], scalar1=-8.0, scalar2=None, op0=ALU.mult), reads=[clB], writes=[clB])

        hT = sb("hT", [128, 8, L], BF16)
        hTB = Buf("hT")
        Y = sb("Y", [128, 8, L], BF16)
        YB = Buf("Y")
        xt = sb("xt", [128, D], F32)
        xtB = Buf("xt")
        sem_x = P.dsem("sem_x")
        win = [sb("win%d" % i, [128, 8, 256], BF16) for i in range(2)]
        winB = [Buf("win%d" % i) for i in range(2)]
        sem_win = [P.dsem("sem_win%d" % i) for i in range(2)]
        gg = sb("gg", [128, L], F32); ggB = Buf("gg")
        upad = sb("upad", [128, L + 3], F32); upB = Buf("upad")
        uc = sb("uc", [128, L], F32); ucB = Buf("uc")
        ucb = sb("ucb", [128, L], BF16); ucbB = Buf("ucb")
        ab = sb("ab", [128, L], F32); abB = Buf("ab")
        bb = sb("bb", [128, L], F32); bbB = Buf("bb")
        tm = sb("tm", [128, L], F32); tmB = Buf("tm")
        hf = sb("hf", [128, L], F32); hfB = Buf("hf")
        z = sb("z", [128, D], F32); zB = Buf("z")
        zo = sb("zo", [128, D], F32); zoB = Buf("zo")
        sem_o = P.dsem("sem_oa")
        P.emit("pool", lambda h: h.memset(upad[:], 0.0), writes=[upB])
        pcs = pieces(L)
        wcount = 0
        for s in range(nseq):
            def loader(xt_, xtb_, sem_, n, p0, s=s):
                load_seq_tile(P, C, "sp", xt_, xtb_, sem_, s, p0, n)
            load_tokens_T(P, C, nc, hT, hTB, xt, xtB, sem_x, idf, idB, pp, loader, [(p0, n, (p0,)) for (p0, n) in pos_tiles()])
            for c in range(8):
                wi = wcount % 2
                wcount += 1
                w = win[wi]
                P.emit("pool", lambda h, w=w, c=c: h.dma_start(out=w[:, :, 0:128], in_=C.w_in[:, c * 128:(c + 1) * 128].rearrange("(kc p) n -> p kc n", p=128)),
                       writes=[winB[wi]], dsem=sem_win[wi])
                P.emit("pool", lambda h, w=w, c=c: h.dma_start(out=w[:, :, 128:256], in_=C.w_in[:, D + c * 128:D + (c + 1) * 128].rearrange("(kc p) n -> p kc n", p=128)),
                       writes=[winB[wi]], dsem=sem_win[wi])
                for (t0, tn) in pcs:
                    pt, pb = pp.get()
                    for kc in range(8):
                        P.emit("pe", lambda h, pt=pt, kc=kc, t0=t0, tn=tn, w=w: h.matmul(pt[:, 0:tn], lhsT=w[:, kc, 0:128], rhs=hT[:, kc, t0:t0 + tn],
                                                                                      start=(kc == 0), stop=(kc == 7)), reads=[winB[wi], hTB], writes=[pb])
                    P.emit("act", lambda h, pt=pt, t0=t0, tn=tn: h.activation(out=gg[:, t0:t0 + tn], in_=pt[:, 0:tn], func=AF.Gelu_apprx_tanh),
                           reads=[pb], writes=[ggB])
                for (t0, tn) in pcs:
                    pt, pb = pp.get()
                    for kc in range(8):
                        P.emit("pe", lambda h, pt=pt, kc=kc, t0=t0, tn=tn, w=w: h.matmul(pt[:, 0:tn], lhsT=w[:, kc, 128:256], rhs=hT[:, kc, t0:t0 + tn],
                                                                                      start=(kc == 0), stop=(kc == 7)), reads=[winB[wi], hTB], writes=[pb])
                    P.emit("act", lambda h, pt=pt, t0=t0, tn=tn: h.activation(out=upad[:, 2 + t0:2 + t0 + tn], in_=pt[:, 0:tn], func=AF.Copy),
                           reads=[pb], writes=[upB])
                cw = PF["conv_w"]
                P.emit("dve", lambda h, c=c: h.tensor_scalar(out=uc[:], in0=upad[:, 0:L], scalar1=pf[:, cw + c:cw + c + 1],
                                                             scalar2=pf[:, PF["conv_b"] + c:PF["conv_b"] + c + 1], op0=ALU.mult, op1=ALU.add),
                       reads=[upB, pfB], writes=[ucB])
                for k in range(1, 4):
                    P.emit("dve", lambda h, c=c, k=k: h.scalar_tensor_tensor(out=uc[:], in0=upad[:, k:k + L], scalar=pf[:, cw + k * 8 + c:cw + k * 8 + c + 1],
                                                                            in1=uc[:], op0=ALU.mult, op1=ALU.add), reads=[upB, pfB, ucB], writes=[ucB])
                P.emit("pool", lambda h: h.tensor_copy(out=ucb[:], in_=uc[:]), reads=[ucB], writes=[ucbB])
                for r in range(2):
                    gi = r * 8 + c
                    for (t0, tn) in pcs:
                        pt, pb = pp.get()
                        P.emit("pe", lambda h, pt=pt, t0=t0, tn=tn, gi=gi: h.matmul(pt[:, 0:tn], lhsT=wga[:, gi, :], rhs=ucb[:, t0:t0 + tn], start=True, stop=True),
                               reads=[wgB, ucbB], writes=[pb])
                        P.emit("act", lambda h, pt=pt, t0=t0, tn=tn, gi=gi: h.activation(out=ab[:, t0:t0 + tn], in_=pt[:, 0:tn], func=AF.Sigmoid,
                                                                                       bias=pf[:, PF["b_a"] + gi:PF["b_a"] + gi + 1], scale=1.0), reads=[pb, pfB], writes=[abB])
                    for (t0, tn) in pcs:
                        pt, pb = pp.get()
                        P.emit("pe", lambda h, pt=pt, t0=t0, tn=tn, gi=gi: h.matmul(pt[:, 0:tn], lhsT=wgx[:, gi, :], rhs=ucb[:, t0:t0 + tn], start=True, stop=True),
                               reads=[wgB, ucbB], writes=[pb])
                        P.emit("act", lambda h, pt=pt, t0=t0, tn=tn, gi=gi: h.activation(out=bb[:, t0:t0 + tn], in_=pt[:, 0:tn], func=AF.Sigmoid,
                                                                                       bias=pf[:, PF["b_x"] + gi:PF["b_x"] + gi + 1], scale=1.0), reads=[pb, pfB], writes=[bbB])
                    P.emit("act", lambda h, gi=gi: h.activation(out=ab[:], in_=ab[:], func=AF.Exp, scale=cl[:, gi:gi + 1]), reads=[abB, clB], writes=[abB])
                    P.emit("pool", lambda h: h.tensor_tensor(out=bb[:], in0=bb[:], in1=uc[:], op=ALU.mult), reads=[bbB, ucB], writes=[bbB])
                    P.emit("act", lambda h: h.activation(out=tm[:], in_=ab[:], func=AF.Square), reads=[abB], writes=[tmB])
                    P.emit("act", lambda h: h.activation(out=tm[:], in_=tm[:], func=AF.Sqrt, bias=1.0, scale=-1.0), reads=[tmB], writes=[tmB])
                    P.emit("pool", lambda h: h.tensor_tensor(out=bb[:], in0=bb[:], in1=tm[:], op=ALU.mult), reads=[bbB, tmB], writes=[bbB])
                    if r == 0:
                        P.emit("dve", lambda h: h.tensor_tensor_scan(out=hf[:], data0=ab[:], data1=bb[:], initial=0.0, op0=ALU.mult, op1=ALU.add),
                               reads=[abB, bbB], writes=[hfB])
                    else:
                        P.emit("dve", lambda h: h.tensor_tensor_scan(out=tm[:, ::-1], data0=ab[:, ::-1], data1=bb[:, ::-1], initial=0.0, op0=ALU.mult, op1=ALU.add),
                               reads=[abB, bbB], writes=[tmB])
                P.emit("pool", lambda h: h.tensor_tensor(out=hf[:], in0=hf[:], in1=tm[:], op=ALU.add), reads=[hfB, tmB], writes=[hfB])
                P.emit("dve", lambda h, c=c: h.tensor_tensor(out=Y[:, c, :], in0=hf[:], in1=gg[:], op=ALU.mult), reads=[hfB, ggB], writes=[YB])
            for (p0, n) in pos_tiles():
                loader(xt, xtB, sem_x, n, p0)
                for half in range(2):
                    pt, pb = pp.get()
                    for kc in range(8):
                        P.emit("pe", lambda h, pt=pt, kc=kc, p0=p0, n=n, half=half: h.matmul(pt[0:n, :], lhsT=Y[:, kc, p0:p0 + n], rhs=wout[:, kc, half * 512:(half + 1) * 512],
                                                                                          start=(kc == 0), stop=(kc == 7)), reads=[YB, woutB], writes=[pb])
                    P.emit("dve", lambda h, pt=pt, n=n, half=half: h.scalar_tensor_tensor(out=z[0:n, half * 512:(half + 1) * 512], in0=xt[0:n, half * 512:(half + 1) * 512],
                                                                                         scalar=ALPHA, in1=pt[0:n, :], op0=ALU.mult, op1=ALU.add),
                           reads=[xtB, pb], writes=[zB])
                layer_norm_tile(P, C, gtB, z, zB, n, gt[:, 0, :], gt[:, 1, :], zo, zoB, lnt)
                r0 = s * L + p0
                P.emit("sp", lambda h, r0=r0, n=n: h.dma_start(out=C.H1[r0:r0 + n, :], in_=zo[0:n, :]), reads=[zoB], writes=[C.H1B], dsem=sem_o)
        P.wait_all("sp", [(sem_o, sem_o.count)])
        P.flush_block([C.H1B])


def build_program(nseq, stop_after=None):
    nc = bass.Bass("TRN2", target_bir_lowering=False)
    C = Ctx()
    T = nseq * L
    di = lambda name, shape: nc.dram_tensor(name, shape, F32, kind="ExternalInput").ap()
    C.x = di("x", [nseq, SEQ, D])
    C.meta = di("meta", [NMETA, D])
    C.w_in = di("w_in", [D, 2 * D])
    C.w_a = di("w_a", [2, 8, 128, 128])
    C.w_x = di("w_x", [2, 8, 128, 128])
    C.w_out = di("w_out", [D, D])
    C.pw1 = di("pw1", [D, 2 * D])
    C.pw2 = di("pw2", [D, D])
    C.wq = di("wq", [2, D, 2 * D])
    C.kt = di("kt", [2, 2, 128, 128])
    C.ut = di("ut", [2, D, NEXP])
    C.v = di("v", [2, NEXP, D])
    C.pf = di("pf", [128, PF_COLS])
    C.pt = di("pt", [len(PT), D])
    dbg = stop_after is not None
    mk = lambda name, shape, out: nc.dram_tensor(name, shape, F32, kind=("ExternalOutput" if out else "Internal")).ap()
    C.H1 = mk("H1", [T, D], dbg and stop_after == 1)
    C.H1B = Buf("H1")
    C.H2 = mk("H2", [T, D], dbg and stop_after == 2)
    C.H2B = Buf("H2")
    C.H3 = mk("H3", [T, D], dbg and stop_after == 3)
    C.H3B = Buf("H3")
    C.out = nc.dram_tensor("out", [nseq, SEQ, D], F32, kind="ExternalOutput").ap() if (stop_after is None or stop_after == 4) else None
    C.outB = Buf("out")
    with contextlib.ExitStack() as st:
        P = Prog(nc, st)
        phase_rglru(P, C, nc, nseq)
        if stop_after == 1:
            return nc
        phase_peer(P, C, nc, T, 0, C.H1, C.H1B, C.H2, C.H2B, False, "b_")
        if stop_after == 2:
            return nc
        phase_conf(P, C, nc, nseq, C.H2, C.H2B, C.H3, C.H3B, "c_")
        if stop_after == 3:
            return nc
        phase_peer(P, C, nc, T, 1, C.H3, C.H3B, C.out, C.outB, True, "d_")
    return nc


def pack_fm(v):
    v = np.asarray(v, np.float32).reshape(-1, 128)
    return np.ascontiguousarray(v.T)


def make_shared_inputs(inp):
    f = lambda a: np.ascontiguousarray(np.asarray(a, np.float32))
    pfm = np.zeros((128, PF_COLS), np.float32)
    cw = inp["lru_conv_w"][0]
    for k in range(4):
        pfm[:, PF["conv_w"] + k * 8:PF["conv_w"] + k * 8 + 8] = pack_fm(cw[k])
    pfm[:, PF["conv_b"]:PF["conv_b"] + 8] = pack_fm(inp["lru_conv_b"][0])
    pfm[:, PF["b_a"]:PF["b_a"] + 16] = pack_fm(inp["lru_b_a"][0].reshape(-1))
    pfm[:, PF["b_x"]:PF["b_x"] + 16] = pack_fm(inp["lru_b_x"][0].reshape(-1))
    pfm[:, PF["lam"]:PF["lam"] + 16] = pack_fm(inp["lru_lambda"][0].reshape(-1))
    pfm[:, PF["b_pw1"]:PF["b_pw1"] + 16] = pack_fm(inp["conf_b_pw1"][0])
    dw = inp["conf_dw_w"][0]
    for k in range(31):
        pfm[:, PF["dw_w"] + k * 8:PF["dw_w"] + k * 8 + 8] = pack_fm(dw[k])
    pfm[:, PF["dw_b"]:PF["dw_b"] + 8] = pack_fm(inp["conf_dw_b"][0])
    pfm[:, PF["cln_g"]:PF["cln_g"] + 8] = pack_fm(inp["conf_ln_g"][0])
    pfm[:, PF["cln_b"]:PF["cln_b"] + 8] = pack_fm(inp["conf_ln_b"][0])
    ptm = np.zeros((len(PT), D), np.float32)
    for i in range(2):
        ptm[PT["mix_g%d" % i]] = inp["ln_mix_g"][i]
        ptm[PT["mix_b%d" % i]] = inp["ln_mix_b"][i]
        ptm[PT["ffn_g%d" % i]] = inp["ln_ffn_g"][i]
        ptm[PT["ffn_b%d" % i]] = inp["ln_ffn_b"][i]
    ptm[PT["b_pw2"]] = inp["conf_b_pw2"][0]
    sh = {
        "meta": f(inp["meta_tokens"]),
        "w_in": f(inp["lru_w_in"][0]),
        "w_a": f(inp["lru_w_a"][0]),
        "w_x": f(inp["lru_w_x"][0]),
        "w_out": f(inp["lru_w_out"][0]),
        "pw1": f(inp["conf_w_pw1"][0]),
        "pw2": f(inp["conf_w_pw2"][0]),
        "wq": f(inp["peer_w_query"]),
        "kt": f(np.transpose(np.asarray(inp["peer_sub_keys"], np.float32), (0, 1, 3, 2))),
        "ut": f(np.transpose(np.asarray(inp["peer_u"], np.float32), (0, 2, 1))),
        "v": f(inp["peer_v"]),
        "pf": pfm,
        "pt": ptm,
    }
    return sh


def kernel(**inputs):
    x = np.asarray(inputs["x"], np.float32)
    nseq = x.shape[0] // NCORES
    sh = make_shared_inputs(inputs)
    nc = build_program(nseq)
    in_maps = []
    for c in range(NCORES):
        m = dict(sh)
        m["x"] = np.ascontiguousarray(x[c * nseq:(c + 1) * nseq])
        in_maps.append(m)
    res = run_bass_kernel_spmd(nc, in_maps, core_ids=list(range(NCORES)))
    return np.concatenate([r["out"] for r in res.results], axis=0)


def out_segments(r0, n):
    segs = []
    r = r0
    while r < r0 + n:
        s, pos = divmod(r, L)
        if pos < NMETA:
            r += min(NMETA - pos, r0 + n - r)
            continue
        cnt = min(L - pos, r0 + n - r)
        segs.append((r - r0, cnt, s, pos - NMETA))
        r += cnt
    return segs


def phase_peer(P, C, nc, T, layer, Hin, HinB, Hout, HoutB, final, pfx):
    GN = 4
    NG = NKEY // GN
    GE = GN * NKEY
    with contextlib.ExitStack() as st:
        sb = lambda name, shape, dt: st.enter_context(nc.sbuf_tensor(pfx + name, shape, dt))
        idf, idb, idB = make_ident_named(P, nc, st, pfx)
        lnt = alloc_ln_tmp_named(nc, st, P, pfx)
        ppA = PsumPool(nc, st, 2, [128, 512], F32, pfx + "psA")
        _ptt = st.enter_context(nc.psum_tensor(pfx + "psT", [128, 2, 4, 128], BF16))
        ppT = PsumPool.__new__(PsumPool); ppT.t = [_ptt[:, 0], _ptt[:, 1]]; ppT.b = [Buf("psT0"), Buf("psT1")]; ppT.i = 0
        ppO = PsumPool(nc, st, 2, [128, 1024], F32, pfx + "psO")
        s1g = st.enter_context(nc.psum_tensor(pfx + "s1g", [128, 2, 8, GN], F32)); s1gB = [Buf("s1g0"), Buf("s1g1")]
        wq = sb("wq", [128, 8, 2 * D], BF16); wqB = Buf("wq")
        sem_wq = P.dsem(pfx + "sem_wq")
        for kc in range(8):
            P.emit("pool", lambda h, kc=kc: h.dma_start(out=wq[:, kc, :], in_=C.wq[layer, kc * 128:(kc + 1) * 128, :], max_dma_last_dim=4096), writes=[wqB], dsem=sem_wq)
        kt = sb("kt", [128, 2, 128], BF16); ktB = Buf("kt")
        sem_kt = P.dsem(pfx + "sem_kt")
        P.emit("pool", lambda h: h.dma_start(out=kt[:], in_=C.kt[layer].rearrange("p k n -> k p n")), writes=[ktB], dsem=sem_kt)
        gt = sb("lng", [128, 2, D], F32); gtB = Buf("lng")
        sem_g = P.dsem(pfx + "sem_lng")
        gi, bi = PT["ffn_g%d" % layer], PT["ffn_b%d" % layer]
        P.emit("sp", lambda h: h.dma_start(out=gt[:, 0, :], in_=C.pt[gi:gi + 1, :].partition_broadcast(128)), writes=[gtB], dsem=sem_g)
        P.emit("sp", lambda h: h.dma_start(out=gt[:, 1, :], in_=C.pt[bi:bi + 1, :].partition_broadcast(128)), writes=[gtB], dsem=sem_g)
        xt = sb("xt", [128, D], F32); xtB = Buf("xt"); sem_x = P.dsem(pfx + "sem_x")
        hT = sb("hT", [128, 8, 512], BF16); hTB = Buf("hT")
        P.emit("pool", lambda h: h.memset(hT[:], 0.0), writes=[hTB])
        P.emit("pool", lambda h: h.memset(xt[:], 0.0), writes=[xtB])
        qTs = [sb("qT%d" % i, [128, 512], BF16) for i in range(2)]; qTB = [Buf("qT%d" % i) for i in range(2)]
        S = sb("S", [128, 4, 16, 128], F32); SB = [Buf("S%d" % i) for i in range(4)]
        TAU = sb("TAU", [128, 4, 8], F32); TAUB = [Buf("TAU%d" % i) for i in range(4)]
        O = sb("O", [128, 4, D], F32); OB = [Buf("O%d" % i) for i in range(4)]
        T16 = sb("T16", [128, 16, 16], F32); T16B = Buf("T16")
        tmpS = sb("tmpS", [128, 128], F32); tmpSB = Buf("tmpS")
        cand2 = sb("cand2", [128, 256], F32); cand2B = Buf("cand2")
        c24 = sb("c24", [128, 8, 24], F32); c24B = Buf("c24")
        e16 = sb("e16", [128, 8, 16], F32); e16B = Buf("e16")
        zs = sb("zs", [128, 8], F32); zsB = Buf("zs")
        mb = sb("mb", [128, 8], F32); mbB = Buf("mb")
        UT = [sb("UT%d" % i, [128, 8, GE], BF16) for i in range(2)]; UTB = [Buf("UT%d" % i) for i in range(2)]
        VG = [sb("VG%d" % i, [128, GN, D], BF16) for i in range(2)]; VGB = [Buf("VG%d" % i) for i in range(2)]
        sem_u = [P.dsem(pfx + "sem_u%d" % i) for i in range(2)]
        sem_v = [P.dsem(pfx + "sem_v%d" % i) for i in range(2)]
        G = [sb("G%d" % i, [128, 8, GN, 128], F32) for i in range(2)]; GB = [[Buf("G%d_%d" % (i, j)) for j in range(2)] for i in range(2)]
        cand = G[0][:, 0:4].rearrange("p h a n -> p (h a n)").rearrange("p (h c) -> p h c", h=8); candB = GB[0][0]
        E = [sb("E%d" % i, [128, 8, GN, 128], BF16) for i in range(2)]
        EB = [[Buf("E%d_%d" % (i, j)) for j in range(8)] for i in range(2)]
        A = [sb("A%d" % i, [128, GE], BF16) for i in range(2)]; AB = [Buf("A%d" % i) for i in range(2)]
        WA = [sb("WA%d" % i, [128, GE], BF16) for i in range(2)]; WAB = [Buf("WA%d" % i) for i in range(2)]
        WT = [sb("WT%d" % i, [128, GN, 128], BF16) for i in range(2)]; WTB = [Buf("WT%d" % i) for i in range(2)]
        z = sb("z", [128, D], F32); zB = Buf("z")
        zo = sb("zo", [128, D], F32); zoB = Buf("zo")
        sem_o = P.dsem(pfx + "sem_o")

        tiles = [(r0, min(128, T - r0)) for r0 in range(0, T, 128)]
        cnt = 0
        for s0 in range(0, len(tiles), 4):
            tl = tiles[s0:s0 + 4]
            ntl = len(tl)
            ncol = ntl * 128

            def loader(xt_, xtb_, sem_, n, r0):
                P.emit("sp", lambda h: h.dma_start(out=xt_[0:n, :], in_=Hin[r0:r0 + n, :]), reads=[HinB], writes=[xtb_], dsem=sem_)
            for i, (r0, n) in enumerate(tl):
                loader(xt, xtB, sem_x, n, r0)
                for half in range(2):
                    pt, pb = ppA.get()
                    for j in range(4):
                        kc = half * 4 + j
                        P.emit("pe", lambda h, kc=kc, j=j, pt=pt: h.transpose(out=pt[:, j * 128:(j + 1) * 128], in_=xt[:, kc * 128:(kc + 1) * 128], identity=idf[:]),
                               reads=[xtB, idB], writes=[pb])
                    if half == 0:
                        P.emit("act", lambda h, pt=pt, i=i: h.activation(out=hT[:, 0:4, i * 128:(i + 1) * 128], in_=pt[:, :].rearrange("p (j t) -> p j t", j=4), func=AF.Copy),
                               reads=[pb], writes=[hTB])
                    else:
                        P.emit("dve", lambda h, pt=pt, i=i: h.tensor_copy(out=hT[:, 4:8, i * 128:(i + 1) * 128], in_=pt[:, :].rearrange("p (j t) -> p j t", j=4)),
                               reads=[pb], writes=[hTB])
            for j in range(16):
                pt, pb = ppA.get()
                for kc in range(8):
                    P.emit("pe", lambda h, pt=pt, kc=kc, j=j: h.matmul(pt[:, 0:ncol], lhsT=wq[:, kc, j * 128:(j + 1) * 128], rhs=hT[:, kc, 0:ncol],
                                                                     start=(kc == 0), stop=(kc == 7)), reads=[wqB, hTB], writes=[pb])
                q = qTs[j % 2]; qb = qTB[j % 2]
                if j % 2 == 0:
                    P.emit("act", lambda h, pt=pt, q=q: h.activation(out=q[:, 0:ncol], in_=pt[:, 0:ncol], func=AF.Copy), reads=[pb], writes=[qb])
                else:
                    P.emit("dve", lambda h, pt=pt, q=q: h.tensor_copy(out=q[:, 0:ncol], in_=pt[:, 0:ncol]), reads=[pb], writes=[qb])
                po, pob = ppO.get()
                for i in range(ntl):
                    P.emit("pe", lambda h, po=po, i=i, q=q, j=j: h.matmul(po[:, i * 128:(i + 1) * 128], lhsT=q[:, i * 128:(i + 1) * 128], rhs=kt[:, j % 2, :], start=True, stop=True),
                           reads=[qb, ktB], writes=[pob])
                eng = "act" if j % 2 == 1 else "dve"
                if eng == "act":
                    P.emit("act", lambda h, po=po, j=j: h.activation(out=S[:, 0:ntl, j, :], in_=po[:, 0:ncol].rearrange("p (t n) -> p t n", n=128), func=AF.Copy),
                           reads=[pob], writes=SB[0:ntl])
                else:
                    P.emit("dve", lambda h, po=po, j=j: h.tensor_copy(out=S[:, 0:ntl, j, :], in_=po[:, 0:ncol].rearrange("p (t n) -> p t n", n=128)),
                           reads=[pob], writes=SB[0:ntl])
            for i in range(ntl):
                for j in range(16):
                    P.emit("dve", lambda h, i=i, j=j: h.max(out=T16[:, j, 0:8], in_=S[:, i, j, :]), reads=[SB[i]], writes=[T16B])
                    P.emit("dve", lambda h, i=i, j=j: h.match_replace(out=tmpS[:], in_to_replace=T16[:, j, 0:8], in_values=S[:, i, j, :], imm_value=-1e30),
                           reads=[SB[i], T16B], writes=[tmpSB])
                    P.emit("dve", lambda h, j=j: h.max(out=T16[:, j, 8:16], in_=tmpS[:]), reads=[tmpSB], writes=[T16B])
                P.emit("dve", lambda h: h.tensor_tensor(out=cand[:].rearrange("p h (a b) -> p h a b", a=16),
                                                        in0=T16[:, 0::2, :].unsqueeze(3).to_broadcast([128, 8, 16, 16]),
                                                        in1=T16[:, 1::2, :].unsqueeze(2).to_broadcast([128, 8, 16, 16]), op=ALU.add), reads=[T16B], writes=[candB])
                for hh in range(8):
                    P.emit("dve", lambda h, hh=hh: h.max(out=c24[:, hh, 0:8], in_=cand[:, hh, :]), reads=[candB], writes=[c24B])
                    P.emit("dve", lambda h, hh=hh: h.match_replace(out=cand2[:], in_to_replace=c24[:, hh, 0:8], in_values=cand[:, hh, :], imm_value=-1e30),
                           reads=[candB, c24B], writes=[cand2B])
                    P.emit("dve", lambda h, hh=hh: h.max(out=c24[:, hh, 8:16], in_=cand2[:]), reads=[cand2B], writes=[c24B])
                    P.emit("dve", lambda h, hh=hh: h.match_replace(out=cand2[:], in_to_replace=c24[:, hh, 8:16], in_values=cand2[:], imm_value=-1e30),
                           reads=[cand2B, c24B], writes=[cand2B])
                    P.emit("dve", lambda h, hh=hh: h.max(out=c24[:, hh, 16:24], in_=cand2[:]), reads=[cand2B], writes=[c24B])
                P.emit("dve", lambda h: h.tensor_tensor(out=e16[:], in0=c24[:, :, 0:16], in1=c24[:, :, 0:1].to_broadcast([128, 8, 16]), op=ALU.subtract),
                       reads=[c24B], writes=[e16B])
                P.emit("act", lambda h: h.activation(out=e16[:], in_=e16[:], func=AF.Exp), reads=[e16B], writes=[e16B])
                P.emit("dve", lambda h: h.tensor_reduce(out=zs[:], in_=e16[:], axis=AX.X, op=ALU.add), reads=[e16B], writes=[zsB])
                P.emit("act", lambda h: h.activation(out=zs[:], in_=zs[:], func=AF.Ln), reads=[zsB], writes=[zsB])
                P.emit("dve", lambda h: h.tensor_tensor(out=mb[:], in0=zs[:], in1=c24[:, :, 0], op=ALU.add), reads=[zsB, c24B], writes=[mbB])
                P.emit("dve", lambda h: h.tensor_tensor(out=zs[:], in0=c24[:, :, 15], in1=c24[:, :, 16], op=ALU.add), reads=[c24B, zsB], writes=[zsB])
                P.emit("dve", lambda h, i=i: h.scalar_tensor_tensor(out=TAU[:, i, :], in0=zs[:], scalar=0.5, in1=mb[:], op0=ALU.mult, op1=ALU.subtract),
                       reads=[zsB, mbB], writes=[TAUB[i]])
                P.emit("dve", lambda h, i=i: h.tensor_tensor(out=mb[:], in0=mb[:], in1=TAU[:, i, :], op=ALU.add), reads=[mbB, TAUB[i]], writes=[mbB])
                P.emit("dve", lambda h, i=i: h.tensor_tensor(out=S[:, i, 0::2, :], in0=S[:, i, 0::2, :], in1=mb[:].unsqueeze(2).to_broadcast([128, 8, 128]), op=ALU.subtract),
                       reads=[SB[i], mbB], writes=[SB[i]])

            def load_group(g):
                b = g % 2
                P.emit("pool", lambda h: h.dma_start(out=UT[b][:], in_=C.ut[layer, :, g * GE:(g + 1) * GE].rearrange("(kc p) e -> p kc e", p=128)),
                       writes=[UTB[b]], dsem=sem_u[b])
                P.emit("pool", lambda h: h.dma_start(out=VG[b][:], in_=C.v[layer, g * GE:(g + 1) * GE, :].rearrange("(ec p) d -> p ec d", p=128)),
                       writes=[VGB[b]], dsem=sem_v[b])

            items = [(g, i) for g in range(NG) for i in range(ntl)]
            po_of = {}
            ptt_of = {}

            def st_tr(k):
                eb = k % 2
                ptt, pttb = ppT.get()
                for ec in range(GN):
                    P.emit("pe", lambda h: h.transpose(out=ptt[:, ec, :], in_=WA[eb][:, ec * 128:(ec + 1) * 128], identity=idb[:]),
                           reads=[WAB[eb], idB], writes=[pttb])
                P.emit("act", lambda h: h.activation(out=WT[eb][:], in_=ptt[:], func=AF.Copy), reads=[pttb], writes=[WTB[eb]])

            def st_s1g(k):
                g, i = items[k]
                eb = k % 2
                P.emit("act", lambda h: h.activation(out=s1g[:, eb], in_=S[:, i, 0::2, g * GN:(g + 1) * GN], func=AF.Copy), reads=[SB[i]], writes=[s1gB[eb]])

            def st_front(k):
                g, i = items[k]
                b = g % 2
                eb = k % 2
                Ei, EiB = E[eb], EB[eb]
                Gk, GkB = G[eb], GB[eb]
                pa, pab = ppA.get()
                for hf in range(2):
                    hs = slice(hf * 4, hf * 4 + 4)
                    P.emit("dve", lambda h: h.tensor_tensor(out=Gk[:, hs], in0=S[:, i, 1::2, :][:, hs].unsqueeze(2).to_broadcast([128, 4, GN, 128]),
                                                            in1=s1g[:, eb, hs, :].unsqueeze(3).to_broadcast([128, 4, GN, 128]), op=ALU.add),
                           reads=[SB[i], s1gB[eb]], writes=[GkB[hf]])
                    for hh in range(hf * 4, hf * 4 + 4):
                        P.emit("act", lambda h: h.activation(out=Ei[:, hh], in_=Gk[:, hh], func=AF.Exp, bias=TAU[:, i, hh:hh + 1], scale=1.0),
                               reads=[GkB[hf], TAUB[i]], writes=[EiB[hh]])
                for kc in range(8):
                    P.emit("pe", lambda h: h.matmul(pa[:, :], lhsT=hT[:, kc, i * 128:(i + 1) * 128], rhs=UT[b][:, kc, :], start=(kc == 0), stop=(kc == 7)),
                           reads=[hTB, UTB[b]], writes=[pab])
                P.emit("act", lambda h: h.activation(out=A[eb][:], in_=pa[:, :], func=AF.Gelu_apprx_tanh), reads=[pab], writes=[AB[eb]])

            def st_mid(k):
                g, i = items[k]
                eb = k % 2
                Ei, EiB = E[eb], EB[eb]
                Gk, GkB = G[eb], GB[eb]
                for hf in range(2):
                    hs = slice(hf * 4, hf * 4 + 4)
                    P.emit("dve", lambda h: h.scalar_tensor_tensor(out=Ei[:, hs], in0=Gk[:, hs], scalar=0.0, in1=Ei[:, hs], op0=ALU.is_ge, op1=ALU.mult),
                           reads=[GkB[hf]] + EiB[hf * 4:hf * 4 + 4], writes=EiB[hf * 4:hf * 4 + 4])
                P.emit("dve", lambda h: h.tensor_tensor(out=Ei[:, 0:4], in0=Ei[:, 0:4], in1=Ei[:, 4:8], op=ALU.add), reads=EiB[0:8], writes=EiB[0:4])
                P.emit("dve", lambda h: h.tensor_tensor(out=Ei[:, 0:2], in0=Ei[:, 0:2], in1=Ei[:, 2:4], op=ALU.add), reads=EiB[0:4], writes=EiB[0:2])
                P.emit("dve", lambda h: h.tensor_tensor(out=Ei[:, 0], in0=Ei[:, 0], in1=Ei[:, 1], op=ALU.add), reads=EiB[0:2], writes=EiB[0:1])
                P.emit("dve", lambda h: h.tensor_tensor(out=WA[eb][:], in0=A[eb][:], in1=Ei[:, 0].rearrange("p a n -> p (a n)"), op=ALU.mult),
                       reads=[AB[eb], EiB[0]], writes=[WAB[eb]])

            def st_vmm(k):
                g, i = items[k]
                b = g % 2
                eb = k % 2
                po, pob = ppO.get()
                for dh in range(2):
                    for ec in range(GN):
                        P.emit("pe", lambda h: h.matmul(po[:, dh * 512:(dh + 1) * 512], lhsT=WT[eb][:, ec, :], rhs=VG[b][:, ec, dh * 512:(dh + 1) * 512], start=(ec == 0), stop=(ec == GN - 1)),
                               reads=[WTB[eb], VGB[b]], writes=[pob])
                po_of[k] = (po, pob)

            def st_acc(k):
                g, i = items[k]
                po, pob = po_of.pop(k)
                if g == 0:
                    P.emit("act", lambda h: h.activation(out=O[:, i, :], in_=po[:, :], func=AF.Copy), reads=[pob], writes=[OB[i]])
                else:
                    P.emit("dve", lambda h: h.tensor_tensor(out=O[:, i, :], in0=O[:, i, :], in1=po[:, :], op=ALU.add), reads=[pob, OB[i]], writes=[OB[i]])

            load_group(0)
            if NG > 1:
                load_group(1)
            nit = len(items)
            st_s1g(0)
            for k in range(nit + 3):
                if k < nit:
                    st_front(k)
                if k + 1 < nit:
                    st_s1g(k + 1)
                if 0 <= k - 1 < nit:
                    st_mid(k - 1)
                    st_tr(k - 1)
                    st_vmm(k - 1)
                    gk, ik = items[k - 1]
                    if ik == ntl - 1 and gk + 2 < NG:
                        load_group(gk + 2)
                if 0 <= k - 2 < nit:
                    st_acc(k - 2)
            for i, (r0, n) in enumerate(tl):
                loader(xt, xtB, sem_x, n, r0)
                P.emit("dve", lambda h, i=i: h.scalar_tensor_tensor(out=z[:], in0=xt[:], scalar=ALPHA, in1=O[:, i, :], op0=ALU.mult, op1=ALU.add),
                       reads=[xtB, OB[i]], writes=[zB])
                layer_norm_tile(P, C, gtB, z, zB, 128, gt[:, 0, :], gt[:, 1, :], zo, zoB, lnt)
                if not final:
                    P.emit("sp", lambda h, r0=r0, n=n: h.dma_start(out=Hout[r0:r0 + n, :], in_=zo[0:n, :]), reads=[zoB], writes=[HoutB], dsem=sem_o)
                else:
                    for (ro, c, sq, ps) in out_segments(r0, n):
                        P.emit("sp", lambda h, ro=ro, c=c, sq=sq, ps=ps: h.dma_start(out=Hout[sq, ps:ps + c, :], in_=zo[ro:ro + c, :]), reads=[zoB], writes=[HoutB], dsem=sem_o)
        P.wait_all("sp", [(sem_o, sem_o.count)])
        P.flush_block([HinB, HoutB])


def make_ident_named(P, nc, st, pfx):
    idf = st.enter_context(nc.sbuf_tensor(pfx + "identf", [128, 128], F32))
    idb = st.enter_context(nc.sbuf_tensor(pfx + "identb", [128, 128], BF16))
    B = Buf("ident")
    P.emit("pool", lambda h: h.memset(idf[:], 1.0), writes=[B])
    P.emit("pool", lambda h: h.affine_select(out=idf[:], in_=idf[:], pattern=[[-1, 128]], base=0, channel_multiplier=1,
                                              compare_op=ALU.is_equal, fill=0.0), reads=[B], writes=[B])
    P.emit("pool", lambda h: h.tensor_copy(out=idb[:], in_=idf[:]), reads=[B], writes=[B])
    return idf, idb, B


def alloc_ln_tmp_named(nc, st, P, pfx):
    t = {}
    t["stats"] = st.enter_context(nc.sbuf_tensor(pfx + "ln_stats", [128, 2, 6], F32))
    t["statsb"] = Buf("ln_stats")
    t["mv"] = st.enter_context(nc.sbuf_tensor(pfx + "ln_mv", [128, 2], F32))
    t["mvb"] = Buf("ln_mv")
    t["rs"] = st.enter_context(nc.sbuf_tensor(pfx + "ln_rs", [128, 1], F32))
    t["rsb"] = Buf("ln_rs")
    t["eps"] = st.enter_context(nc.sbuf_tensor(pfx + "ln_eps", [128, 1], F32))
    P.emit("pool", lambda h: h.memset(t["eps"][:], EPS), writes=[Buf()])
    return t


def phase_conf(P, C, nc, nseq, Hin, HinB, Hout, HoutB, pfx):
    KW = 31
    PADW = KW // 2
    with contextlib.ExitStack() as st:
        sb = lambda name, shape, dt: st.enter_context(nc.sbuf_tensor(pfx + name, shape, dt))
        idf, idb, idB = make_ident_named(P, nc, st, pfx)
        lnt = alloc_ln_tmp_named(nc, st, P, pfx)
        pp = PsumPool(nc, st, 8, [128, 512], F32, pfx + "ps")
        pf = sb("pf", [128, PF_COLS], F32); pfB = Buf("pf")
        sem_c = P.dsem(pfx + "semc")
        P.emit("sp", lambda h: h.dma_start(out=pf[:], in_=C.pf), writes=[pfB], dsem=sem_c)
        ones = sb("ones", [128, 128], F32); onesB = Buf("ones")
        P.emit("pool", lambda h: h.memset(ones[:], 1.0 / D), writes=[onesB])
        pw2 = sb("pw2", [128, 8, D], BF16); pw2B = Buf("pw2")
        sem_p2 = P.dsem(pfx + "sem_p2")
        for kc in range(8):
            P.emit("pool", lambda h, kc=kc: h.dma_start(out=pw2[:, kc, :], in_=C.pw2[kc * 128:(kc + 1) * 128, :]), writes=[pw2B], dsem=sem_p2)
        gt = sb("lng", [128, 3, D], F32); gtB = Buf("lng")
        sem_g = P.dsem(pfx + "sem_lng")
        for k, nm in enumerate(("mix_g1", "mix_b1", "b_pw2")):
            P.emit("sp", lambda h, k=k, nm=nm: h.dma_start(out=gt[:, k, :], in_=C.pt[PT[nm]:PT[nm] + 1, :].partition_broadcast(128)), writes=[gtB], dsem=sem_g)
        hT = sb("hT", [128, 8, L], BF16); hTB = Buf("hT")
        CV = sb("CV", [128, 8, L], F32); CVB = [Buf("CV%d" % i) for i in range(8)]
        xt = sb("xt", [128, D], F32); xtB = Buf("xt"); sem_x = P.dsem(pfx + "sem_x")
        win = [sb("win%d" % i, [128, 8, 256], BF16) for i in range(2)]
        winB = [Buf("win%d" % i) for i in range(2)]
        sem_win = [P.dsem(pfx + "sem_win%d" % i) for i in range(2)]
        sig = sb("sig", [128, L], F32); sigB = Buf("sig")
        gpad = sb("gpad", [128, L + 2 * PADW], F32); gpB = Buf("gpad")
        P.emit("pool", lambda h: h.memset(gpad[:], 0.0), writes=[gpB])
        sq = [sb("sq%d" % i, [128, 512], F32) for i in range(2)]; sqB = [Buf("sq%d" % i) for i in range(2)]
        mean = sb("mean", [128, 512], F32); meanB = Buf("mean")
        rstd = sb("rstd", [128, 512], F32); rstdB = Buf("rstd")
        tq = [sb("tq%d" % i, [128, 512], F32) for i in range(2)]; tqB = [Buf("tq%d" % i) for i in range(2)]
        z = sb("z", [128, D], F32); zB = Buf("z")
        zo = sb("zo", [128, D], F32); zoB = Buf("zo")
        sem_o = P.dsem(pfx + "sem_o")
        pcs = pieces(L)
        wcount = 0
        sqc = 0
        for s in range(nseq):
            def loader(xt_, xtb_, sem_, n, p0, s=s):
                r0 = s * L + p0
                P.emit("sp", lambda h: h.dma_start(out=xt_[0:n, :], in_=Hin[r0:r0 + n, :]), reads=[HinB], writes=[xtb_], dsem=sem_)
            load_tokens_T(P, C, nc, hT, hTB, xt, xtB, sem_x, idf, idB, pp, loader, [(p0, n, (p0,)) for (p0, n) in pos_tiles()])
            for c in range(8):
                wi = wcount % 2
                wcount += 1
                w = win[wi]
                P.emit("pool", lambda h: h.dma_start(out=w[:, :, 0:128], in_=C.pw1[:, c * 128:(c + 1) * 128].rearrange("(kc p) n -> p kc n", p=128)),
                       writes=[winB[wi]], dsem=sem_win[wi])
                P.emit("pool", lambda h: h.dma_start(out=w[:, :, 128:256], in_=C.pw1[:, D + c * 128:D + (c + 1) * 128].rearrange("(kc p) n -> p kc n", p=128)),
                       writes=[winB[wi]], dsem=sem_win[wi])
                for (t0, tn) in pcs:
                    pa, pab = pp.get()
                    pg, pgb = pp.get()
                    for kc in range(8):
                        P.emit("pe", lambda h: h.matmul(pg[:, 0:tn], lhsT=w[:, kc, 128:256], rhs=hT[:, kc, t0:t0 + tn], start=(kc == 0), stop=(kc == 7)),
                               reads=[winB[wi], hTB], writes=[pgb])
                    for kc in range(8):
                        P.emit("pe", lambda h: h.matmul(pa[:, 0:tn], lhsT=w[:, kc, 0:128], rhs=hT[:, kc, t0:t0 + tn], start=(kc == 0), stop=(kc == 7)),
                               reads=[winB[wi], hTB], writes=[pab])
                    bg = PF["b_pw1"] + 8 + c
                    ba = PF["b_pw1"] + c
                    P.emit("act", lambda h: h.activation(out=sig[:, t0:t0 + tn], in_=pg[:, 0:tn], func=AF.Sigmoid, bias=pf[:, bg:bg + 1], scale=1.0),
                           reads=[pgb, pfB], writes=[sigB])
                    P.emit("dve", lambda h: h.scalar_tensor_tensor(out=gpad[:, PADW + t0:PADW + t0 + tn], in0=pa[:, 0:tn], scalar=pf[:, ba:ba + 1], in1=sig[:, t0:t0 + tn],
                                                                   op0=ALU.add, op1=ALU.mult), reads=[pab, pfB, sigB], writes=[gpB])
                dw = PF["dw_w"]
                db = PF["dw_b"] + c
                P.emit("dve", lambda h: h.tensor_scalar(out=CV[:, c, :], in0=gpad[:, 0:L], scalar1=pf[:, dw + c:dw + c + 1], scalar2=pf[:, db:db + 1],
                                                        op0=ALU.mult, op1=ALU.add), reads=[gpB, pfB], writes=[CVB[c]])
                for k in range(1, KW):
                    P.emit("dve", lambda h: h.scalar_tensor_tensor(out=CV[:, c, :], in0=gpad[:, k:k + L], scalar=pf[:, dw + k * 8 + c:dw + k * 8 + c + 1],
                                                                   in1=CV[:, c, :], op0=ALU.mult, op1=ALU.add), reads=[gpB, pfB, CVB[c]], writes=[CVB[c]])
            Yc, YcB = hT, hTB
            for (t0, tn) in pcs:
                pm, pmb = pp.get()
                pq, pqb = pp.get()
                for c in range(8):
                    P.emit("pe", lambda h: h.matmul(pm[:, 0:tn], lhsT=ones[:], rhs=CV[:, c, t0:t0 + tn], start=(c == 0), stop=(c == 7)),
                           reads=[onesB, CVB[c]], writes=[pmb])
                for c in range(8):
                    si = sqc % 2
                    sqc += 1
                    P.emit("act", lambda h: h.activation(out=sq[si][:, 0:tn], in_=CV[:, c, t0:t0 + tn], func=AF.Square), reads=[CVB[c]], writes=[sqB[si]])
                    P.emit("pe", lambda h: h.matmul(pq[:, 0:tn], lhsT=ones[:], rhs=sq[si][:, 0:tn], start=(c == 0), stop=(c == 7)),
                           reads=[onesB, sqB[si]], writes=[pqb])
                P.emit("act", lambda h: h.activation(out=mean[:, 0:tn], in_=pm[:, 0:tn], func=AF.Copy), reads=[pmb], writes=[meanB])
                P.emit("dve", lambda h: h.tensor_tensor(out=rstd[:, 0:tn], in0=mean[:, 0:tn], in1=mean[:, 0:tn], op=ALU.mult), reads=[meanB], writes=[rstdB])
                P.emit("dve", lambda h: h.tensor_tensor(out=rstd[:, 0:tn], in0=pq[:, 0:tn], in1=rstd[:, 0:tn], op=ALU.subtract), reads=[pqb, rstdB], writes=[rstdB])
                P.emit("act", lambda h: h.activation(out=rstd[:, 0:tn], in_=rstd[:, 0:tn], func=AF.Sqrt, bias=lnt["eps"][:, :], scale=1.0), reads=[rstdB], writes=[rstdB])
                P.emit("dve", lambda h: h.reciprocal(out=rstd[:, 0:tn], in_=rstd[:, 0:tn]), reads=[rstdB], writes=[rstdB])
                for c in range(8):
                    ti = c % 2
                    P.emit("dve", lambda h: h.tensor_tensor(out=tq[ti][:, 0:tn], in0=CV[:, c, t0:t0 + tn], in1=mean[:, 0:tn], op=ALU.subtract),
                           reads=[CVB[c], meanB], writes=[tqB[ti]])
                    P.emit("pool", lambda h: h.tensor_tensor(out=tq[ti][:, 0:tn], in0=tq[ti][:, 0:tn], in1=rstd[:, 0:tn], op=ALU.mult),
                           reads=[tqB[ti], rstdB], writes=[tqB[ti]])
                    gcol = PF["cln_g"] + c
                    bcol = PF["cln_b"] + c
                    P.emit("act", lambda h: h.activation(out=Yc[:, c, t0:t0 + tn], in_=tq[ti][:, 0:tn], func=AF.Silu, bias=pf[:, bcol:bcol + 1], scale=pf[:, gcol:gcol + 1]),
                           reads=[tqB[ti], pfB], writes=[YcB])
            for (p0, n) in pos_tiles():
                loader(xt, xtB, sem_x, n, p0)
                for half in range(2):
                    pt, pb = pp.get()
                    for kc in range(8):
                        P.emit("pe", lambda h: h.matmul(pt[0:n, :], lhsT=Yc[:, kc, p0:p0 + n], rhs=pw2[:, kc, half * 512:(half + 1) * 512], start=(kc == 0), stop=(kc == 7)),
                               reads=[YcB, pw2B], writes=[pb])
                    P.emit("dve", lambda h: h.scalar_tensor_tensor(out=z[0:n, half * 512:(half + 1) * 512], in0=xt[0:n, half * 512:(half + 1) * 512],
                                                                   scalar=ALPHA, in1=pt[0:n, :], op0=ALU.mult, op1=ALU.add), reads=[xtB, pb], writes=[zB])
                P.emit("pool", lambda h: h.tensor_tensor(out=z[0:n, :], in0=z[0:n, :], in1=gt[0:n, 2, :], op=ALU.add), reads=[zB, gtB], writes=[zB])
                layer_norm_tile(P, C, gtB, z, zB, n, gt[:, 0, :], gt[:, 1, :], zo, zoB, lnt)
                r0 = s * L + p0
                P.emit("sp", lambda h: h.dma_start(out=Hout[r0:r0 + n, :], in_=zo[0:n, :]), reads=[zoB], writes=[HoutB], dsem=sem_o)
        P.wait_all("sp", [(sem_o, sem_o.count)])
        P.flush_block([HinB, HoutB])
```

```python
import contextlib
import types
import numpy as np
import concourse.bass as bass
import concourse.mybir as mybir
from concourse.bass_utils import run_bass_kernel_spmd

F32 = mybir.dt.float32
BF16 = mybir.dt.bfloat16
ALU = mybir.AluOpType
AF = mybir.ActivationFunctionType
AX = mybir.AxisListType

D = 1024
SEQ = 2048
NMETA = 16
L = SEQ + NMETA
NCORES = 8
ALPHA = float(4.0 ** 0.25)
EPS = 1e-5
NKEY = 128
NEXP = NKEY * NKEY
GELU_K = 1.5957691216057308

PF = {}
_c = 0
for _n, _w in (("conv_w", 32), ("conv_b", 8), ("b_a", 16), ("b_x", 16), ("lam", 16), ("b_pw1", 16),
               ("dw_w", 248), ("dw_b", 8), ("cln_g", 8), ("cln_b", 8)):
    PF[_n] = _c
    _c += _w
PF_COLS = _c
PT = {"mix_g0": 0, "mix_b0": 1, "ffn_g0": 2, "ffn_b0": 3, "mix_g1": 4, "mix_b1": 5, "ffn_g1": 6, "ffn_b1": 7, "b_pw2": 8}


def freeze(fn):
    if fn.__closure__ is None:
        return fn
    cells = []
    for c in fn.__closure__:
        try:
            cells.append(types.CellType(c.cell_contents))
        except ValueError:
            cells.append(c)
    g = types.FunctionType(fn.__code__, fn.__globals__, fn.__name__, fn.__defaults__, tuple(cells))
    g.__kwdefaults__ = fn.__kwdefaults__
    return g


class Buf:
    __slots__ = ("name", "w", "r")

    def __init__(self, name=""):
        self.name = name
        self.w = None
        self.r = []


class DSem:
    __slots__ = ("h", "count")

    def __init__(self, h):
        self.h = h
        self.count = 0


class Eng:
    def __init__(self, name, sem):
        self.name = name
        self.sem = sem
        self.ops = []
        self.seen = {}


class Prog:
    def __init__(self, nc, stack):
        self.nc = nc
        self.stack = stack
        self.engs = {}
        self.nblk = 0
        for n in ("pe", "act", "dve", "pool", "sp"):
            self.engs[n] = Eng(n, None)
        self._new_sems()
        self.nops = 0

    def _new_sems(self):
        for n, e in self.engs.items():
            e.sem = DSem(self.stack.enter_context(self.nc.semaphore("s_%s_%d" % (n, self.nblk))))
            e.seen = {}

    def dsem(self, name):
        return DSem(self.stack.enter_context(self.nc.semaphore(name)))

    def emit(self, eng, fn, reads=(), writes=(), dsem=None):
        e = self.engs[eng]
        deps = {}

        def dep(sig):
            s, v = sig
            if deps.get(s, 0) < v:
                deps[s] = v

        for b in reads:
            if b.w is not None:
                dep(b.w)
        for b in writes:
            if b.w is not None:
                dep(b.w)
            for r in b.r:
                dep(r)
        if dsem is not None and dsem.count > 0:
            dep((dsem, dsem.count))
        waits = []
        for s, v in deps.items():
            if eng == "pe" and s is e.sem:
                continue
            if e.seen.get(s, 0) < v:
                e.seen[s] = v
                waits.append((s.h, v))
        if dsem is not None:
            dsem.count += 16
            sig = (dsem, dsem.count)
            inc = 16
        else:
            e.sem.count += 1
            sig = (e.sem, e.sem.count)
            inc = 1
        for b in reads:
            b.r.append(sig)
        for b in writes:
            b.w = sig
            b.r = []
        e.ops.append((waits, freeze(fn), sig[0].h, inc))
        self.nops += 1
        return sig

    def wait_all(self, eng, sigs):
        e = self.engs[eng]
        for s, v in sigs:
            if e.seen.get(s, 0) < v:
                e.seen[s] = v
                e.ops.append(([(s.h, v)], None, None, 0))

    def flush_block(self, bufs=()):
        nc = self.nc
        engs = self.engs

        def replay(e, h):
            for waits, fn, sh, inc in e.ops:
                for s, v in waits:
                    h.wait_ge(s, v)
                if fn is not None:
                    fn(h).then_inc(sh, inc)
            e.ops = []

        with nc.Block() as block:
            @block.tensor
            def _(h):
                replay(engs["pe"], h)

            @block.scalar
            def _(h):
                replay(engs["act"], h)

            @block.vector
            def _(h):
                replay(engs["dve"], h)

            @block.gpsimd
            def _(h):
                replay(engs["pool"], h)

            @block.sync
            def _(h):
                replay(engs["sp"], h)
        self.nblk += 1
        self._new_sems()
        for b in bufs:
            b.w = None
            b.r = []


class Ctx:
    pass


def pieces(n, step=512):
    return [(s, min(step, n - s)) for s in range(0, n, step)]


def pos_tiles():
    return [(s, min(128, L - s)) for s in range(0, L, 128)]


class PsumPool:
    def __init__(self, nc, st, n, shape, dtype, name):
        self.t = [st.enter_context(nc.psum_tensor("%s%d" % (name, i), shape, dtype)) for i in range(n)]
        self.b = [Buf("%s%d" % (name, i)) for i in range(n)]
        self.i = 0

    def get(self):
        i = self.i
        self.i = (i + 1) % len(self.t)
        return self.t[i], self.b[i]


def load_seq_tile(P, C, eng, dst, dbuf, dsem, s, p0, n):
    sigs = []
    if p0 < NMETA:
        m = min(NMETA - p0, n)
        P.emit(eng, lambda h: h.dma_start(out=dst[0:m, :], in_=C.meta[p0:p0 + m, :]), writes=[dbuf], dsem=dsem)
        if n > m:
            P.emit(eng, lambda h: h.dma_start(out=dst[m:n, :], in_=C.x[s, 0:n - m, :]), writes=[dbuf], dsem=dsem)
    else:
        P.emit(eng, lambda h: h.dma_start(out=dst[0:n, :], in_=C.x[s, p0 - NMETA:p0 - NMETA + n, :]), writes=[dbuf], dsem=dsem)


def layer_norm_tile(P, C, gbB, z, zb, n, g_ap, b_ap, out, outb, tmp):
    stats, sb = tmp["stats"], tmp["statsb"]
    for k in range(2):
        P.emit("dve", lambda h, k=k: h.bn_stats(out=stats[0:n, k, :], in_=z[0:n, k * 512:(k + 1) * 512]), reads=[zb], writes=[sb])
    mv, mvb = tmp["mv"], tmp["mvb"]
    P.emit("dve", lambda h: h.bn_aggr(out=mv[0:n, :], in_=stats[0:n, :, :]), reads=[sb], writes=[mvb])
    rs, rsb = tmp["rs"], tmp["rsb"]
    P.emit("act", lambda h: h.activation(out=rs[0:n, :], in_=mv[0:n, 1:2], func=AF.Sqrt, bias=tmp["eps"][0:n, :], scale=1.0), reads=[mvb], writes=[rsb])
    P.emit("dve", lambda h: h.reciprocal(out=rs[0:n, :], in_=rs[0:n, :]), reads=[rsb], writes=[rsb])
    P.emit("dve", lambda h: h.tensor_scalar(out=out[0:n, :], in0=z[0:n, :], scalar1=mv[0:n, 0:1], scalar2=rs[0:n, 0:1],
                                            op0=ALU.subtract, op1=ALU.mult), reads=[zb, mvb, rsb], writes=[outb])
    P.emit("pool", lambda h: h.tensor_tensor(out=out[0:n, :], in0=out[0:n, :], in1=g_ap[0:n, :], op=ALU.mult), reads=[outb, gbB], writes=[outb])
    P.emit("pool", lambda h: h.tensor_tensor(out=out[0:n, :], in0=out[0:n, :], in1=b_ap[0:n, :], op=ALU.add), reads=[outb, gbB], writes=[outb])


def alloc_ln_tmp(nc, st, P):
    t = {}
    t["stats"] = st.enter_context(nc.sbuf_tensor("ln_stats", [128, 2, 6], F32))
    t["statsb"] = Buf("ln_stats")
    t["mv"] = st.enter_context(nc.sbuf_tensor("ln_mv", [128, 2], F32))
    t["mvb"] = Buf("ln_mv")
    t["rs"] = st.enter_context(nc.sbuf_tensor("ln_rs", [128, 1], F32))
    t["rsb"] = Buf("ln_rs")
    t["eps"] = st.enter_context(nc.sbuf_tensor("ln_eps", [128, 1], F32))
    P.emit("pool", lambda h: h.memset(t["eps"][:], EPS), writes=[Buf()])
    return t


def make_ident(P, nc, st):
    idf = st.enter_context(nc.sbuf_tensor("identf", [128, 128], F32))
    idb = st.enter_context(nc.sbuf_tensor("identb", [128, 128], BF16))
    B = Buf("ident")
    P.emit("pool", lambda h: h.memset(idf[:], 1.0), writes=[B])
    P.emit("pool", lambda h: h.affine_select(out=idf[:], in_=idf[:], pattern=[[-1, 128]], base=0, channel_multiplier=1,
                                              compare_op=ALU.is_equal, fill=0.0), reads=[B], writes=[B])
    P.emit("pool", lambda h: h.tensor_copy(out=idb[:], in_=idf[:]), reads=[B], writes=[B])
    return idf, idb, B


def load_tokens_T(P, C, nc, hT, hTb, xt, xtb, xsem, idf, idB, pp, loader, tiles):
    for (c0, n, args) in tiles:
        loader(xt, xtb, xsem, n, *args)
        for half in range(2):
            pt, pb = pp.get()
            for j in range(4):
                kc = half * 4 + j
                P.emit("pe", lambda h, kc=kc, j=j, pt=pt, n=n: h.transpose(out=pt[:, j * 128:j * 128 + n], in_=xt[0:n, kc * 128:(kc + 1) * 128],
                                                                         identity=idf[0:n, 0:n]), reads=[xtb, idB], writes=[pb])
            e = "act" if half == 0 else "dve"
            if e == "act":
                P.emit("act", lambda h, half=half, pt=pt, n=n, c0=c0: h.activation(
                    out=hT[:, half * 4:half * 4 + 4, c0:c0 + n], in_=pt[:, :].rearrange("p (j t) -> p j t", j=4)[:, :, 0:n], func=AF.Copy),
                    reads=[pb], writes=[hTb])
            else:
                P.emit("dve", lambda h, half=half, pt=pt, n=n, c0=c0: h.tensor_copy(
                    out=hT[:, half * 4:half * 4 + 4, c0:c0 + n], in_=pt[:, :].rearrange("p (j t) -> p j t", j=4)[:, :, 0:n]),
                    reads=[pb], writes=[hTb])


def phase_rglru(P, C, nc, nseq):
    with contextlib.ExitStack() as st:
        sb = lambda name, shape, dt: st.enter_context(nc.sbuf_tensor("a_" + name, shape, dt))
        idf, idb, idB = make_ident(P, nc, st)
        lnt = alloc_ln_tmp(nc, st, P)
        pp = PsumPool(nc, st, 8, [128, 512], F32, "psA")
        pf = sb("pf", [128, PF_COLS], F32)
        pfB = Buf("pf")
        sem_c = P.dsem("semc_a")
        P.emit("sp", lambda h: h.dma_start(out=pf[:], in_=C.pf), writes=[pfB], dsem=sem_c)
        wout = sb("wout", [128, 8, D], BF16)
        woutB = Buf("wout")
        sem_wo = P.dsem("sem_wo")
        for kc in range(8):
            P.emit("pool", lambda h, kc=kc: h.dma_start(out=wout[:, kc, :], in_=C.w_out[kc * 128:(kc + 1) * 128, :]), writes=[woutB], dsem=sem_wo)
        wga = sb("wga", [128, 16, 128], BF16)
        wgx = sb("wgx", [128, 16, 128], BF16)
        wgB = Buf("wg")
        sem_wg = P.dsem("sem_wg")
        P.emit("pool", lambda h: h.dma_start(out=wga[:], in_=C.w_a.rearrange("r n c d -> c (r n) d")), writes=[wgB], dsem=sem_wg)
        P.emit("pool", lambda h: h.dma_start(out=wgx[:], in_=C.w_x.rearrange("r n c d -> c (r n) d")), writes=[wgB], dsem=sem_wg)
        gt = sb("lng", [128, 2, D], F32)
        gtB = Buf("lng")
        sem_g = P.dsem("sem_lng")
        P.emit("sp", lambda h: h.dma_start(out=gt[:, 0, :], in_=C.pt[PT["mix_g0"]:PT["mix_g0"] + 1, :].partition_broadcast(128)), writes=[gtB], dsem=sem_g)
        P.emit("sp", lambda h: h.dma_start(out=gt[:, 1, :], in_=C.pt[PT["mix_b0"]:PT["mix_b0"] + 1, :].partition_broadcast(128)), writes=[gtB], dsem=sem_g)
        cl = sb("cl", [128, 16], F32)
        clB = Buf("cl")
        lam = pf[:, PF["lam"]:PF["lam"] + 16]
        P.emit("act", lambda h: h.activation(out=cl[:], in_=lam, func=AF.Exp, scale=-1.0), reads=[pfB], writes=[clB])
        P.emit("act", lambda h: h.activation(out=cl[:], in_=cl[:], func=AF.Ln, bias=1.0, scale=1.0), reads=[clB], writes=[clB])
        P.emit("dve", lambda h: h.tensor_scalar(out=cl[:], in0=cl[:], scalar1=-8.0, scalar2=None, op0=ALU.mult), reads=[clB], writes=[clB])

        hT = sb("hT", [128, 8, L], BF16)
        hTB = Buf("hT")
        Y = sb("Y", [128, 8, L], BF16)
        YB = Buf("Y")
        xt = sb("xt", [128, D], F32)
        xtB = Buf("xt")
        sem_x = P.dsem("sem_x")
        win = [sb("win%d" % i, [128, 8, 256], BF16) for i in range(2)]
        winB = [Buf("win%d" % i) for i in range(2)]
        sem_win = [P.dsem("sem_win%d" % i) for i in range(2)]
        gg = sb("gg", [128, L], F32); ggB = Buf("gg")
        upad = sb("upad", [128, L + 3], F32); upB = Buf("upad")
        uc = sb("uc", [128, L], F32); ucB = Buf("uc")
        ucb = sb("ucb", [128, L], BF16); ucbB = Buf("ucb")
        ab = sb("ab", [128, L], F32); abB = Buf("ab")
        bb = sb("bb", [128, L], F32); bbB = Buf("bb")
        tm = sb("tm", [128, L], F32); tmB = Buf("tm")
        hf = sb("hf", [128, L], F32); hfB = Buf("hf")
        z = sb("z", [128, D], F32); zB = Buf("z")
        zo = sb("zo", [128, D], F32); zoB = Buf("zo")
        sem_o = P.dsem("sem_oa")
        P.emit("pool", lambda h: h.memset(upad[:], 0.0), writes=[upB])
        pcs = pieces(L)
        wcount = 0
        for s in range(nseq):
            def loader(xt_, xtb_, sem_, n, p0, s=s):
                load_seq_tile(P, C, "sp", xt_, xtb_, sem_, s, p0, n)
            load_tokens_T(P, C, nc, hT, hTB, xt, xtB, sem_x, idf, idB, pp, loader, [(p0, n, (p0,)) for (p0, n) in pos_tiles()])
            for c in range(8):
                wi = wcount % 2
                wcount += 1
                w = win[wi]
                P.emit("pool", lambda h, w=w, c=c: h.dma_start(out=w[:, :, 0:128], in_=C.w_in[:, c * 128:(c + 1) * 128].rearrange("(kc p) n -> p kc n", p=128)),
                       writes=[winB[wi]], dsem=sem_win[wi])
                P.emit("pool", lambda h, w=w, c=c: h.dma_start(out=w[:, :, 128:256], in_=C.w_in[:, D + c * 128:D + (c + 1) * 128].rearrange("(kc p) n -> p kc n", p=128)),
                       writes=[winB[wi]], dsem=sem_win[wi])
                for (t0, tn) in pcs:
                    pt, pb = pp.get()
                    for kc in range(8):
                        P.emit("pe", lambda h, pt=pt, kc=kc, t0=t0, tn=tn, w=w: h.matmul(pt[:, 0:tn], lhsT=w[:, kc, 0:128], rhs=hT[:, kc, t0:t0 + tn],
                                                                                      start=(kc == 0), stop=(kc == 7)), reads=[winB[wi], hTB], writes=[pb])
                    P.emit("act", lambda h, pt=pt, t0=t0, tn=tn: h.activation(out=gg[:, t0:t0 + tn], in_=pt[:, 0:tn], func=AF.Gelu_apprx_tanh),
                           reads=[pb], writes=[ggB])
                for (t0, tn) in pcs:
                    pt, pb = pp.get()
                    for kc in range(8):
                        P.emit("pe", lambda h, pt=pt, kc=kc, t0=t0, tn=tn, w=w: h.matmul(pt[:, 0:tn], lhsT=w[:, kc, 128:256], rhs=hT[:, kc, t0:t0 + tn],
                                                                                      start=(kc == 0), stop=(kc == 7)), reads=[winB[wi], hTB], writes=[pb])
                    P.emit("act", lambda h, pt=pt, t0=t0, tn=tn: h.activation(out=upad[:, 2 + t0:2 + t0 + tn], in_=pt[:, 0:tn], func=AF.Copy),
                           reads=[pb], writes=[upB])
                cw = PF["conv_w"]
                P.emit("dve", lambda h, c=c: h.tensor_scalar(out=uc[:], in0=upad[:, 0:L], scalar1=pf[:, cw + c:cw + c + 1],
                                                             scalar2=pf[:, PF["conv_b"] + c:PF["conv_b"] + c + 1], op0=ALU.mult, op1=ALU.add),
                       reads=[upB, pfB], writes=[ucB])
                for k in range(1, 4):
                    P.emit("dve", lambda h, c=c, k=k: h.scalar_tensor_tensor(out=uc[:], in0=upad[:, k:k + L], scalar=pf[:, cw + k * 8 + c:cw + k * 8 + c + 1],
                                                                            in1=uc[:], op0=ALU.mult, op1=ALU.add), reads=[upB, pfB, ucB], writes=[ucB])
                P.emit("pool", lambda h: h.tensor_copy(out=ucb[:], in_=uc[:]), reads=[ucB], writes=[ucbB])
                for r in range(2):
                    gi = r * 8 + c
                    for (t0, tn) in pcs:
                        pt, pb = pp.get()
                        P.emit("pe", lambda h, pt=pt, t0=t0, tn=tn, gi=gi: h.matmul(pt[:, 0:tn], lhsT=wga[:, gi, :], rhs=ucb[:, t0:t0 + tn], start=True, stop=True),
                               reads=[wgB, ucbB], writes=[pb])
                        P.emit("act", lambda h, pt=pt, t0=t0, tn=tn, gi=gi: h.activation(out=ab[:, t0:t0 + tn], in_=pt[:, 0:tn], func=AF.Sigmoid,
                                                                                       bias=pf[:, PF["b_a"] + gi:PF["b_a"] + gi + 1], scale=1.0), reads=[pb, pfB], writes=[abB])
                    for (t0, tn) in pcs:
                        pt, pb = pp.get()
                        P.emit("pe", lambda h, pt=pt, t0=t0, tn=tn, gi=gi: h.matmul(pt[:, 0:tn], lhsT=wgx[:, gi, :], rhs=ucb[:, t0:t0 + tn], start=True, stop=True),
                               reads=[wgB, ucbB], writes=[pb])
                        P.emit("act", lambda h, pt=pt, t0=t0, tn=tn, gi=gi: h.activation(out=bb[:, t0:t0 + tn], in_=pt[:, 0:tn], func=AF.Sigmoid,
                                                                                       bias=pf[:, PF["b_x"] + gi:PF["b_x"] + gi + 1], scale=1.0), reads=[pb, pfB], writes=[bbB])
                    P.emit("act", lambda h, gi=gi: h.activation(out=ab[:], in_=ab[:], func=AF.Exp, scale=cl[:, gi:gi + 1]), reads=[abB, clB], writes=[abB])
                    P.emit("pool", lambda h: h.tensor_tensor(out=bb[:], in0=bb[:], in1=uc[:], op=ALU.mult), reads=[bbB, ucB], writes=[bbB])
                    P.emit("act", lambda h: h.activation(out=tm[:], in_=ab[:], func=AF.Square), reads=[abB], writes=[tmB])
                    P.emit("act", lambda h: h.activation(out=tm[:], in_=tm[:], func=AF.Sqrt, bias=1.0, scale=-1.0), reads=[tmB], writes=[tmB])
                    P.emit("pool", lambda h: h.tensor_tensor(out=bb[:], in0=bb[:], in1=tm[:], op=ALU.mult), reads=[bbB, tmB], writes=[bbB])
                    if r == 0:
                        P.emit("dve", lambda h: h.tensor_tensor_scan(out=hf[:], data0=ab[:], data1=bb[:], initial=0.0, op0=ALU.mult, op1=ALU.add),
                               reads=[abB, bbB], writes=[hfB])
                    else:
                        P.emit("dve", lambda h: h.tensor_tensor_scan(out=tm[:, ::-1], data0=ab[:, ::-1], data1=bb[:, ::-1], initial=0.0, op0=ALU.mult, op1=ALU.add),
                               reads=[abB, bbB], writes=[tmB])
                P.emit("pool", lambda h: h.tensor_tensor(out=hf[:], in0=hf[:], in1=tm[:], op=ALU.add), reads=[hfB, tmB], writes=[hfB])
                P.emit("dve", lambda h, c=c: h.tensor_tensor(out=Y[:, c, :], in0=hf[:], in1=gg[:], op=ALU.mult), reads=[hfB, ggB], writes=[YB])
            for (p0, n) in pos_tiles():
                loader(xt, xtB, sem_x, n, p0)
                for half in range(2):
                    pt, pb = pp.get()
                    for kc in range(8):
                        P.emit("pe", lambda h, pt=pt, kc=kc, p0=p0, n=n, half=half: h.matmul(pt[0:n, :], lhsT=Y[:, kc, p0:p0 + n], rhs=wout[:, kc, half * 512:(half + 1) * 512],
                                                                                          start=(kc == 0), stop=(kc == 7)), reads=[YB, woutB], writes=[pb])
                    P.emit("dve", lambda h, pt=pt, n=n, half=half: h.scalar_tensor_tensor(out=z[0:n, half * 512:(half + 1) * 512], in0=xt[0:n, half * 512:(half + 1) * 512],
                                                                                         scalar=ALPHA, in1=pt[0:n, :], op0=ALU.mult, op1=ALU.add),
                           reads=[xtB, pb], writes=[zB])
                layer_norm_tile(P, C, gtB, z, zB, n, gt[:, 0, :], gt[:, 1, :], zo, zoB, lnt)
                r0 = s * L + p0
                P.emit("sp", lambda h, r0=r0, n=n: h.dma_start(out=C.H1[r0:r0 + n, :], in_=zo[0:n, :]), reads=[zoB], writes=[C.H1B], dsem=sem_o)
        P.wait_all("sp", [(sem_o, sem_o.count)])
        P.flush_block([C.H1B])


def build_program(nseq, stop_after=None):
    nc = bass.Bass("TRN2", target_bir_lowering=False)
    C = Ctx()
    T = nseq * L
    di = lambda name, shape: nc.dram_tensor(name, shape, F32, kind="ExternalInput").ap()
    C.x = di("x", [nseq, SEQ, D])
    C.meta = di("meta", [NMETA, D])
    C.w_in = di("w_in", [D, 2 * D])
    C.w_a = di("w_a", [2, 8, 128, 128])
    C.w_x = di("w_x", [2, 8, 128, 128])
    C.w_out = di("w_out", [D, D])
    C.pw1 = di("pw1", [D, 2 * D])
    C.pw2 = di("pw2", [D, D])
    C.wq = di("wq", [2, D, 2 * D])
    C.kt = di("kt", [2, 2, 128, 128])
    C.ut = di("ut", [2, D, NEXP])
    C.v = di("v", [2, NEXP, D])
    C.pf = di("pf", [128, PF_COLS])
    C.pt = di("pt", [len(PT), D])
    dbg = stop_after is not None
    mk = lambda name, shape, out: nc.dram_tensor(name, shape, F32, kind=("ExternalOutput" if out else "Internal")).ap()
    C.H1 = mk("H1", [T, D], dbg and stop_after == 1)
    C.H1B = Buf("H1")
    C.H2 = mk("H2", [T, D], dbg and stop_after == 2)
    C.H2B = Buf("H2")
    C.H3 = mk("H3", [T, D], dbg and stop_after == 3)
    C.H3B = Buf("H3")
    C.out = nc.dram_tensor("out", [nseq, SEQ, D], F32, kind="ExternalOutput").ap() if (stop_after is None or stop_after == 4) else None
    C.outB = Buf("out")
    with contextlib.ExitStack() as st:
        P = Prog(nc, st)
        phase_rglru(P, C, nc, nseq)
        if stop_after == 1:
            return nc
        phase_peer(P, C, nc, T, 0, C.H1, C.H1B, C.H2, C.H2B, False, "b_")
        if stop_after == 2:
            return nc
        phase_conf(P, C, nc, nseq, C.H2, C.H2B, C.H3, C.H3B, "c_")
        if stop_after == 3:
            return nc
        phase_peer(P, C, nc, T, 1, C.H3, C.H3B, C.out, C.outB, True, "d_")
    return nc


def pack_fm(v):
    v = np.asarray(v, np.float32).reshape(-1, 128)
    return np.ascontiguousarray(v.T)


def make_shared_inputs(inp):
    f = lambda a: np.ascontiguousarray(np.asarray(a, np.float32))
    pfm = np.zeros((128, PF_COLS), np.float32)
    cw = inp["lru_conv_w"][0]
    for k in range(4):
        pfm[:, PF["conv_w"] + k * 8:PF["conv_w"] + k * 8 + 8] = pack_fm(cw[k])
    pfm[:, PF["conv_b"]:PF["conv_b"] + 8] = pack_fm(inp["lru_conv_b"][0])
    pfm[:, PF["b_a"]:PF["b_a"] + 16] = pack_fm(inp["lru_b_a"][0].reshape(-1))
    pfm[:, PF["b_x"]:PF["b_x"] + 16] = pack_fm(inp["lru_b_x"][0].reshape(-1))
    pfm[:, PF["lam"]:PF["lam"] + 16] = pack_fm(inp["lru_lambda"][0].reshape(-1))
    pfm[:, PF["b_pw1"]:PF["b_pw1"] + 16] = pack_fm(inp["conf_b_pw1"][0])
    dw = inp["conf_dw_w"][0]
    for k in range(31):
        pfm[:, PF["dw_w"] + k * 8:PF["dw_w"] + k * 8 + 8] = pack_fm(dw[k])
    pfm[:, PF["dw_b"]:PF["dw_b"] + 8] = pack_fm(inp["conf_dw_b"][0])
    pfm[:, PF["cln_g"]:PF["cln_g"] + 8] = pack_fm(inp["conf_ln_g"][0])
    pfm[:, PF["cln_b"]:PF["cln_b"] + 8] = pack_fm(inp["conf_ln_b"][0])
    ptm = np.zeros((len(PT), D), np.float32)
    for i in range(2):
        ptm[PT["mix_g%d" % i]] = inp["ln_mix_g"][i]
        ptm[PT["mix_b%d" % i]] = inp["ln_mix_b"][i]
        ptm[PT["ffn_g%d" % i]] = inp["ln_ffn_g"][i]
        ptm[PT["ffn_b%d" % i]] = inp["ln_ffn_b"][i]
    ptm[PT["b_pw2"]] = inp["conf_b_pw2"][0]
    sh = {
        "meta": f(inp["meta_tokens"]),
        "w_in": f(inp["lru_w_in"][0]),
        "w_a": f(inp["lru_w_a"][0]),
        "w_x": f(inp["lru_w_x"][0]),
        "w_out": f(inp["lru_w_out"][0]),
        "pw1": f(inp["conf_w_pw1"][0]),
        "pw2": f(inp["conf_w_pw2"][0]),
        "wq": f(inp["peer_w_query"]),
        "kt": f(np.transpose(np.asarray(inp["peer_sub_keys"], np.float32), (0, 1, 3, 2))),
        "ut": f(np.transpose(np.asarray(inp["peer_u"], np.float32), (0, 2, 1))),
        "v": f(inp["peer_v"]),
        "pf": pfm,
        "pt": ptm,
    }
    return sh


def kernel(**inputs):
    x = np.asarray(inputs["x"], np.float32)
    nseq = x.shape[0] // NCORES
    sh = make_shared_inputs(inputs)
    nc = build_program(nseq)
    in_maps = []
    for c in range(NCORES):
        m = dict(sh)
        m["x"] = np.ascontiguousarray(x[c * nseq:(c + 1) * nseq])
        in_maps.append(m)
    res = run_bass_kernel_spmd(nc, in_maps, core_ids=list(range(NCORES)))
    return np.concatenate([r["out"] for r in res.results], axis=0)


def out_segments(r0, n):
    segs = []
    r = r0
    while r < r0 + n:
        s, pos = divmod(r, L)
        if pos < NMETA:
            r += min(NMETA - pos, r0 + n - r)
            continue
        cnt = min(L - pos, r0 + n - r)
        segs.append((r - r0, cnt, s, pos - NMETA))
        r += cnt
    return segs


def phase_peer(P, C, nc, T, layer, Hin, HinB, Hout, HoutB, final, pfx):
    GN = 4
    NG = NKEY // GN
    GE = GN * NKEY
    with contextlib.ExitStack() as st:
        sb = lambda name, shape, dt: st.enter_context(nc.sbuf_tensor(pfx + name, shape, dt))
        idf, idb, idB = make_ident_named(P, nc, st, pfx)
        lnt = alloc_ln_tmp_named(nc, st, P, pfx)
        ppA = PsumPool(nc, st, 2, [128, 512], F32, pfx + "psA")
        _ptt = st.enter_context(nc.psum_tensor(pfx + "psT", [128, 2, 4, 128], BF16))
        ppT = PsumPool.__new__(PsumPool); ppT.t = [_ptt[:, 0], _ptt[:, 1]]; _pTB = Buf("psT"); ppT.b = [_pTB, _pTB]; ppT.i = 0
        ppO = PsumPool(nc, st, 2, [128, 1024], F32, pfx + "psO")
        s1g = st.enter_context(nc.psum_tensor(pfx + "s1g", [128, 2, 8, GN], F32)); _s1gB = Buf("s1g"); s1gB = [_s1gB, _s1gB]
        wq = sb("wq", [128, 8, 2 * D], BF16); wqB = Buf("wq")
        sem_wq = P.dsem(pfx + "sem_wq")
        for kc in range(8):
            P.emit("pool", lambda h, kc=kc: h.dma_start(out=wq[:, kc, :], in_=C.wq[layer, kc * 128:(kc + 1) * 128, :], max_dma_last_dim=4096), writes=[wqB], dsem=sem_wq)
        kt = sb("kt", [128, 2, 128], BF16); ktB = Buf("kt")
        sem_kt = P.dsem(pfx + "sem_kt")
        P.emit("pool", lambda h: h.dma_start(out=kt[:], in_=C.kt[layer].rearrange("p k n -> k p n")), writes=[ktB], dsem=sem_kt)
        gt = sb("lng", [128, 2, D], F32); gtB = Buf("lng")
        sem_g = P.dsem(pfx + "sem_lng")
        gi, bi = PT["ffn_g%d" % layer], PT["ffn_b%d" % layer]
        P.emit("sp", lambda h: h.dma_start(out=gt[:, 0, :], in_=C.pt[gi:gi + 1, :].partition_broadcast(128)), writes=[gtB], dsem=sem_g)
        P.emit("sp", lambda h: h.dma_start(out=gt[:, 1, :], in_=C.pt[bi:bi + 1, :].partition_broadcast(128)), writes=[gtB], dsem=sem_g)
        xt = sb("xt", [128, D], F32); xtB = Buf("xt"); sem_x = P.dsem(pfx + "sem_x")
        hT = sb("hT", [128, 8, 512], BF16); hTB = Buf("hT")
        P.emit("pool", lambda h: h.memset(hT[:], 0.0), writes=[hTB])
        P.emit("pool", lambda h: h.memset(xt[:], 0.0), writes=[xtB])
        qTs = [sb("qT%d" % i, [128, 512], BF16) for i in range(2)]; qTB = [Buf("qT%d" % i) for i in range(2)]
        S = sb("S", [128, 4, 16, 128], F32); SB = [[Buf("S%d_%d" % (i, j)) for j in range(16)] for i in range(4)]
        TAU = sb("TAU", [128, 4, 8], F32); TAUB = [Buf("TAU%d" % i) for i in range(4)]
        O = sb("O", [128, 4, D], F32); OB = [Buf("O%d" % i) for i in range(4)]
        T16 = sb("T16", [128, 4, 16, 16], F32); T16B = [Buf("T16_%d" % i) for i in range(4)]
        tmpS = sb("tmpS", [128, 2, 128], F32); tmpSB = [Buf("tmpS0"), Buf("tmpS1")]
        cand2 = sb("cand2", [128, 256], F32); cand2B = Buf("cand2")
        c24 = sb("c24", [128, 8, 24], F32); c24B = Buf("c24")
        e16 = sb("e16", [128, 8, 16], F32); e16B = Buf("e16")
        zs = sb("zs", [128, 8], F32); zsB = Buf("zs")
        mb = sb("mb", [128, 8], F32); mbB = Buf("mb")
        UT = [sb("UT%d" % i, [128, 8, GE], BF16) for i in range(2)]; UTB = [Buf("UT%d" % i) for i in range(2)]
        VG = [sb("VG%d" % i, [128, GN, D], BF16) for i in range(2)]; VGB = [Buf("VG%d" % i) for i in range(2)]
        sem_u = [P.dsem(pfx + "sem_u%d" % i) for i in range(2)]
        sem_v = [P.dsem(pfx + "sem_v%d" % i) for i in range(2)]
        G = [sb("G%d" % i, [128, 8, GN, 128], F32) for i in range(2)]; GB = [[Buf("G%d_%d" % (i, j)) for j in range(2)] for i in range(2)]
        cand = G[0][:, 0:4].rearrange("p h a n -> p (h a n)").rearrange("p (h c) -> p h c", h=8); candB = GB[0][0]
        E = [sb("E%d" % i, [128, 8, GN, 128], BF16) for i in range(2)]
        EB = [[Buf("E%d_%d" % (i, j)) for j in range(8)] for i in range(2)]
        A = [sb("A%d" % i, [128, GE], BF16) for i in range(2)]; AB = [Buf("A%d" % i) for i in range(2)]
        WA = [sb("WA%d" % i, [128, GE], BF16) for i in range(2)]; WAB = [Buf("WA%d" % i) for i in range(2)]
        WT = [sb("WT%d" % i, [128, GN, 128], BF16) for i in range(2)]; WTB = [Buf("WT%d" % i) for i in range(2)]
        z = sb("z", [128, D], F32); zB = Buf("z")
        zo = sb("zo", [128, D], F32); zoB = Buf("zo")
        sem_o = P.dsem(pfx + "sem_o")

        tiles = [(r0, min(128, T - r0)) for r0 in range(0, T, 128)]
        cnt = 0
        for s0 in range(0, len(tiles), 4):
            tl = tiles[s0:s0 + 4]
            ntl = len(tl)
            ncol = ntl * 128

            def loader(xt_, xtb_, sem_, n, r0):
                P.emit("sp", lambda h: h.dma_start(out=xt_[0:n, :], in_=Hin[r0:r0 + n, :]), reads=[HinB], writes=[xtb_], dsem=sem_)
            for i, (r0, n) in enumerate(tl):
                loader(xt, xtB, sem_x, n, r0)
                for half in range(2):
                    pt, pb = ppA.get()
                    for j in range(4):
                        kc = half * 4 + j
                        P.emit("pe", lambda h, kc=kc, j=j, pt=pt: h.transpose(out=pt[:, j * 128:(j + 1) * 128], in_=xt[:, kc * 128:(kc + 1) * 128], identity=idf[:]),
                               reads=[xtB, idB], writes=[pb])
                    if half == 0:
                        P.emit("act", lambda h, pt=pt, i=i: h.activation(out=hT[:, 0:4, i * 128:(i + 1) * 128], in_=pt[:, :].rearrange("p (j t) -> p j t", j=4), func=AF.Copy),
                               reads=[pb], writes=[hTB])
                    else:
                        P.emit("dve", lambda h, pt=pt, i=i: h.tensor_copy(out=hT[:, 4:8, i * 128:(i + 1) * 128], in_=pt[:, :].rearrange("p (j t) -> p j t", j=4)),
                               reads=[pb], writes=[hTB])
            for j in range(16):
                pt, pb = ppA.get()
                for kc in range(8):
                    P.emit("pe", lambda h, pt=pt, kc=kc, j=j: h.matmul(pt[:, 0:ncol], lhsT=wq[:, kc, j * 128:(j + 1) * 128], rhs=hT[:, kc, 0:ncol],
                                                                     start=(kc == 0), stop=(kc == 7)), reads=[wqB, hTB], writes=[pb])
                q = qTs[j % 2]; qb = qTB[j % 2]
                if j % 2 == 0:
                    P.emit("act", lambda h, pt=pt, q=q: h.activation(out=q[:, 0:ncol], in_=pt[:, 0:ncol], func=AF.Copy), reads=[pb], writes=[qb])
                else:
                    P.emit("dve", lambda h, pt=pt, q=q: h.tensor_copy(out=q[:, 0:ncol], in_=pt[:, 0:ncol]), reads=[pb], writes=[qb])
                po, pob = ppO.get()
                for i in range(ntl):
                    P.emit("pe", lambda h, po=po, i=i, q=q, j=j: h.matmul(po[:, i * 128:(i + 1) * 128], lhsT=q[:, i * 128:(i + 1) * 128], rhs=kt[:, j % 2, :], start=True, stop=True),
                           reads=[qb, ktB], writes=[pob])
                eng = "act" if j % 2 == 1 else "dve"
                if eng == "act":
                    P.emit("act", lambda h, po=po, j=j: h.activation(out=S[:, 0:ntl, j, :], in_=po[:, 0:ncol].rearrange("p (t n) -> p t n", n=128), func=AF.Copy),
                           reads=[pob], writes=[SB[i][j] for i in range(ntl)])
                else:
                    P.emit("dve", lambda h, po=po, j=j: h.tensor_copy(out=S[:, 0:ntl, j, :], in_=po[:, 0:ncol].rearrange("p (t n) -> p t n", n=128)),
                           reads=[pob], writes=[SB[i][j] for i in range(ntl)])
                for i in range(ntl):
                    tb = (j * 4 + i) % 2
                    P.emit("dve", lambda h: h.max(out=T16[:, i, j, 0:8], in_=S[:, i, j, :]), reads=[SB[i][j]], writes=[T16B[i]])
                    P.emit("dve", lambda h: h.match_replace(out=tmpS[:, tb], in_to_replace=T16[:, i, j, 0:8], in_values=S[:, i, j, :], imm_value=-1e30),
                           reads=[SB[i][j], T16B[i]], writes=[tmpSB[tb]])
                    P.emit("dve", lambda h: h.max(out=T16[:, i, j, 8:16], in_=tmpS[:, tb]), reads=[tmpSB[tb]], writes=[T16B[i]])
            for i in range(ntl):
                P.emit("dve", lambda h, i=i: h.tensor_tensor(out=cand[:].rearrange("p h (a b) -> p h a b", a=16),
                                                        in0=T16[:, i, 0::2, :].unsqueeze(3).to_broadcast([128, 8, 16, 16]),
                                                        in1=T16[:, i, 1::2, :].unsqueeze(2).to_broadcast([128, 8, 16, 16]), op=ALU.add), reads=[T16B[i]], writes=[candB])
                for hh in range(8):
                    P.emit("dve", lambda h, hh=hh: h.max(out=c24[:, hh, 0:8], in_=cand[:, hh, :]), reads=[candB], writes=[c24B])
                    P.emit("dve", lambda h, hh=hh: h.match_replace(out=cand2[:], in_to_replace=c24[:, hh, 0:8], in_values=cand[:, hh, :], imm_value=-1e30),
                           reads=[candB, c24B], writes=[cand2B])
                    P.emit("dve", lambda h, hh=hh: h.max(out=c24[:, hh, 8:16], in_=cand2[:]), reads=[cand2B], writes=[c24B])
                    P.emit("dve", lambda h, hh=hh: h.match_replace(out=cand2[:], in_to_replace=c24[:, hh, 8:16], in_values=cand2[:], imm_value=-1e30),
                           reads=[cand2B, c24B], writes=[cand2B])
                    P.emit("dve", lambda h, hh=hh: h.max(out=c24[:, hh, 16:24], in_=cand2[:]), reads=[cand2B], writes=[c24B])
                P.emit("dve", lambda h: h.tensor_tensor(out=e16[:], in0=c24[:, :, 0:16], in1=c24[:, :, 0:1].to_broadcast([128, 8, 16]), op=ALU.subtract),
                       reads=[c24B], writes=[e16B])
                P.emit("act", lambda h: h.activation(out=e16[:], in_=e16[:], func=AF.Exp), reads=[e16B], writes=[e16B])
                P.emit("dve", lambda h: h.tensor_reduce(out=zs[:], in_=e16[:], axis=AX.X, op=ALU.add), reads=[e16B], writes=[zsB])
                P.emit("act", lambda h: h.activation(out=zs[:], in_=zs[:], func=AF.Ln), reads=[zsB], writes=[zsB])
                P.emit("dve", lambda h: h.tensor_tensor(out=mb[:], in0=zs[:], in1=c24[:, :, 0], op=ALU.add), reads=[zsB, c24B], writes=[mbB])
                P.emit("dve", lambda h: h.tensor_tensor(out=zs[:], in0=c24[:, :, 15], in1=c24[:, :, 16], op=ALU.add), reads=[c24B, zsB], writes=[zsB])
                P.emit("dve", lambda h, i=i: h.scalar_tensor_tensor(out=TAU[:, i, :], in0=zs[:], scalar=0.5, in1=mb[:], op0=ALU.mult, op1=ALU.subtract),
                       reads=[zsB, mbB], writes=[TAUB[i]])
                P.emit("dve", lambda h, i=i: h.tensor_tensor(out=mb[:], in0=mb[:], in1=TAU[:, i, :], op=ALU.add), reads=[mbB, TAUB[i]], writes=[mbB])
                P.emit("dve", lambda h, i=i: h.tensor_tensor(out=S[:, i, 0::2, :], in0=S[:, i, 0::2, :], in1=mb[:].unsqueeze(2).to_broadcast([128, 8, 128]), op=ALU.subtract),
                       reads=SB[i][0::2] + [mbB], writes=SB[i][0::2])

            def load_group(g):
                b = g % 2
                P.emit("pool", lambda h: h.dma_start(out=UT[b][:], in_=C.ut[layer, :, g * GE:(g + 1) * GE].rearrange("(kc p) e -> p kc e", p=128)),
                       writes=[UTB[b]], dsem=sem_u[b])
                P.emit("pool", lambda h: h.dma_start(out=VG[b][:], in_=C.v[layer, g * GE:(g + 1) * GE, :].rearrange("(ec p) d -> p ec d", p=128)),
                       writes=[VGB[b]], dsem=sem_v[b])

            items = [(g, i) for g in range(NG) for i in range(ntl)]
            po_of = {}
            ptt_of = {}

            def st_tr(k):
                eb = k % 2
                ptt, pttb = ppT.get()
                for ec in range(GN):
                    P.emit("pe", lambda h: h.transpose(out=ptt[:, ec, :], in_=WA[eb][:, ec * 128:(ec + 1) * 128], identity=idb[:]),
                           reads=[WAB[eb], idB], writes=[pttb])
                P.emit("act", lambda h: h.activation(out=WT[eb][:], in_=ptt[:], func=AF.Copy), reads=[pttb], writes=[WTB[eb]])

            def st_s1g(k):
                g, i = items[k]
                eb = k % 2
                P.emit("act", lambda h: h.activation(out=s1g[:, eb], in_=S[:, i, 0::2, g * GN:(g + 1) * GN], func=AF.Copy), reads=SB[i][0::2], writes=[s1gB[eb]])

            def st_front(k):
                g, i = items[k]
                b = g % 2
                eb = k % 2
                Ei, EiB = E[eb], EB[eb]
                Gk, GkB = G[eb], GB[eb]
                pa, pab = ppA.get()
                P.emit("dve", lambda h: h.tensor_tensor(out=Gk[:], in0=S[:, i, 1::2, :].unsqueeze(2).to_broadcast([128, 8, GN, 128]),
                                                        in1=s1g[:, eb].unsqueeze(3).to_broadcast([128, 8, GN, 128]), op=ALU.add),
                       reads=SB[i][1::2] + [s1gB[eb]], writes=[GkB[0]])
                if k + 1 < nit:
                    st_s1g(k + 1)
                for hh in range(8):
                    P.emit("act", lambda h: h.activation(out=Ei[:, hh], in_=Gk[:, hh], func=AF.Exp, bias=TAU[:, i, hh:hh + 1], scale=1.0),
                           reads=[GkB[0], TAUB[i]], writes=[EiB[hh]])
                for kc in range(8):
                    P.emit("pe", lambda h: h.matmul(pa[:, :], lhsT=hT[:, kc, i * 128:(i + 1) * 128], rhs=UT[b][:, kc, :], start=(kc == 0), stop=(kc == 7)),
                           reads=[hTB, UTB[b]], writes=[pab])
                P.emit("act", lambda h: h.activation(out=A[eb][:], in_=pa[:, :], func=AF.Gelu_apprx_tanh), reads=[pab], writes=[AB[eb]])

            def st_mid(k):
                g, i = items[k]
                eb = k % 2
                Ei, EiB = E[eb], EB[eb]
                Gk, GkB = G[eb], GB[eb]
                P.emit("dve", lambda h: h.scalar_tensor_tensor(out=Ei[:].rearrange("p h a n -> p (h a n)"), in0=Gk[:].rearrange("p h a n -> p (h a n)"), scalar=0.0,
                                                               in1=Ei[:].rearrange("p h a n -> p (h a n)"), op0=ALU.is_ge, op1=ALU.mult),
                       reads=[GkB[0]] + EiB[0:8], writes=EiB[0:8])
                P.emit("dve", lambda h: h.tensor_tensor(out=Ei[:, 0:4], in0=Ei[:, 0:4], in1=Ei[:, 4:8], op=ALU.add), reads=EiB[0:8], writes=EiB[0:4])
                P.emit("dve", lambda h: h.tensor_tensor(out=Ei[:, 0:2], in0=Ei[:, 0:2], in1=Ei[:, 2:4], op=ALU.add), reads=EiB[0:4], writes=EiB[0:2])
                P.emit("dve", lambda h: h.tensor_tensor(out=Ei[:, 0], in0=Ei[:, 0], in1=Ei[:, 1], op=ALU.add), reads=EiB[0:2], writes=EiB[0:1])
                P.emit("dve", lambda h: h.tensor_tensor(out=WA[eb][:], in0=A[eb][:], in1=Ei[:, 0].rearrange("p a n -> p (a n)"), op=ALU.mult),
                       reads=[AB[eb], EiB[0]], writes=[WAB[eb]])

            def st_vmm(k):
                g, i = items[k]
                b = g % 2
                eb = k % 2
                po, pob = ppO.get()
                for dh in range(2):
                    if g > 0:
                        P.emit("pe", lambda h: h.matmul(po[:, dh * 512:(dh + 1) * 512], lhsT=idf[:], rhs=O[:, i, dh * 512:(dh + 1) * 512], start=True, stop=False),
                               reads=[idB, OB[i]], writes=[pob])
                    for ec in range(GN):
                        P.emit("pe", lambda h: h.matmul(po[:, dh * 512:(dh + 1) * 512], lhsT=WT[eb][:, ec, :], rhs=VG[b][:, ec, dh * 512:(dh + 1) * 512],
                                                        start=(ec == 0 and g == 0), stop=(ec == GN - 1)),
                               reads=[WTB[eb], VGB[b]], writes=[pob])
                po_of[k] = (po, pob)

            def st_acc(k):
                g, i = items[k]
                po, pob = po_of.pop(k)
                P.emit("act", lambda h: h.activation(out=O[:, i, :], in_=po[:, :], func=AF.Copy), reads=[pob], writes=[OB[i]])

            load_group(0)
            if NG > 1:
                load_group(1)
            nit = len(items)
            st_s1g(0)
            for k in range(nit + 3):
                if k < nit:
                    st_front(k)
                if 0 <= k - 2 < nit:
                    st_acc(k - 2)
                if 0 <= k - 1 < nit:
                    st_mid(k - 1)
                    st_tr(k - 1)
                    st_vmm(k - 1)
                    gk, ik = items[k - 1]
                    if ik == ntl - 1 and gk + 2 < NG:
                        load_group(gk + 2)
            for i, (r0, n) in enumerate(tl):
                loader(xt, xtB, sem_x, n, r0)
                P.emit("dve", lambda h, i=i: h.scalar_tensor_tensor(out=z[:], in0=xt[:], scalar=ALPHA, in1=O[:, i, :], op0=ALU.mult, op1=ALU.add),
                       reads=[xtB, OB[i]], writes=[zB])
                layer_norm_tile(P, C, gtB, z, zB, 128, gt[:, 0, :], gt[:, 1, :], zo, zoB, lnt)
                if not final:
                    P.emit("sp", lambda h, r0=r0, n=n: h.dma_start(out=Hout[r0:r0 + n, :], in_=zo[0:n, :]), reads=[zoB], writes=[HoutB], dsem=sem_o)
                else:
                    for (ro, c, sq, ps) in out_segments(r0, n):
                        P.emit("sp", lambda h, ro=ro, c=c, sq=sq, ps=ps: h.dma_start(out=Hout[sq, ps:ps + c, :], in_=zo[ro:ro + c, :]), reads=[zoB], writes=[HoutB], dsem=sem_o)
        P.wait_all("sp", [(sem_o, sem_o.count)])
        P.flush_block([HinB, HoutB])


def make_ident_named(P, nc, st, pfx):
    idf = st.enter_context(nc.sbuf_tensor(pfx + "identf", [128, 128], F32))
    idb = st.enter_context(nc.sbuf_tensor(pfx + "identb", [128, 128], BF16))
    B = Buf("ident")
    P.emit("pool", lambda h: h.memset(idf[:], 1.0), writes=[B])
    P.emit("pool", lambda h: h.affine_select(out=idf[:], in_=idf[:], pattern=[[-1, 128]], base=0, channel_multiplier=1,
                                              compare_op=ALU.is_equal, fill=0.0), reads=[B], writes=[B])
    P.emit("pool", lambda h: h.tensor_copy(out=idb[:], in_=idf[:]), reads=[B], writes=[B])
    return idf, idb, B


def alloc_ln_tmp_named(nc, st, P, pfx):
    t = {}
    t["stats"] = st.enter_context(nc.sbuf_tensor(pfx + "ln_stats", [128, 2, 6], F32))
    t["statsb"] = Buf("ln_stats")
    t["mv"] = st.enter_context(nc.sbuf_tensor(pfx + "ln_mv", [128, 2], F32))
    t["mvb"] = Buf("ln_mv")
    t["rs"] = st.enter_context(nc.sbuf_tensor(pfx + "ln_rs", [128, 1], F32))
    t["rsb"] = Buf("ln_rs")
    t["eps"] = st.enter_context(nc.sbuf_tensor(pfx + "ln_eps", [128, 1], F32))
    P.emit("pool", lambda h: h.memset(t["eps"][:], EPS), writes=[Buf()])
    return t


def phase_conf(P, C, nc, nseq, Hin, HinB, Hout, HoutB, pfx):
    KW = 31
    PADW = KW // 2
    with contextlib.ExitStack() as st:
        sb = lambda name, shape, dt: st.enter_context(nc.sbuf_tensor(pfx + name, shape, dt))
        idf, idb, idB = make_ident_named(P, nc, st, pfx)
        lnt = alloc_ln_tmp_named(nc, st, P, pfx)
        pp = PsumPool(nc, st, 8, [128, 512], F32, pfx + "ps")
        pf = sb("pf", [128, PF_COLS], F32); pfB = Buf("pf")
        sem_c = P.dsem(pfx + "semc")
        P.emit("sp", lambda h: h.dma_start(out=pf[:], in_=C.pf), writes=[pfB], dsem=sem_c)
        ones = sb("ones", [128, 128], F32); onesB = Buf("ones")
        P.emit("pool", lambda h: h.memset(ones[:], 1.0 / D), writes=[onesB])
        pw2 = sb("pw2", [128, 8, D], BF16); pw2B = Buf("pw2")
        sem_p2 = P.dsem(pfx + "sem_p2")
        for kc in range(8):
            P.emit("pool", lambda h, kc=kc: h.dma_start(out=pw2[:, kc, :], in_=C.pw2[kc * 128:(kc + 1) * 128, :]), writes=[pw2B], dsem=sem_p2)
        gt = sb("lng", [128, 3, D], F32); gtB = Buf("lng")
        sem_g = P.dsem(pfx + "sem_lng")
        for k, nm in enumerate(("mix_g1", "mix_b1", "b_pw2")):
            P.emit("sp", lambda h, k=k, nm=nm: h.dma_start(out=gt[:, k, :], in_=C.pt[PT[nm]:PT[nm] + 1, :].partition_broadcast(128)), writes=[gtB], dsem=sem_g)
        hT = sb("hT", [128, 8, L], BF16); hTB = Buf("hT")
        CV = sb("CV", [128, 8, L], F32); CVB = [Buf("CV%d" % i) for i in range(8)]
        xt = sb("xt", [128, D], F32); xtB = Buf("xt"); sem_x = P.dsem(pfx + "sem_x")
        win = [sb("win%d" % i, [128, 8, 256], BF16) for i in range(2)]
        winB = [Buf("win%d" % i) for i in range(2)]
        sem_win = [P.dsem(pfx + "sem_win%d" % i) for i in range(2)]
        sig = sb("sig", [128, L], F32); sigB = Buf("sig")
        gpad = sb("gpad", [128, L + 2 * PADW], F32); gpB = Buf("gpad")
        P.emit("pool", lambda h: h.memset(gpad[:], 0.0), writes=[gpB])
        sq = [sb("sq%d" % i, [128, 512], F32) for i in range(2)]; sqB = [Buf("sq%d" % i) for i in range(2)]
        mean = sb("mean", [128, 512], F32); meanB = Buf("mean")
        rstd = sb("rstd", [128, 512], F32); rstdB = Buf("rstd")
        tq = [sb("tq%d" % i, [128, 512], F32) for i in range(2)]; tqB = [Buf("tq%d" % i) for i in range(2)]
        z = sb("z", [128, D], F32); zB = Buf("z")
        zo = sb("zo", [128, D], F32); zoB = Buf("zo")
        sem_o = P.dsem(pfx + "sem_o")
        pcs = pieces(L)
        wcount = 0
        sqc = 0
        for s in range(nseq):
            def loader(xt_, xtb_, sem_, n, p0, s=s):
                r0 = s * L + p0
                P.emit("sp", lambda h: h.dma_start(out=xt_[0:n, :], in_=Hin[r0:r0 + n, :]), reads=[HinB], writes=[xtb_], dsem=sem_)
            load_tokens_T(P, C, nc, hT, hTB, xt, xtB, sem_x, idf, idB, pp, loader, [(p0, n, (p0,)) for (p0, n) in pos_tiles()])
            for c in range(8):
                wi = wcount % 2
                wcount += 1
                w = win[wi]
                P.emit("pool", lambda h: h.dma_start(out=w[:, :, 0:128], in_=C.pw1[:, c * 128:(c + 1) * 128].rearrange("(kc p) n -> p kc n", p=128)),
                       writes=[winB[wi]], dsem=sem_win[wi])
                P.emit("pool", lambda h: h.dma_start(out=w[:, :, 128:256], in_=C.pw1[:, D + c * 128:D + (c + 1) * 128].rearrange("(kc p) n -> p kc n", p=128)),
                       writes=[winB[wi]], dsem=sem_win[wi])
                for (t0, tn) in pcs:
                    pa, pab = pp.get()
                    pg, pgb = pp.get()
                    for kc in range(8):
                        P.emit("pe", lambda h: h.matmul(pg[:, 0:tn], lhsT=w[:, kc, 128:256], rhs=hT[:, kc, t0:t0 + tn], start=(kc == 0), stop=(kc == 7)),
                               reads=[winB[wi], hTB], writes=[pgb])
                    for kc in range(8):
                        P.emit("pe", lambda h: h.matmul(pa[:, 0:tn], lhsT=w[:, kc, 0:128], rhs=hT[:, kc, t0:t0 + tn], start=(kc == 0), stop=(kc == 7)),
                               reads=[winB[wi], hTB], writes=[pab])
                    bg = PF["b_pw1"] + 8 + c
                    ba = PF["b_pw1"] + c
                    P.emit("act", lambda h: h.activation(out=sig[:, t0:t0 + tn], in_=pg[:, 0:tn], func=AF.Sigmoid, bias=pf[:, bg:bg + 1], scale=1.0),
                           reads=[pgb, pfB], writes=[sigB])
                    P.emit("dve", lambda h: h.scalar_tensor_tensor(out=gpad[:, PADW + t0:PADW + t0 + tn], in0=pa[:, 0:tn], scalar=pf[:, ba:ba + 1], in1=sig[:, t0:t0 + tn],
                                                                   op0=ALU.add, op1=ALU.mult), reads=[pab, pfB, sigB], writes=[gpB])
                dw = PF["dw_w"]
                db = PF["dw_b"] + c
                P.emit("dve", lambda h: h.tensor_scalar(out=CV[:, c, :], in0=gpad[:, 0:L], scalar1=pf[:, dw + c:dw + c + 1], scalar2=pf[:, db:db + 1],
                                                        op0=ALU.mult, op1=ALU.add), reads=[gpB, pfB], writes=[CVB[c]])
                for k in range(1, KW):
                    P.emit("dve", lambda h: h.scalar_tensor_tensor(out=CV[:, c, :], in0=gpad[:, k:k + L], scalar=pf[:, dw + k * 8 + c:dw + k * 8 + c + 1],
                                                                   in1=CV[:, c, :], op0=ALU.mult, op1=ALU.add), reads=[gpB, pfB, CVB[c]], writes=[CVB[c]])
            Yc, YcB = hT, hTB
            for (t0, tn) in pcs:
                pm, pmb = pp.get()
                pq, pqb = pp.get()
                for c in range(8):
                    P.emit("pe", lambda h: h.matmul(pm[:, 0:tn], lhsT=ones[:], rhs=CV[:, c, t0:t0 + tn], start=(c == 0), stop=(c == 7)),
                           reads=[onesB, CVB[c]], writes=[pmb])
                for c in range(8):
                    si = sqc % 2
                    sqc += 1
                    P.emit("act", lambda h: h.activation(out=sq[si][:, 0:tn], in_=CV[:, c, t0:t0 + tn], func=AF.Square), reads=[CVB[c]], writes=[sqB[si]])
                    P.emit("pe", lambda h: h.matmul(pq[:, 0:tn], lhsT=ones[:], rhs=sq[si][:, 0:tn], start=(c == 0), stop=(c == 7)),
                           reads=[onesB, sqB[si]], writes=[pqb])
                P.emit("act", lambda h: h.activation(out=mean[:, 0:tn], in_=pm[:, 0:tn], func=AF.Copy), reads=[pmb], writes=[meanB])
                P.emit("dve", lambda h: h.tensor_tensor(out=rstd[:, 0:tn], in0=mean[:, 0:tn], in1=mean[:, 0:tn], op=ALU.mult), reads=[meanB], writes=[rstdB])
                P.emit("dve", lambda h: h.tensor_tensor(out=rstd[:, 0:tn], in0=pq[:, 0:tn], in1=rstd[:, 0:tn], op=ALU.subtract), reads=[pqb, rstdB], writes=[rstdB])
                P.emit("act", lambda h: h.activation(out=rstd[:, 0:tn], in_=rstd[:, 0:tn], func=AF.Sqrt, bias=lnt["eps"][:, :], scale=1.0), reads=[rstdB], writes=[rstdB])
                P.emit("dve", lambda h: h.reciprocal(out=rstd[:, 0:tn], in_=rstd[:, 0:tn]), reads=[rstdB], writes=[rstdB])
                for c in range(8):
                    ti = c % 2
                    P.emit("dve", lambda h: h.tensor_tensor(out=tq[ti][:, 0:tn], in0=CV[:, c, t0:t0 + tn], in1=mean[:, 0:tn], op=ALU.subtract),
                           reads=[CVB[c], meanB], writes=[tqB[ti]])
                    P.emit("pool", lambda h: h.tensor_tensor(out=tq[ti][:, 0:tn], in0=tq[ti][:, 0:tn], in1=rstd[:, 0:tn], op=ALU.mult),
                           reads=[tqB[ti], rstdB], writes=[tqB[ti]])
                    gcol = PF["cln_g"] + c
                    bcol = PF["cln_b"] + c
                    P.emit("act", lambda h: h.activation(out=Yc[:, c, t0:t0 + tn], in_=tq[ti][:, 0:tn], func=AF.Silu, bias=pf[:, bcol:bcol + 1], scale=pf[:, gcol:gcol + 1]),
                           reads=[tqB[ti], pfB], writes=[YcB])
            for (p0, n) in pos_tiles():
                loader(xt, xtB, sem_x, n, p0)
                for half in range(2):
                    pt, pb = pp.get()
                    for kc in range(8):
                        P.emit("pe", lambda h: h.matmul(pt[0:n, :], lhsT=Yc[:, kc, p0:p0 + n], rhs=pw2[:, kc, half * 512:(half + 1) * 512], start=(kc == 0), stop=(kc == 7)),
                               reads=[YcB, pw2B], writes=[pb])
                    P.emit("dve", lambda h: h.scalar_tensor_tensor(out=z[0:n, half * 512:(half + 1) * 512], in0=xt[0:n, half * 512:(half + 1) * 512],
                                                                   scalar=ALPHA, in1=pt[0:n, :], op0=ALU.mult, op1=ALU.add), reads=[xtB, pb], writes=[zB])
                P.emit("pool", lambda h: h.tensor_tensor(out=z[0:n, :], in0=z[0:n, :], in1=gt[0:n, 2, :], op=ALU.add), reads=[zB, gtB], writes=[zB])
                layer_norm_tile(P, C, gtB, z, zB, n, gt[:, 0, :], gt[:, 1, :], zo, zoB, lnt)
                r0 = s * L + p0
                P.emit("sp", lambda h: h.dma_start(out=Hout[r0:r0 + n, :], in_=zo[0:n, :]), reads=[zoB], writes=[HoutB], dsem=sem_o)
        P.wait_all("sp", [(sem_o, sem_o.count)])
        P.flush_block([HinB, HoutB])
```

```python
import contextlib
import types
import numpy as np
import concourse.bass as bass
import concourse.mybir as mybir
from concourse.bass_utils import run_bass_kernel_spmd

F32 = mybir.dt.float32
BF16 = mybir.dt.bfloat16
ALU = mybir.AluOpType
AF = mybir.ActivationFunctionType
AX = mybir.AxisListType

D = 1024
SEQ = 2048
NMETA = 16
L = SEQ + NMETA
NCORES = 8
ALPHA = float(4.0 ** 0.25)
EPS = 1e-5
NKEY = 128
NEXP = NKEY * NKEY
GELU_K = 1.5957691216057308

PF = {}
_c = 0
for _n, _w in (("conv_w", 32), ("conv_b", 8), ("b_a", 16), ("b_x", 16), ("lam", 16), ("b_pw1", 16),
               ("dw_w", 248), ("dw_b", 8), ("cln_g", 8), ("cln_b", 8)):
    PF[_n] = _c
    _c += _w
PF_COLS = _c
PT = {"mix_g0": 0, "mix_b0": 1, "ffn_g0": 2, "ffn_b0": 3, "mix_g1": 4, "mix_b1": 5, "ffn_g1": 6, "ffn_b1": 7, "b_pw2": 8}


def freeze(fn):
    if fn.__closure__ is None:
        return fn
    cells = []
    for c in fn.__closure__:
        try:
            cells.append(types.CellType(c.cell_contents))
        except ValueError:
            cells.append(c)
    g = types.FunctionType(fn.__code__, fn.__globals__, fn.__name__, fn.__defaults__, tuple(cells))
    g.__kwdefaults__ = fn.__kwdefaults__
    return g


class Buf:
    __slots__ = ("name", "w", "r")

    def __init__(self, name=""):
        self.name = name
        self.w = None
        self.r = []


class DSem:
    __slots__ = ("h", "count")

    def __init__(self, h):
        self.h = h
        self.count = 0


class Eng:
    def __init__(self, name, sem):
        self.name = name
        self.sem = sem
        self.ops = []
        self.seen = {}


class Prog:
    def __init__(self, nc, stack):
        self.nc = nc
        self.stack = stack
        self.engs = {}
        self.nblk = 0
        for n in ("pe", "act", "dve", "pool", "sp"):
            self.engs[n] = Eng(n, None)
        self._new_sems()
        self.nops = 0

    def _new_sems(self):
        for n, e in self.engs.items():
            e.sem = DSem(self.stack.enter_context(self.nc.semaphore("s_%s_%d" % (n, self.nblk))))
            e.seen = {}

    def dsem(self, name):
        return DSem(self.stack.enter_context(self.nc.semaphore(name)))

    def emit(self, eng, fn, reads=(), writes=(), dsem=None):
        e = self.engs[eng]
        deps = {}

        def dep(sig):
            s, v = sig
            if deps.get(s, 0) < v:
                deps[s] = v

        for b in reads:
            if b.w is not None:
                dep(b.w)
        for b in writes:
            if b.w is not None:
                dep(b.w)
            for r in b.r:
                dep(r)
        if dsem is not None and dsem.count > 0:
            dep((dsem, dsem.count))
        waits = []
        for s, v in deps.items():
            if eng == "pe" and s is e.sem:
                continue
            if e.seen.get(s, 0) < v:
                e.seen[s] = v
                waits.append((s.h, v))
        if dsem is not None:
            dsem.count += 16
            sig = (dsem, dsem.count)
            inc = 16
        else:
            e.sem.count += 1
            sig = (e.sem, e.sem.count)
            inc = 1
        for b in reads:
            b.r.append(sig)
        for b in writes:
            b.w = sig
            b.r = []
        e.ops.append((waits, freeze(fn), sig[0].h, inc))
        self.nops += 1
        return sig

    def wait_all(self, eng, sigs):
        e = self.engs[eng]
        for s, v in sigs:
            if e.seen.get(s, 0) < v:
                e.seen[s] = v
                e.ops.append(([(s.h, v)], None, None, 0))

    def flush_block(self, bufs=()):
        nc = self.nc
        engs = self.engs

        def replay(e, h):
            for waits, fn, sh, inc in e.ops:
                for s, v in waits:
                    h.wait_ge(s, v)
                if fn is not None:
                    fn(h).then_inc(sh, inc)
            e.ops = []

        with nc.Block() as block:
            @block.tensor
            def _(h):
                replay(engs["pe"], h)

            @block.scalar
            def _(h):
                replay(engs["act"], h)

            @block.vector
            def _(h):
                replay(engs["dve"], h)

            @block.gpsimd
            def _(h):
                replay(engs["pool"], h)

            @block.sync
            def _(h):
                replay(engs["sp"], h)
        self.nblk += 1
        self._new_sems()
        for b in bufs:
            b.w = None
            b.r = []


class Ctx:
    pass


def pieces(n, step=512):
    return [(s, min(step, n - s)) for s in range(0, n, step)]


def pos_tiles():
    return [(s, min(128, L - s)) for s in range(0, L, 128)]


class PsumPool:
    def __init__(self, nc, st, n, shape, dtype, name):
        self.t = [st.enter_context(nc.psum_tensor("%s%d" % (name, i), shape, dtype)) for i in range(n)]
        self.b = [Buf("%s%d" % (name, i)) for i in range(n)]
        self.i = 0

    def get(self):
        i = self.i
        self.i = (i + 1) % len(self.t)
        return self.t[i], self.b[i]


def load_seq_tile(P, C, eng, dst, dbuf, dsem, s, p0, n):
    sigs = []
    if p0 < NMETA:
        m = min(NMETA - p0, n)
        P.emit(eng, lambda h: h.dma_start(out=dst[0:m, :], in_=C.meta[p0:p0 + m, :]), writes=[dbuf], dsem=dsem)
        if n > m:
            P.emit(eng, lambda h: h.dma_start(out=dst[m:n, :], in_=C.x[s, 0:n - m, :]), writes=[dbuf], dsem=dsem)
    else:
        P.emit(eng, lambda h: h.dma_start(out=dst[0:n, :], in_=C.x[s, p0 - NMETA:p0 - NMETA + n, :]), writes=[dbuf], dsem=dsem)


def layer_norm_tile(P, C, gbB, z, zb, n, g_ap, b_ap, out, outb, tmp):
    stats, sb = tmp["stats"], tmp["statsb"]
    for k in range(2):
        P.emit("dve", lambda h, k=k: h.bn_stats(out=stats[0:n, k, :], in_=z[0:n, k * 512:(k + 1) * 512]), reads=[zb], writes=[sb])
    mv, mvb = tmp["mv"], tmp["mvb"]
    P.emit("dve", lambda h: h.bn_aggr(out=mv[0:n, :], in_=stats[0:n, :, :]), reads=[sb], writes=[mvb])
    rs, rsb = tmp["rs"], tmp["rsb"]
    P.emit("act", lambda h: h.activation(out=rs[0:n, :], in_=mv[0:n, 1:2], func=AF.Sqrt, bias=tmp["eps"][0:n, :], scale=1.0), reads=[mvb], writes=[rsb])
    P.emit("dve", lambda h: h.reciprocal(out=rs[0:n, :], in_=rs[0:n, :]), reads=[rsb], writes=[rsb])
    P.emit("dve", lambda h: h.tensor_scalar(out=out[0:n, :], in0=z[0:n, :], scalar1=mv[0:n, 0:1], scalar2=rs[0:n, 0:1],
                                            op0=ALU.subtract, op1=ALU.mult), reads=[zb, mvb, rsb], writes=[outb])
    P.emit("pool", lambda h: h.tensor_tensor(out=out[0:n, :], in0=out[0:n, :], in1=g_ap[0:n, :], op=ALU.mult), reads=[outb, gbB], writes=[outb])
    P.emit("pool", lambda h: h.tensor_tensor(out=out[0:n, :], in0=out[0:n, :], in1=b_ap[0:n, :], op=ALU.add), reads=[outb, gbB], writes=[outb])


def alloc_ln_tmp(nc, st, P):
    t = {}
    t["stats"] = st.enter_context(nc.sbuf_tensor("ln_stats", [128, 2, 6], F32))
    t["statsb"] = Buf("ln_stats")
    t["mv"] = st.enter_context(nc.sbuf_tensor("ln_mv", [128, 2], F32))
    t["mvb"] = Buf("ln_mv")
    t["rs"] = st.enter_context(nc.sbuf_tensor("ln_rs", [128, 1], F32))
    t["rsb"] = Buf("ln_rs")
    t["eps"] = st.enter_context(nc.sbuf_tensor("ln_eps", [128, 1], F32))
    P.emit("pool", lambda h: h.memset(t["eps"][:], EPS), writes=[Buf()])
    return t


def make_ident(P, nc, st):
    idf = st.enter_context(nc.sbuf_tensor("identf", [128, 128], F32))
    idb = st.enter_context(nc.sbuf_tensor("identb", [128, 128], BF16))
    B = Buf("ident")
    P.emit("pool", lambda h: h.memset(idf[:], 1.0), writes=[B])
    P.emit("pool", lambda h: h.affine_select(out=idf[:], in_=idf[:], pattern=[[-1, 128]], base=0, channel_multiplier=1,
                                              compare_op=ALU.is_equal, fill=0.0), reads=[B], writes=[B])
    P.emit("pool", lambda h: h.tensor_copy(out=idb[:], in_=idf[:]), reads=[B], writes=[B])
    return idf, idb, B


def load_tokens_T(P, C, nc, hT, hTb, xt, xtb, xsem, idf, idB, pp, loader, tiles):
    for (c0, n, args) in tiles:
        loader(xt, xtb, xsem, n, *args)
        for half in range(2):
            pt, pb = pp.get()
            for j in range(4):
                kc = half * 4 + j
                P.emit("pe", lambda h, kc=kc, j=j, pt=pt, n=n: h.transpose(out=pt[:, j * 128:j * 128 + n], in_=xt[0:n, kc * 128:(kc + 1) * 128],
                                                                         identity=idf[0:n, 0:n]), reads=[xtb, idB], writes=[pb])
            e = "act" if half == 0 else "dve"
            if e == "act":
                P.emit("act", lambda h, half=half, pt=pt, n=n, c0=c0: h.activation(
                    out=hT[:, half * 4:half * 4 + 4, c0:c0 + n], in_=pt[:, :].rearrange("p (j t) -> p j t", j=4)[:, :, 0:n], func=AF.Copy),
                    reads=[pb], writes=[hTb])
            else:
                P.emit("dve", lambda h, half=half, pt=pt, n=n, c0=c0: h.tensor_copy(
                    out=hT[:, half * 4:half * 4 + 4, c0:c0 + n], in_=pt[:, :].rearrange("p (j t) -> p j t", j=4)[:, :, 0:n]),
                    reads=[pb], writes=[hTb])


def phase_rglru(P, C, nc, nseq):
    with contextlib.ExitStack() as st:
        sb = lambda name, shape, dt: st.enter_context(nc.sbuf_tensor("a_" + name, shape, dt))
        idf, idb, idB = make_ident(P, nc, st)
        lnt = alloc_ln_tmp(nc, st, P)
        pp = PsumPool(nc, st, 8, [128, 512], F32, "psA")
        pf = sb("pf", [128, PF_COLS], F32)
        pfB = Buf("pf")
        sem_c = P.dsem("semc_a")
        P.emit("sp", lambda h: h.dma_start(out=pf[:], in_=C.pf), writes=[pfB], dsem=sem_c)
        wout = sb("wout", [128, 8, D], BF16)
        woutB = Buf("wout")
        sem_wo = P.dsem("sem_wo")
        for kc in range(8):
            P.emit("pool", lambda h, kc=kc: h.dma_start(out=wout[:, kc, :], in_=C.w_out[kc * 128:(kc + 1) * 128, :]), writes=[woutB], dsem=sem_wo)
        wga = sb("wga", [128, 16, 128], BF16)
        wgx = sb("wgx", [128, 16, 128], BF16)
        wgB = Buf("wg")
        sem_wg = P.dsem("sem_wg")
        P.emit("pool", lambda h: h.dma_start(out=wga[:], in_=C.w_a.rearrange("r n c d -> c (r n) d")), writes=[wgB], dsem=sem_wg)
        P.emit("pool", lambda h: h.dma_start(out=wgx[:], in_=C.w_x.rearrange("r n c d -> c (r n) d")), writes=[wgB], dsem=sem_wg)
        gt = sb("lng", [128, 2, D], F32)
        gtB = Buf("lng")
        sem_g = P.dsem("sem_lng")
        P.emit("sp", lambda h: h.dma_start(out=gt[:, 0, :], in_=C.pt[PT["mix_g0"]:PT["mix_g0"] + 1, :].partition_broadcast(128)), writes=[gtB], dsem=sem_g)
        P.emit("sp", lambda h: h.dma_start(out=gt[:, 1, :], in_=C.pt[PT["mix_b0"]:PT["mix_b0"] + 1, :].partition_broadcast(128)), writes=[gtB], dsem=sem_g)
        cl = sb("cl", [128, 16], F32)
        clB = Buf("cl")
        lam = pf[:, PF["lam"]:PF["lam"] + 16]
        P.emit("act", lambda h: h.activation(out=cl[:], in_=lam, func=AF.Exp, scale=-1.0), reads=[pfB], writes=[clB])
        P.emit("act", lambda h: h.activation(out=cl[:], in_=cl[:], func=AF.Ln, bias=1.0, scale=1.0), reads=[clB], writes=[clB])
        P.emit("dve", lambda h: h.tensor_scalar(out=cl[:], in0=cl[:], scalar1=-8.0, scalar2=None, op0=ALU.mult), reads=[clB], writes=[clB])

        hT = sb("hT", [128, 8, L], BF16)
        hTB = Buf("hT")
        Y = sb("Y", [128, 8, L], BF16)
        YB = Buf("Y")
        xt = sb("xt", [128, D], F32)
        xtB = Buf("xt")
        sem_x = P.dsem("sem_x")
        win = [sb("win%d" % i, [128, 8, 256], BF16) for i in range(2)]
        winB = [Buf("win%d" % i) for i in range(2)]
        sem_win = [P.dsem("sem_win%d" % i) for i in range(2)]
        gg = sb("gg", [128, L], F32); ggB = Buf("gg")
        upad = sb("upad", [128, L + 3], F32); upB = Buf("upad")
        uc = sb("uc", [128, L], F32); ucB = Buf("uc")
        ucb = sb("ucb", [128, L], BF16); ucbB = Buf("ucb")
        ab = sb("ab", [128, L], F32); abB = Buf("ab")
        bb = sb("bb", [128, L], F32); bbB = Buf("bb")
        tm = sb("tm", [128, L], F32); tmB = Buf("tm")
        hf = sb("hf", [128, L], F32); hfB = Buf("hf")
        z = sb("z", [128, D], F32); zB = Buf("z")
        zo = sb("zo", [128, D], F32); zoB = Buf("zo")
        sem_o = P.dsem("sem_oa")
        xt2 = sb("xt2", [128, D], F32); xt2B = Buf("xt2"); sem_x2 = P.dsem("sem_x2")
        z2 = sb("z2", [128, D], F32); z2B = Buf("z2")
        zo2 = sb("zo2", [128, D], F32); zo2B = Buf("zo2")
        sem_o2 = P.dsem("sem_oa2")
        lnt2 = alloc_ln_tmp_named(nc, st, P, "a2_")
        LNB = [(xt, xtB, sem_x, z, zB, zo, zoB, sem_o, lnt), (xt2, xt2B, sem_x2, z2, z2B, zo2, zo2B, sem_o2, lnt2)]
        P.emit("pool", lambda h: h.memset(upad[:], 0.0), writes=[upB])
        pcs = pieces(L)
        wcount = 0
        for s in range(nseq):
            def loader(xt_, xtb_, sem_, n, p0, s=s):
                load_seq_tile(P, C, "sp", xt_, xtb_, sem_, s, p0, n)
            load_tokens_T(P, C, nc, hT, hTB, xt, xtB, sem_x, idf, idB, pp, loader, [(p0, n, (p0,)) for (p0, n) in pos_tiles()])
            for c in range(8):
                wi = wcount % 2
                wcount += 1
                w = win[wi]
                P.emit("pool", lambda h, w=w, c=c: h.dma_start(out=w[:, :, 0:128], in_=C.w_in[:, c * 128:(c + 1) * 128].rearrange("(kc p) n -> p kc n", p=128)),
                       writes=[winB[wi]], dsem=sem_win[wi])
                P.emit("pool", lambda h, w=w, c=c: h.dma_start(out=w[:, :, 128:256], in_=C.w_in[:, D + c * 128:D + (c + 1) * 128].rearrange("(kc p) n -> p kc n", p=128)),
                       writes=[winB[wi]], dsem=sem_win[wi])
                for (t0, tn) in pcs:
                    pt, pb = pp.get()
                    for kc in range(8):
                        P.emit("pe", lambda h, pt=pt, kc=kc, t0=t0, tn=tn, w=w: h.matmul(pt[:, 0:tn], lhsT=w[:, kc, 0:128], rhs=hT[:, kc, t0:t0 + tn],
                                                                                      start=(kc == 0), stop=(kc == 7)), reads=[winB[wi], hTB], writes=[pb])
                    P.emit("act", lambda h, pt=pt, t0=t0, tn=tn: h.activation(out=gg[:, t0:t0 + tn], in_=pt[:, 0:tn], func=AF.Gelu_apprx_tanh),
                           reads=[pb], writes=[ggB])
                for (t0, tn) in pcs:
                    pt, pb = pp.get()
                    for kc in range(8):
                        P.emit("pe", lambda h, pt=pt, kc=kc, t0=t0, tn=tn, w=w: h.matmul(pt[:, 0:tn], lhsT=w[:, kc, 128:256], rhs=hT[:, kc, t0:t0 + tn],
                                                                                      start=(kc == 0), stop=(kc == 7)), reads=[winB[wi], hTB], writes=[pb])
                    P.emit("act", lambda h, pt=pt, t0=t0, tn=tn: h.activation(out=upad[:, 2 + t0:2 + t0 + tn], in_=pt[:, 0:tn], func=AF.Copy),
                           reads=[pb], writes=[upB])
                cw = PF["conv_w"]
                P.emit("dve", lambda h, c=c: h.tensor_scalar(out=uc[:], in0=upad[:, 0:L], scalar1=pf[:, cw + c:cw + c + 1],
                                                             scalar2=pf[:, PF["conv_b"] + c:PF["conv_b"] + c + 1], op0=ALU.mult, op1=ALU.add),
                       reads=[upB, pfB], writes=[ucB])
                for k in range(1, 4):
                    P.emit("dve", lambda h, c=c, k=k: h.scalar_tensor_tensor(out=uc[:], in0=upad[:, k:k + L], scalar=pf[:, cw + k * 8 + c:cw + k * 8 + c + 1],
                                                                            in1=uc[:], op0=ALU.mult, op1=ALU.add), reads=[upB, pfB, ucB], writes=[ucB])
                P.emit("pool", lambda h: h.tensor_copy(out=ucb[:], in_=uc[:]), reads=[ucB], writes=[ucbB])
                for r in range(2):
                    gi = r * 8 + c
                    for (t0, tn) in pcs:
                        pt, pb = pp.get()
                        P.emit("pe", lambda h, pt=pt, t0=t0, tn=tn, gi=gi: h.matmul(pt[:, 0:tn], lhsT=wga[:, gi, :], rhs=ucb[:, t0:t0 + tn], start=True, stop=True),
                               reads=[wgB, ucbB], writes=[pb])
                        P.emit("act", lambda h, pt=pt, t0=t0, tn=tn, gi=gi: h.activation(out=ab[:, t0:t0 + tn], in_=pt[:, 0:tn], func=AF.Sigmoid,
                                                                                       bias=pf[:, PF["b_a"] + gi:PF["b_a"] + gi + 1], scale=1.0), reads=[pb, pfB], writes=[abB])
                    for (t0, tn) in pcs:
                        pt, pb = pp.get()
                        P.emit("pe", lambda h, pt=pt, t0=t0, tn=tn, gi=gi: h.matmul(pt[:, 0:tn], lhsT=wgx[:, gi, :], rhs=ucb[:, t0:t0 + tn], start=True, stop=True),
                               reads=[wgB, ucbB], writes=[pb])
                        P.emit("act", lambda h, pt=pt, t0=t0, tn=tn, gi=gi: h.activation(out=bb[:, t0:t0 + tn], in_=pt[:, 0:tn], func=AF.Sigmoid,
                                                                                       bias=pf[:, PF["b_x"] + gi:PF["b_x"] + gi + 1], scale=1.0), reads=[pb, pfB], writes=[bbB])
                    P.emit("act", lambda h, gi=gi: h.activation(out=ab[:], in_=ab[:], func=AF.Exp, scale=cl[:, gi:gi + 1]), reads=[abB, clB], writes=[abB])
                    P.emit("pool", lambda h: h.tensor_tensor(out=bb[:], in0=bb[:], in1=uc[:], op=ALU.mult), reads=[bbB, ucB], writes=[bbB])
                    P.emit("act", lambda h: h.activation(out=tm[:], in_=ab[:], func=AF.Square), reads=[abB], writes=[tmB])
                    P.emit("act", lambda h: h.activation(out=tm[:], in_=tm[:], func=AF.Sqrt, bias=1.0, scale=-1.0), reads=[tmB], writes=[tmB])
                    P.emit("pool", lambda h: h.tensor_tensor(out=bb[:], in0=bb[:], in1=tm[:], op=ALU.mult), reads=[bbB, tmB], writes=[bbB])
                    if r == 0:
                        P.emit("dve", lambda h: h.tensor_tensor_scan(out=hf[:], data0=ab[:], data1=bb[:], initial=0.0, op0=ALU.mult, op1=ALU.add),
                               reads=[abB, bbB], writes=[hfB])
                    else:
                        P.emit("dve", lambda h: h.tensor_tensor_scan(out=tm[:, ::-1], data0=ab[:, ::-1], data1=bb[:, ::-1], initial=0.0, op0=ALU.mult, op1=ALU.add),
                               reads=[abB, bbB], writes=[tmB])
                P.emit("pool", lambda h: h.tensor_tensor(out=hf[:], in0=hf[:], in1=tm[:], op=ALU.add), reads=[hfB, tmB], writes=[hfB])
                P.emit("dve", lambda h, c=c: h.tensor_tensor(out=Y[:, c, :], in0=hf[:], in1=gg[:], op=ALU.mult), reads=[hfB, ggB], writes=[YB])
            for ti, (p0, n) in enumerate(pos_tiles()):
                (xt_, xtB_, sem_x_, z_, zB_, zo_, zoB_, sem_o_, lnt_) = LNB[ti % 2]
                loader(xt_, xtB_, sem_x_, n, p0)
                for half in range(2):
                    pt, pb = pp.get()
                    for kc in range(8):
                        P.emit("pe", lambda h: h.matmul(pt[0:n, :], lhsT=Y[:, kc, p0:p0 + n], rhs=wout[:, kc, half * 512:(half + 1) * 512],
                                                        start=(kc == 0), stop=(kc == 7)), reads=[YB, woutB], writes=[pb])
                    P.emit("dve", lambda h: h.scalar_tensor_tensor(out=z_[0:n, half * 512:(half + 1) * 512], in0=xt_[0:n, half * 512:(half + 1) * 512],
                                                                   scalar=ALPHA, in1=pt[0:n, :], op0=ALU.mult, op1=ALU.add),
                           reads=[xtB_, pb], writes=[zB_])
                layer_norm_tile(P, C, gtB, z_, zB_, n, gt[:, 0, :], gt[:, 1, :], zo_, zoB_, lnt_)
                r0 = s * L + p0
                P.emit("sp", lambda h: h.dma_start(out=C.H1[r0:r0 + n, :], in_=zo_[0:n, :]), reads=[zoB_], writes=[C.H1B], dsem=sem_o_)
        P.wait_all("sp", [(sem_o, sem_o.count), (sem_o2, sem_o2.count)])
        P.flush_block([C.H1B])


def build_program(nseq, stop_after=None):
    nc = bass.Bass("TRN2", target_bir_lowering=False)
    C = Ctx()
    T = nseq * L
    di = lambda name, shape: nc.dram_tensor(name, shape, F32, kind="ExternalInput").ap()
    C.x = di("x", [nseq, SEQ, D])
    C.meta = di("meta", [NMETA, D])
    C.w_in = di("w_in", [D, 2 * D])
    C.w_a = di("w_a", [2, 8, 128, 128])
    C.w_x = di("w_x", [2, 8, 128, 128])
    C.w_out = di("w_out", [D, D])
    C.pw1 = di("pw1", [D, 2 * D])
    C.pw2 = di("pw2", [D, D])
    C.wq = di("wq", [2, D, 2 * D])
    C.kt = di("kt", [2, 2, 128, 128])
    C.ut = di("ut", [2, D, NEXP])
    C.v = di("v", [2, NEXP, D])
    C.pf = di("pf", [128, PF_COLS])
    C.pt = di("pt", [len(PT), D])
    dbg = stop_after is not None
    mk = lambda name, shape, out: nc.dram_tensor(name, shape, F32, kind=("ExternalOutput" if out else "Internal")).ap()
    C.H1 = mk("H1", [T, D], dbg and stop_after == 1)
    C.H1B = Buf("H1")
    C.H2 = mk("H2", [T, D], dbg and stop_after == 2)
    C.H2B = Buf("H2")
    C.H3 = mk("H3", [T, D], dbg and stop_after == 3)
    C.H3B = Buf("H3")
    C.out = nc.dram_tensor("out", [nseq, SEQ, D], F32, kind="ExternalOutput").ap() if (stop_after is None or stop_after == 4) else None
    C.outB = Buf("out")
    with contextlib.ExitStack() as st:
        P = Prog(nc, st)
        phase_rglru(P, C, nc, nseq)
        if stop_after == 1:
            return nc
        phase_peer(P, C, nc, T, 0, C.H1, C.H1B, C.H2, C.H2B, False, "b_")
        if stop_after == 2:
            return nc
        phase_conf(P, C, nc, nseq, C.H2, C.H2B, C.H3, C.H3B, "c_")
        if stop_after == 3:
            return nc
        phase_peer(P, C, nc, T, 1, C.H3, C.H3B, C.out, C.outB, True, "d_")
    return nc


def pack_fm(v):
    v = np.asarray(v, np.float32).reshape(-1, 128)
    return np.ascontiguousarray(v.T)


def make_shared_inputs(inp):
    f = lambda a: np.ascontiguousarray(np.asarray(a, np.float32))
    pfm = np.zeros((128, PF_COLS), np.float32)
    cw = inp["lru_conv_w"][0]
    for k in range(4):
        pfm[:, PF["conv_w"] + k * 8:PF["conv_w"] + k * 8 + 8] = pack_fm(cw[k])
    pfm[:, PF["conv_b"]:PF["conv_b"] + 8] = pack_fm(inp["lru_conv_b"][0])
    pfm[:, PF["b_a"]:PF["b_a"] + 16] = pack_fm(inp["lru_b_a"][0].reshape(-1))
    pfm[:, PF["b_x"]:PF["b_x"] + 16] = pack_fm(inp["lru_b_x"][0].reshape(-1))
    pfm[:, PF["lam"]:PF["lam"] + 16] = pack_fm(inp["lru_lambda"][0].reshape(-1))
    pfm[:, PF["b_pw1"]:PF["b_pw1"] + 16] = pack_fm(inp["conf_b_pw1"][0])
    dw = inp["conf_dw_w"][0]
    for k in range(31):
        pfm[:, PF["dw_w"] + k * 8:PF["dw_w"] + k * 8 + 8] = pack_fm(dw[k])
    pfm[:, PF["dw_b"]:PF["dw_b"] + 8] = pack_fm(inp["conf_dw_b"][0])
    pfm[:, PF["cln_g"]:PF["cln_g"] + 8] = pack_fm(inp["conf_ln_g"][0])
    pfm[:, PF["cln_b"]:PF["cln_b"] + 8] = pack_fm(inp["conf_ln_b"][0])
    ptm = np.zeros((len(PT), D), np.float32)
    for i in range(2):
        ptm[PT["mix_g%d" % i]] = inp["ln_mix_g"][i]
        ptm[PT["mix_b%d" % i]] = inp["ln_mix_b"][i]
        ptm[PT["ffn_g%d" % i]] = inp["ln_ffn_g"][i]
        ptm[PT["ffn_b%d" % i]] = inp["ln_ffn_b"][i]
    ptm[PT["b_pw2"]] = inp["conf_b_pw2"][0]
    sh = {
        "meta": f(inp["meta_tokens"]),
        "w_in": f(inp["lru_w_in"][0]),
        "w_a": f(inp["lru_w_a"][0]),
        "w_x": f(inp["lru_w_x"][0]),
        "w_out": f(inp["lru_w_out"][0]),
        "pw1": f(inp["conf_w_pw1"][0]),
        "pw2": f(inp["conf_w_pw2"][0]),
        "wq": f(inp["peer_w_query"]),
        "kt": f(np.transpose(np.asarray(inp["peer_sub_keys"], np.float32), (0, 1, 3, 2))),
        "ut": f(np.transpose(np.asarray(inp["peer_u"], np.float32), (0, 2, 1))),
        "v": f(inp["peer_v"]),
        "pf": pfm,
        "pt": ptm,
    }
    return sh


def kernel(**inputs):
    x = np.asarray(inputs["x"], np.float32)
    nseq = x.shape[0] // NCORES
    sh = make_shared_inputs(inputs)
    nc = build_program(nseq)
    in_maps = []
    for c in range(NCORES):
        m = dict(sh)
        m["x"] = np.ascontiguousarray(x[c * nseq:(c + 1) * nseq])
        in_maps.append(m)
    res = run_bass_kernel_spmd(nc, in_maps, core_ids=list(range(NCORES)))
    return np.concatenate([r["out"] for r in res.results], axis=0)


def out_segments(r0, n):
    segs = []
    r = r0
    while r < r0 + n:
        s, pos = divmod(r, L)
        if pos < NMETA:
            r += min(NMETA - pos, r0 + n - r)
            continue
        cnt = min(L - pos, r0 + n - r)
        segs.append((r - r0, cnt, s, pos - NMETA))
        r += cnt
    return segs


def phase_peer(P, C, nc, T, layer, Hin, HinB, Hout, HoutB, final, pfx):
    GN = 4
    NG = NKEY // GN
    GE = GN * NKEY
    with contextlib.ExitStack() as st:
        sb = lambda name, shape, dt: st.enter_context(nc.sbuf_tensor(pfx + name, shape, dt))
        idf, idb, idB = make_ident_named(P, nc, st, pfx)
        lnt = alloc_ln_tmp_named(nc, st, P, pfx)
        ppA = PsumPool(nc, st, 2, [128, 512], F32, pfx + "psA")
        _ptt = st.enter_context(nc.psum_tensor(pfx + "psT", [128, 2, 4, 128], BF16))
        ppT = PsumPool.__new__(PsumPool); ppT.t = [_ptt[:, 0], _ptt[:, 1]]; _pTB = Buf("psT"); ppT.b = [_pTB, _pTB]; ppT.i = 0
        ppO = PsumPool(nc, st, 2, [128, 1024], F32, pfx + "psO")
        s1g = st.enter_context(nc.psum_tensor(pfx + "s1g", [128, 2, 8, GN], F32)); _s1gB = Buf("s1g"); s1gB = [_s1gB, _s1gB]
        wq = sb("wq", [128, 8, 2 * D], BF16); wqB = Buf("wq")
        sem_wq = P.dsem(pfx + "sem_wq")
        for kc in range(8):
            P.emit("pool", lambda h, kc=kc: h.dma_start(out=wq[:, kc, :], in_=C.wq[layer, kc * 128:(kc + 1) * 128, :], max_dma_last_dim=4096), writes=[wqB], dsem=sem_wq)
        kt = sb("kt", [128, 2, 128], BF16); ktB = Buf("kt")
        sem_kt = P.dsem(pfx + "sem_kt")
        P.emit("pool", lambda h: h.dma_start(out=kt[:], in_=C.kt[layer].rearrange("p k n -> k p n")), writes=[ktB], dsem=sem_kt)
        gt = sb("lng", [128, 2, D], F32); gtB = Buf("lng")
        sem_g = P.dsem(pfx + "sem_lng")
        gi, bi = PT["ffn_g%d" % layer], PT["ffn_b%d" % layer]
        P.emit("sp", lambda h: h.dma_start(out=gt[:, 0, :], in_=C.pt[gi:gi + 1, :].partition_broadcast(128)), writes=[gtB], dsem=sem_g)
        P.emit("sp", lambda h: h.dma_start(out=gt[:, 1, :], in_=C.pt[bi:bi + 1, :].partition_broadcast(128)), writes=[gtB], dsem=sem_g)
        xt = sb("xt", [128, D], F32); xtB = Buf("xt"); sem_x = P.dsem(pfx + "sem_x")
        hT = sb("hT", [128, 8, 512], BF16); hTB = Buf("hT")
        P.emit("pool", lambda h: h.memset(hT[:], 0.0), writes=[hTB])
        P.emit("pool", lambda h: h.memset(xt[:], 0.0), writes=[xtB])
        qTs = [sb("qT%d" % i, [128, 512], BF16) for i in range(2)]; qTB = [Buf("qT%d" % i) for i in range(2)]
        S = sb("S", [128, 4, 16, 128], F32); SB = [[Buf("S%d_%d" % (i, j)) for j in range(16)] for i in range(4)]
        TAU = sb("TAU", [128, 4, 8], F32); TAUB = [Buf("TAU%d" % i) for i in range(4)]
        O = sb("O", [128, 4, D], F32); OB = [Buf("O%d" % i) for i in range(4)]
        T16 = sb("T16", [128, 4, 16, 16], F32); T16B = [Buf("T16_%d" % i) for i in range(4)]
        tmpS = sb("tmpS", [128, 2, 128], F32); tmpSB = [Buf("tmpS0"), Buf("tmpS1")]
        cand2 = sb("cand2", [128, 256], F32); cand2B = Buf("cand2")
        c24 = sb("c24", [128, 8, 24], F32); c24B = Buf("c24")
        e16 = sb("e16", [128, 8, 16], F32); e16B = Buf("e16")
        zs = sb("zs", [128, 8], F32); zsB = Buf("zs")
        mb = sb("mb", [128, 8], F32); mbB = Buf("mb")
        UT = [sb("UT%d" % i, [128, 8, GE], BF16) for i in range(2)]; UTB = [Buf("UT%d" % i) for i in range(2)]
        VG = [sb("VG%d" % i, [128, GN, D], BF16) for i in range(2)]; VGB = [Buf("VG%d" % i) for i in range(2)]
        sem_u = [P.dsem(pfx + "sem_u%d" % i) for i in range(2)]
        sem_v = [P.dsem(pfx + "sem_v%d" % i) for i in range(2)]
        G = [sb("G%d" % i, [128, 8, GN, 128], F32) for i in range(2)]; GB = [[Buf("G%d_%d" % (i, j)) for j in range(2)] for i in range(2)]
        cand = G[0][:, 0:4].rearrange("p h a n -> p (h a n)").rearrange("p (h c) -> p h c", h=8); candB = GB[0][0]
        E = [sb("E%d" % i, [128, 8, GN, 128], BF16) for i in range(2)]
        EB = [[Buf("E%d_%d" % (i, j)) for j in range(8)] for i in range(2)]
        A = [sb("A%d" % i, [128, GE], BF16) for i in range(2)]; AB = [Buf("A%d" % i) for i in range(2)]
        WA = [sb("WA%d" % i, [128, GE], BF16) for i in range(2)]; WAB = [Buf("WA%d" % i) for i in range(2)]
        WT = [sb("WT%d" % i, [128, GN, 128], BF16) for i in range(2)]; WTB = [Buf("WT%d" % i) for i in range(2)]
        z = sb("z", [128, D], F32); zB = Buf("z")
        zo = sb("zo", [128, D], F32); zoB = Buf("zo")
        sem_o = P.dsem(pfx + "sem_o")

        tiles = [(r0, min(128, T - r0)) for r0 in range(0, T, 128)]
        cnt = 0
        for s0 in range(0, len(tiles), 4):
            tl = tiles[s0:s0 + 4]
            ntl = len(tl)
            ncol = ntl * 128

            def loader(xt_, xtb_, sem_, n, r0):
                P.emit("sp", lambda h: h.dma_start(out=xt_[0:n, :], in_=Hin[r0:r0 + n, :]), reads=[HinB], writes=[xtb_], dsem=sem_)
            for i, (r0, n) in enumerate(tl):
                loader(xt, xtB, sem_x, n, r0)
                for half in range(2):
                    pt, pb = ppA.get()
                    for j in range(4):
                        kc = half * 4 + j
                        P.emit("pe", lambda h, kc=kc, j=j, pt=pt: h.transpose(out=pt[:, j * 128:(j + 1) * 128], in_=xt[:, kc * 128:(kc + 1) * 128], identity=idf[:]),
                               reads=[xtB, idB], writes=[pb])
                    if half == 0:
                        P.emit("act", lambda h, pt=pt, i=i: h.activation(out=hT[:, 0:4, i * 128:(i + 1) * 128], in_=pt[:, :].rearrange("p (j t) -> p j t", j=4), func=AF.Copy),
                               reads=[pb], writes=[hTB])
                    else:
                        P.emit("dve", lambda h, pt=pt, i=i: h.tensor_copy(out=hT[:, 4:8, i * 128:(i + 1) * 128], in_=pt[:, :].rearrange("p (j t) -> p j t", j=4)),
                               reads=[pb], writes=[hTB])
            for j in range(16):
                pt, pb = ppA.get()
                for kc in range(8):
                    P.emit("pe", lambda h, pt=pt, kc=kc, j=j: h.matmul(pt[:, 0:ncol], lhsT=wq[:, kc, j * 128:(j + 1) * 128], rhs=hT[:, kc, 0:ncol],
                                                                     start=(kc == 0), stop=(kc == 7)), reads=[wqB, hTB], writes=[pb])
                q = qTs[j % 2]; qb = qTB[j % 2]
                if j % 2 == 0:
                    P.emit("act", lambda h, pt=pt, q=q: h.activation(out=q[:, 0:ncol], in_=pt[:, 0:ncol], func=AF.Copy), reads=[pb], writes=[qb])
                else:
                    P.emit("dve", lambda h, pt=pt, q=q: h.tensor_copy(out=q[:, 0:ncol], in_=pt[:, 0:ncol]), reads=[pb], writes=[qb])
                po, pob = ppO.get()
                for i in range(ntl):
                    P.emit("pe", lambda h, po=po, i=i, q=q, j=j: h.matmul(po[:, i * 128:(i + 1) * 128], lhsT=q[:, i * 128:(i + 1) * 128], rhs=kt[:, j % 2, :], start=True, stop=True),
                           reads=[qb, ktB], writes=[pob])
                eng = "act" if j % 2 == 1 else "dve"
                if eng == "act":
                    P.emit("act", lambda h, po=po, j=j: h.activation(out=S[:, 0:ntl, j, :], in_=po[:, 0:ncol].rearrange("p (t n) -> p t n", n=128), func=AF.Copy),
                           reads=[pob], writes=[SB[i][j] for i in range(ntl)])
                else:
                    P.emit("dve", lambda h, po=po, j=j: h.tensor_copy(out=S[:, 0:ntl, j, :], in_=po[:, 0:ncol].rearrange("p (t n) -> p t n", n=128)),
                           reads=[pob], writes=[SB[i][j] for i in range(ntl)])
                for i in range(ntl):
                    tb = (j * 4 + i) % 2
                    P.emit("dve", lambda h: h.max(out=T16[:, i, j, 0:8], in_=S[:, i, j, :]), reads=[SB[i][j]], writes=[T16B[i]])
                    P.emit("dve", lambda h: h.match_replace(out=tmpS[:, tb], in_to_replace=T16[:, i, j, 0:8], in_values=S[:, i, j, :], imm_value=-1e30),
                           reads=[SB[i][j], T16B[i]], writes=[tmpSB[tb]])
                    P.emit("dve", lambda h: h.max(out=T16[:, i, j, 8:16], in_=tmpS[:, tb]), reads=[tmpSB[tb]], writes=[T16B[i]])
            for i in range(ntl):
                P.emit("dve", lambda h, i=i: h.tensor_tensor(out=cand[:].rearrange("p h (a b) -> p h a b", a=16),
                                                        in0=T16[:, i, 0::2, :].unsqueeze(3).to_broadcast([128, 8, 16, 16]),
                                                        in1=T16[:, i, 1::2, :].unsqueeze(2).to_broadcast([128, 8, 16, 16]), op=ALU.add), reads=[T16B[i]], writes=[candB])
                for hh in range(8):
                    P.emit("dve", lambda h, hh=hh: h.max(out=c24[:, hh, 0:8], in_=cand[:, hh, :]), reads=[candB], writes=[c24B])
                    P.emit("dve", lambda h, hh=hh: h.match_replace(out=cand2[:], in_to_replace=c24[:, hh, 0:8], in_values=cand[:, hh, :], imm_value=-1e30),
                           reads=[candB, c24B], writes=[cand2B])
                    P.emit("dve", lambda h, hh=hh: h.max(out=c24[:, hh, 8:16], in_=cand2[:]), reads=[cand2B], writes=[c24B])
                    P.emit("dve", lambda h, hh=hh: h.match_replace(out=cand2[:], in_to_replace=c24[:, hh, 8:16], in_values=cand2[:], imm_value=-1e30),
                           reads=[cand2B, c24B], writes=[cand2B])
                    P.emit("dve", lambda h, hh=hh: h.max(out=c24[:, hh, 16:24], in_=cand2[:]), reads=[cand2B], writes=[c24B])
                P.emit("dve", lambda h: h.tensor_tensor(out=e16[:], in0=c24[:, :, 0:16], in1=c24[:, :, 0:1].to_broadcast([128, 8, 16]), op=ALU.subtract),
                       reads=[c24B], writes=[e16B])
                P.emit("act", lambda h: h.activation(out=e16[:], in_=e16[:], func=AF.Exp), reads=[e16B], writes=[e16B])
                P.emit("dve", lambda h: h.tensor_reduce(out=zs[:], in_=e16[:], axis=AX.X, op=ALU.add), reads=[e16B], writes=[zsB])
                P.emit("act", lambda h: h.activation(out=zs[:], in_=zs[:], func=AF.Ln), reads=[zsB], writes=[zsB])
                P.emit("dve", lambda h: h.tensor_tensor(out=mb[:], in0=zs[:], in1=c24[:, :, 0], op=ALU.add), reads=[zsB, c24B], writes=[mbB])
                P.emit("dve", lambda h: h.tensor_tensor(out=zs[:], in0=c24[:, :, 15], in1=c24[:, :, 16], op=ALU.add), reads=[c24B, zsB], writes=[zsB])
                P.emit("dve", lambda h, i=i: h.scalar_tensor_tensor(out=TAU[:, i, :], in0=zs[:], scalar=0.5, in1=mb[:], op0=ALU.mult, op1=ALU.subtract),
                       reads=[zsB, mbB], writes=[TAUB[i]])
                P.emit("dve", lambda h, i=i: h.tensor_tensor(out=mb[:], in0=mb[:], in1=TAU[:, i, :], op=ALU.add), reads=[mbB, TAUB[i]], writes=[mbB])
                P.emit("dve", lambda h, i=i: h.tensor_tensor(out=S[:, i, 0::2, :], in0=S[:, i, 0::2, :], in1=mb[:].unsqueeze(2).to_broadcast([128, 8, 128]), op=ALU.subtract),
                       reads=SB[i][0::2] + [mbB], writes=SB[i][0::2])

            def load_group(g):
                b = g % 2
                P.emit("pool", lambda h: h.dma_start(out=UT[b][:], in_=C.ut[layer, :, g * GE:(g + 1) * GE].rearrange("(kc p) e -> p kc e", p=128)),
                       writes=[UTB[b]], dsem=sem_u[b])
                P.emit("pool", lambda h: h.dma_start(out=VG[b][:], in_=C.v[layer, g * GE:(g + 1) * GE, :].rearrange("(ec p) d -> p ec d", p=128)),
                       writes=[VGB[b]], dsem=sem_v[b])

            items = [(g, i) for g in range(NG) for i in range(ntl)]
            po_of = {}
            ptt_of = {}

            def st_tr(k):
                eb = k % 2
                ptt, pttb = ppT.get()
                for ec in range(GN):
                    P.emit("pe", lambda h: h.transpose(out=ptt[:, ec, :], in_=WA[eb][:, ec * 128:(ec + 1) * 128], identity=idb[:]),
                           reads=[WAB[eb], idB], writes=[pttb])
                P.emit("act", lambda h: h.activation(out=WT[eb][:], in_=ptt[:], func=AF.Copy), reads=[pttb], writes=[WTB[eb]])

            def st_s1g(k):
                g, i = items[k]
                eb = k % 2
                P.emit("act", lambda h: h.activation(out=s1g[:, eb], in_=S[:, i, 0::2, g * GN:(g + 1) * GN], func=AF.Copy), reads=SB[i][0::2], writes=[s1gB[eb]])

            def st_front(k):
                g, i = items[k]
                b = g % 2
                eb = k % 2
                Ei, EiB = E[eb], EB[eb]
                Gk, GkB = G[eb], GB[eb]
                pa, pab = ppA.get()
                P.emit("dve", lambda h: h.tensor_tensor(out=Gk[:], in0=S[:, i, 1::2, :].unsqueeze(2).to_broadcast([128, 8, GN, 128]),
                                                        in1=s1g[:, eb].unsqueeze(3).to_broadcast([128, 8, GN, 128]), op=ALU.add),
                       reads=SB[i][1::2] + [s1gB[eb]], writes=[GkB[0]])
                if k + 1 < nit:
                    st_s1g(k + 1)
                for hh in range(8):
                    P.emit("act", lambda h: h.activation(out=Ei[:, hh], in_=Gk[:, hh], func=AF.Exp, bias=TAU[:, i, hh:hh + 1], scale=1.0),
                           reads=[GkB[0], TAUB[i]], writes=[EiB[hh]])
                for kc in range(8):
                    P.emit("pe", lambda h: h.matmul(pa[:, :], lhsT=hT[:, kc, i * 128:(i + 1) * 128], rhs=UT[b][:, kc, :], start=(kc == 0), stop=(kc == 7)),
                           reads=[hTB, UTB[b]], writes=[pab])
                P.emit("act", lambda h: h.activation(out=A[eb][:], in_=pa[:, :], func=AF.Gelu_apprx_tanh), reads=[pab], writes=[AB[eb]])

            def st_mid(k):
                g, i = items[k]
                eb = k % 2
                Ei, EiB = E[eb], EB[eb]
                Gk, GkB = G[eb], GB[eb]
                P.emit("dve", lambda h: h.scalar_tensor_tensor(out=Ei[:].rearrange("p h a n -> p (h a n)"), in0=Gk[:].rearrange("p h a n -> p (h a n)"), scalar=0.0,
                                                               in1=Ei[:].rearrange("p h a n -> p (h a n)"), op0=ALU.is_ge, op1=ALU.mult),
                       reads=[GkB[0]] + EiB[0:8], writes=EiB[0:8])
                P.emit("dve", lambda h: h.tensor_tensor(out=Ei[:, 0:4], in0=Ei[:, 0:4], in1=Ei[:, 4:8], op=ALU.add), reads=EiB[0:8], writes=EiB[0:4])
                P.emit("dve", lambda h: h.tensor_tensor(out=Ei[:, 0:2], in0=Ei[:, 0:2], in1=Ei[:, 2:4], op=ALU.add), reads=EiB[0:4], writes=EiB[0:2])
                P.emit("dve", lambda h: h.tensor_tensor(out=Ei[:, 0], in0=Ei[:, 0], in1=Ei[:, 1], op=ALU.add), reads=EiB[0:2], writes=EiB[0:1])
                P.emit("dve", lambda h: h.tensor_tensor(out=WA[eb][:], in0=A[eb][:], in1=Ei[:, 0].rearrange("p a n -> p (a n)"), op=ALU.mult),
                       reads=[AB[eb], EiB[0]], writes=[WAB[eb]])

            def st_vmm(k):
                g, i = items[k]
                b = g % 2
                eb = k % 2
                po, pob = ppO.get()
                for dh in range(2):
                    if g > 0:
                        P.emit("pe", lambda h: h.matmul(po[:, dh * 512:(dh + 1) * 512], lhsT=idf[:], rhs=O[:, i, dh * 512:(dh + 1) * 512], start=True, stop=False),
                               reads=[idB, OB[i]], writes=[pob])
                    for ec in range(GN):
                        P.emit("pe", lambda h: h.matmul(po[:, dh * 512:(dh + 1) * 512], lhsT=WT[eb][:, ec, :], rhs=VG[b][:, ec, dh * 512:(dh + 1) * 512],
                                                        start=(ec == 0 and g == 0), stop=(ec == GN - 1)),
                               reads=[WTB[eb], VGB[b]], writes=[pob])
                po_of[k] = (po, pob)

            def st_acc(k):
                g, i = items[k]
                po, pob = po_of.pop(k)
                P.emit("act", lambda h: h.activation(out=O[:, i, :], in_=po[:, :], func=AF.Copy), reads=[pob], writes=[OB[i]])

            load_group(0)
            if NG > 1:
                load_group(1)
            nit = len(items)
            st_s1g(0)
            for k in range(nit + 3):
                if k < nit:
                    st_front(k)
                if 0 <= k - 2 < nit:
                    st_acc(k - 2)
                if 0 <= k - 1 < nit:
                    st_mid(k - 1)
                    st_tr(k - 1)
                    st_vmm(k - 1)
                    gk, ik = items[k - 1]
                    if ik == ntl - 1 and gk + 2 < NG:
                        load_group(gk + 2)
            for i, (r0, n) in enumerate(tl):
                loader(xt, xtB, sem_x, n, r0)
                P.emit("dve", lambda h, i=i: h.scalar_tensor_tensor(out=z[:], in0=xt[:], scalar=ALPHA, in1=O[:, i, :], op0=ALU.mult, op1=ALU.add),
                       reads=[xtB, OB[i]], writes=[zB])
                layer_norm_tile(P, C, gtB, z, zB, 128, gt[:, 0, :], gt[:, 1, :], zo, zoB, lnt)
                if not final:
                    P.emit("sp", lambda h, r0=r0, n=n: h.dma_start(out=Hout[r0:r0 + n, :], in_=zo[0:n, :]), reads=[zoB], writes=[HoutB], dsem=sem_o)
                else:
                    for (ro, c, sq, ps) in out_segments(r0, n):
                        P.emit("sp", lambda h, ro=ro, c=c, sq=sq, ps=ps: h.dma_start(out=Hout[sq, ps:ps + c, :], in_=zo[ro:ro + c, :]), reads=[zoB], writes=[HoutB], dsem=sem_o)
        P.wait_all("sp", [(sem_o, sem_o.count)])
        P.flush_block([HinB, HoutB])


def make_ident_named(P, nc, st, pfx):
    idf = st.enter_context(nc.sbuf_tensor(pfx + "identf", [128, 128], F32))
    idb = st.enter_context(nc.sbuf_tensor(pfx + "identb", [128, 128], BF16))
    B = Buf("ident")
    P.emit("pool", lambda h: h.memset(idf[:], 1.0), writes=[B])
    P.emit("pool", lambda h: h.affine_select(out=idf[:], in_=idf[:], pattern=[[-1, 128]], base=0, channel_multiplier=1,
                                              compare_op=ALU.is_equal, fill=0.0), reads=[B], writes=[B])
    P.emit("pool", lambda h: h.tensor_copy(out=idb[:], in_=idf[:]), reads=[B], writes=[B])
    return idf, idb, B


def alloc_ln_tmp_named(nc, st, P, pfx):
    t = {}
    t["stats"] = st.enter_context(nc.sbuf_tensor(pfx + "ln_stats", [128, 2, 6], F32))
    t["statsb"] = Buf("ln_stats")
    t["mv"] = st.enter_context(nc.sbuf_tensor(pfx + "ln_mv", [128, 2], F32))
    t["mvb"] = Buf("ln_mv")
    t["rs"] = st.enter_context(nc.sbuf_tensor(pfx + "ln_rs", [128, 1], F32))
    t["rsb"] = Buf("ln_rs")
    t["eps"] = st.enter_context(nc.sbuf_tensor(pfx + "ln_eps", [128, 1], F32))
    P.emit("pool", lambda h: h.memset(t["eps"][:], EPS), writes=[Buf()])
    return t


def phase_conf(P, C, nc, nseq, Hin, HinB, Hout, HoutB, pfx):
    KW = 31
    PADW = KW // 2
    with contextlib.ExitStack() as st:
        sb = lambda name, shape, dt: st.enter_context(nc.sbuf_tensor(pfx + name, shape, dt))
        idf, idb, idB = make_ident_named(P, nc, st, pfx)
        lnt = alloc_ln_tmp_named(nc, st, P, pfx)
        pp = PsumPool(nc, st, 8, [128, 512], F32, pfx + "ps")
        pf = sb("pf", [128, PF_COLS], F32); pfB = Buf("pf")
        sem_c = P.dsem(pfx + "semc")
        P.emit("sp", lambda h: h.dma_start(out=pf[:], in_=C.pf), writes=[pfB], dsem=sem_c)
        ones = sb("ones", [128, 128], F32); onesB = Buf("ones")
        P.emit("pool", lambda h: h.memset(ones[:], 1.0 / D), writes=[onesB])
        pw2 = sb("pw2", [128, 8, D], BF16); pw2B = Buf("pw2")
        sem_p2 = P.dsem(pfx + "sem_p2")
        for kc in range(8):
            P.emit("pool", lambda h, kc=kc: h.dma_start(out=pw2[:, kc, :], in_=C.pw2[kc * 128:(kc + 1) * 128, :]), writes=[pw2B], dsem=sem_p2)
        gt = sb("lng", [128, 3, D], F32); gtB = Buf("lng")
        sem_g = P.dsem(pfx + "sem_lng")
        for k, nm in enumerate(("mix_g1", "mix_b1", "b_pw2")):
            P.emit("sp", lambda h, k=k, nm=nm: h.dma_start(out=gt[:, k, :], in_=C.pt[PT[nm]:PT[nm] + 1, :].partition_broadcast(128)), writes=[gtB], dsem=sem_g)
        hT = sb("hT", [128, 8, L], BF16); hTB = Buf("hT")
        CV = sb("CV", [128, 8, L], F32); CVB = [Buf("CV%d" % i) for i in range(8)]
        xt = sb("xt", [128, D], F32); xtB = Buf("xt"); sem_x = P.dsem(pfx + "sem_x")
        win = [sb("win%d" % i, [128, 8, 256], BF16) for i in range(2)]
        winB = [Buf("win%d" % i) for i in range(2)]
        sem_win = [P.dsem(pfx + "sem_win%d" % i) for i in range(2)]
        sig = sb("sig", [128, L], F32); sigB = Buf("sig")
        gpad = sb("gpad", [128, L + 2 * PADW], F32); gpB = Buf("gpad")
        P.emit("pool", lambda h: h.memset(gpad[:], 0.0), writes=[gpB])
        sq = [sb("sq%d" % i, [128, 512], F32) for i in range(2)]; sqB = [Buf("sq%d" % i) for i in range(2)]
        mean = sb("mean", [128, 512], F32); meanB = Buf("mean")
        rstd = sb("rstd", [128, 512], F32); rstdB = Buf("rstd")
        tq = [sb("tq%d" % i, [128, 512], F32) for i in range(2)]; tqB = [Buf("tq%d" % i) for i in range(2)]
        z = sb("z", [128, D], F32); zB = Buf("z")
        zo = sb("zo", [128, D], F32); zoB = Buf("zo")
        sem_o = P.dsem(pfx + "sem_o")
        xt2 = sb("xt2", [128, D], F32); xt2B = Buf("xt2"); sem_x2 = P.dsem(pfx + "sem_x2")
        z2 = sb("z2", [128, D], F32); z2B = Buf("z2")
        zo2 = sb("zo2", [128, D], F32); zo2B = Buf("zo2")
        sem_o2 = P.dsem(pfx + "sem_o2")
        lnt2 = alloc_ln_tmp_named(nc, st, P, pfx + "2_")
        LNB = [(xt, xtB, sem_x, z, zB, zo, zoB, sem_o, lnt), (xt2, xt2B, sem_x2, z2, z2B, zo2, zo2B, sem_o2, lnt2)]
        pcs = pieces(L)
        wcount = 0
        sqc = 0
        for s in range(nseq):
            def loader(xt_, xtb_, sem_, n, p0, s=s):
                r0 = s * L + p0
                P.emit("sp", lambda h: h.dma_start(out=xt_[0:n, :], in_=Hin[r0:r0 + n, :]), reads=[HinB], writes=[xtb_], dsem=sem_)
            load_tokens_T(P, C, nc, hT, hTB, xt, xtB, sem_x, idf, idB, pp, loader, [(p0, n, (p0,)) for (p0, n) in pos_tiles()])
            for c in range(8):
                wi = wcount % 2
                wcount += 1
                w = win[wi]
                P.emit("pool", lambda h: h.dma_start(out=w[:, :, 0:128], in_=C.pw1[:, c * 128:(c + 1) * 128].rearrange("(kc p) n -> p kc n", p=128)),
                       writes=[winB[wi]], dsem=sem_win[wi])
                P.emit("pool", lambda h: h.dma_start(out=w[:, :, 128:256], in_=C.pw1[:, D + c * 128:D + (c + 1) * 128].rearrange("(kc p) n -> p kc n", p=128)),
                       writes=[winB[wi]], dsem=sem_win[wi])
                for (t0, tn) in pcs:
                    pa, pab = pp.get()
                    pg, pgb = pp.get()
                    for kc in range(8):
                        P.emit("pe", lambda h: h.matmul(pg[:, 0:tn], lhsT=w[:, kc, 128:256], rhs=hT[:, kc, t0:t0 + tn], start=(kc == 0), stop=(kc == 7)),
                               reads=[winB[wi], hTB], writes=[pgb])
                    for kc in range(8):
                        P.emit("pe", lambda h: h.matmul(pa[:, 0:tn], lhsT=w[:, kc, 0:128], rhs=hT[:, kc, t0:t0 + tn], start=(kc == 0), stop=(kc == 7)),
                               reads=[winB[wi], hTB], writes=[pab])
                    bg = PF["b_pw1"] + 8 + c
                    ba = PF["b_pw1"] + c
                    P.emit("act", lambda h: h.activation(out=sig[:, t0:t0 + tn], in_=pg[:, 0:tn], func=AF.Sigmoid, bias=pf[:, bg:bg + 1], scale=1.0),
                           reads=[pgb, pfB], writes=[sigB])
                    P.emit("dve", lambda h: h.scalar_tensor_tensor(out=gpad[:, PADW + t0:PADW + t0 + tn], in0=pa[:, 0:tn], scalar=pf[:, ba:ba + 1], in1=sig[:, t0:t0 + tn],
                                                                   op0=ALU.add, op1=ALU.mult), reads=[pab, pfB, sigB], writes=[gpB])
                dw = PF["dw_w"]
                db = PF["dw_b"] + c
                P.emit("dve", lambda h: h.tensor_scalar(out=CV[:, c, :], in0=gpad[:, 0:L], scalar1=pf[:, dw + c:dw + c + 1], scalar2=pf[:, db:db + 1],
                                                        op0=ALU.mult, op1=ALU.add), reads=[gpB, pfB], writes=[CVB[c]])
                for k in range(1, KW):
                    P.emit("dve", lambda h: h.scalar_tensor_tensor(out=CV[:, c, :], in0=gpad[:, k:k + L], scalar=pf[:, dw + k * 8 + c:dw + k * 8 + c + 1],
                                                                   in1=CV[:, c, :], op0=ALU.mult, op1=ALU.add), reads=[gpB, pfB, CVB[c]], writes=[CVB[c]])
            Yc, YcB = hT, hTB
            for (t0, tn) in pcs:
                pm, pmb = pp.get()
                pq, pqb = pp.get()
                for c in range(8):
                    P.emit("pe", lambda h: h.matmul(pm[:, 0:tn], lhsT=ones[:], rhs=CV[:, c, t0:t0 + tn], start=(c == 0), stop=(c == 7)),
                           reads=[onesB, CVB[c]], writes=[pmb])
                for c in range(8):
                    si = sqc % 2
                    sqc += 1
                    P.emit("act", lambda h: h.activation(out=sq[si][:, 0:tn], in_=CV[:, c, t0:t0 + tn], func=AF.Square), reads=[CVB[c]], writes=[sqB[si]])
                    P.emit("pe", lambda h: h.matmul(pq[:, 0:tn], lhsT=ones[:], rhs=sq[si][:, 0:tn], start=(c == 0), stop=(c == 7)),
                           reads=[onesB, sqB[si]], writes=[pqb])
                P.emit("act", lambda h: h.activation(out=mean[:, 0:tn], in_=pm[:, 0:tn], func=AF.Copy), reads=[pmb], writes=[meanB])
                P.emit("dve", lambda h: h.tensor_tensor(out=rstd[:, 0:tn], in0=mean[:, 0:tn], in1=mean[:, 0:tn], op=ALU.mult), reads=[meanB], writes=[rstdB])
                P.emit("dve", lambda h: h.tensor_tensor(out=rstd[:, 0:tn], in0=pq[:, 0:tn], in1=rstd[:, 0:tn], op=ALU.subtract), reads=[pqb, rstdB], writes=[rstdB])
                P.emit("act", lambda h: h.activation(out=rstd[:, 0:tn], in_=rstd[:, 0:tn], func=AF.Sqrt, bias=lnt["eps"][:, :], scale=1.0), reads=[rstdB], writes=[rstdB])
                P.emit("dve", lambda h: h.reciprocal(out=rstd[:, 0:tn], in_=rstd[:, 0:tn]), reads=[rstdB], writes=[rstdB])
                for c in range(8):
                    ti = c % 2
                    P.emit("dve", lambda h: h.tensor_tensor(out=tq[ti][:, 0:tn], in0=CV[:, c, t0:t0 + tn], in1=mean[:, 0:tn], op=ALU.subtract),
                           reads=[CVB[c], meanB], writes=[tqB[ti]])
                    P.emit("pool", lambda h: h.tensor_tensor(out=tq[ti][:, 0:tn], in0=tq[ti][:, 0:tn], in1=rstd[:, 0:tn], op=ALU.mult),
                           reads=[tqB[ti], rstdB], writes=[tqB[ti]])
                    gcol = PF["cln_g"] + c
                    bcol = PF["cln_b"] + c
                    P.emit("act", lambda h: h.activation(out=Yc[:, c, t0:t0 + tn], in_=tq[ti][:, 0:tn], func=AF.Silu, bias=pf[:, bcol:bcol + 1], scale=pf[:, gcol:gcol + 1]),
                           reads=[tqB[ti], pfB], writes=[YcB])
            for ti, (p0, n) in enumerate(pos_tiles()):
                (xt_, xtB_, sem_x_, z_, zB_, zo_, zoB_, sem_o_, lnt_) = LNB[ti % 2]
                loader(xt_, xtB_, sem_x_, n, p0)
                for half in range(2):
                    pt, pb = pp.get()
                    for kc in range(8):
                        P.emit("pe", lambda h: h.matmul(pt[0:n, :], lhsT=Yc[:, kc, p0:p0 + n], rhs=pw2[:, kc, half * 512:(half + 1) * 512], start=(kc == 0), stop=(kc == 7)),
                               reads=[YcB, pw2B], writes=[pb])
                    P.emit("dve", lambda h: h.scalar_tensor_tensor(out=z_[0:n, half * 512:(half + 1) * 512], in0=xt_[0:n, half * 512:(half + 1) * 512],
                                                                   scalar=ALPHA, in1=pt[0:n, :], op0=ALU.mult, op1=ALU.add), reads=[xtB_, pb], writes=[zB_])
                P.emit("pool", lambda h: h.tensor_tensor(out=z_[0:n, :], in0=z_[0:n, :], in1=gt[0:n, 2, :], op=ALU.add), reads=[zB_, gtB], writes=[zB_])
                layer_norm_tile(P, C, gtB, z_, zB_, n, gt[:, 0, :], gt[:, 1, :], zo_, zoB_, lnt_)
                r0 = s * L + p0
                P.emit("sp", lambda h: h.dma_start(out=Hout[r0:r0 + n, :], in_=zo_[0:n, :]), reads=[zoB_], writes=[HoutB], dsem=sem_o_)
        P.wait_all("sp", [(sem_o, sem_o.count), (sem_o2, sem_o2.count)])
        P.flush_block([HinB, HoutB])
```

```python
import contextlib
import types
import numpy as np
import concourse.bass as bass
import concourse.mybir as mybir
from concourse.bass_utils import run_bass_kernel_spmd

F32 = mybir.dt.float32
BF16 = mybir.dt.bfloat16
ALU = mybir.AluOpType
AF = mybir.ActivationFunctionType
AX = mybir.AxisListType

D = 1024
SEQ = 2048
NMETA = 16
L = SEQ + NMETA
NCORES = 8
ALPHA = float(4.0 ** 0.25)
EPS = 1e-5
NKEY = 128
NEXP = NKEY * NKEY
GELU_K = 1.5957691216057308

PF = {}
_c = 0
for _n, _w in (("conv_w", 32), ("conv_b", 8), ("b_a", 16), ("b_x", 16), ("lam", 16), ("b_pw1", 16),
               ("dw_w", 248), ("dw_b", 8), ("cln_g", 8), ("cln_b", 8)):
    PF[_n] = _c
    _c += _w
PF_COLS = _c
PT = {"mix_g0": 0, "mix_b0": 1, "ffn_g0": 2, "ffn_b0": 3, "mix_g1": 4, "mix_b1": 5, "ffn_g1": 6, "ffn_b1": 7, "b_pw2": 8}


def freeze(fn):
    if fn.__closure__ is None:
        return fn
    cells = []
    for c in fn.__closure__:
        try:
            cells.append(types.CellType(c.cell_contents))
        except ValueError:
            cells.append(c)
    g = types.FunctionType(fn.__code__, fn.__globals__, fn.__name__, fn.__defaults__, tuple(cells))
    g.__kwdefaults__ = fn.__kwdefaults__
    return g


class Buf:
    __slots__ = ("name", "w", "r")

    def __init__(self, name=""):
        self.name = name
        self.w = None
        self.r = []


class DSem:
    __slots__ = ("h", "count")

    def __init__(self, h):
        self.h = h
        self.count = 0


class Eng:
    def __init__(self, name, sem):
        self.name = name
        self.sem = sem
        self.ops = []
        self.seen = {}


class Prog:
    def __init__(self, nc, stack):
        self.nc = nc
        self.stack = stack
        self.engs = {}
        self.nblk = 0
        for n in ("pe", "act", "dve", "pool", "sp"):
            self.engs[n] = Eng(n, None)
        self._new_sems()
        self.nops = 0

    def _new_sems(self):
        for n, e in self.engs.items():
            e.sem = DSem(self.stack.enter_context(self.nc.semaphore("s_%s_%d" % (n, self.nblk))))
            e.seen = {}

    def dsem(self, name):
        return DSem(self.stack.enter_context(self.nc.semaphore(name)))

    def emit(self, eng, fn, reads=(), writes=(), dsem=None):
        e = self.engs[eng]
        deps = {}

        def dep(sig):
            s, v = sig
            if deps.get(s, 0) < v:
                deps[s] = v

        for b in reads:
            if b.w is not None:
                dep(b.w)
        for b in writes:
            if b.w is not None:
                dep(b.w)
            for r in b.r:
                dep(r)
        if dsem is not None and dsem.count > 0:
            dep((dsem, dsem.count))
        waits = []
        for s, v in deps.items():
            if eng == "pe" and s is e.sem:
                continue
            if e.seen.get(s, 0) < v:
                e.seen[s] = v
                waits.append((s.h, v))
        if dsem is not None:
            dsem.count += 16
            sig = (dsem, dsem.count)
            inc = 16
        else:
            e.sem.count += 1
            sig = (e.sem, e.sem.count)
            inc = 1
        for b in reads:
            b.r.append(sig)
        for b in writes:
            b.w = sig
            b.r = []
        e.ops.append((waits, freeze(fn), sig[0].h, inc))
        self.nops += 1
        return sig

    def wait_all(self, eng, sigs):
        e = self.engs[eng]
        for s, v in sigs:
            if e.seen.get(s, 0) < v:
                e.seen[s] = v
                e.ops.append(([(s.h, v)], None, None, 0))

    def flush_block(self, bufs=()):
        nc = self.nc
        engs = self.engs

        def replay(e, h):
            for waits, fn, sh, inc in e.ops:
                for s, v in waits:
                    h.wait_ge(s, v)
                if fn is not None:
                    fn(h).then_inc(sh, inc)
            e.ops = []

        with nc.Block() as block:
            @block.tensor
            def _(h):
                replay(engs["pe"], h)

            @block.scalar
            def _(h):
                replay(engs["act"], h)

            @block.vector
            def _(h):
                replay(engs["dve"], h)

            @block.gpsimd
            def _(h):
                replay(engs["pool"], h)

            @block.sync
            def _(h):
                replay(engs["sp"], h)
        self.nblk += 1
        self._new_sems()
        for b in bufs:
            b.w = None
            b.r = []


class Ctx:
    pass


def pieces(n, step=512):
    return [(s, min(step, n - s)) for s in range(0, n, step)]


def pos_tiles():
    return [(s, min(128, L - s)) for s in range(0, L, 128)]


class PsumPool:
    def __init__(self, nc, st, n, shape, dtype, name):
        self.t = [st.enter_context(nc.psum_tensor("%s%d" % (name, i), shape, dtype)) for i in range(n)]
        self.b = [Buf("%s%d" % (name, i)) for i in range(n)]
        self.i = 0

    def get(self):
        i = self.i
        self.i = (i + 1) % len(self.t)
        return self.t[i], self.b[i]


def load_seq_tile(P, C, eng, dst, dbuf, dsem, s, p0, n):
    sigs = []
    if p0 < NMETA:
        m = min(NMETA - p0, n)
        P.emit(eng, lambda h: h.dma_start(out=dst[0:m, :], in_=C.meta[p0:p0 + m, :]), writes=[dbuf], dsem=dsem)
        if n > m:
            P.emit(eng, lambda h: h.dma_start(out=dst[m:n, :], in_=C.x[s, 0:n - m, :]), writes=[dbuf], dsem=dsem)
    else:
        P.emit(eng, lambda h: h.dma_start(out=dst[0:n, :], in_=C.x[s, p0 - NMETA:p0 - NMETA + n, :]), writes=[dbuf], dsem=dsem)


def layer_norm_tile(P, C, gbB, z, zb, n, g_ap, b_ap, out, outb, tmp):
    stats, sb = tmp["stats"], tmp["statsb"]
    for k in range(2):
        P.emit("dve", lambda h, k=k: h.bn_stats(out=stats[0:n, k, :], in_=z[0:n, k * 512:(k + 1) * 512]), reads=[zb], writes=[sb])
    mv, mvb = tmp["mv"], tmp["mvb"]
    P.emit("dve", lambda h: h.bn_aggr(out=mv[0:n, :], in_=stats[0:n, :, :]), reads=[sb], writes=[mvb])
    rs, rsb = tmp["rs"], tmp["rsb"]
    P.emit("act", lambda h: h.activation(out=rs[0:n, :], in_=mv[0:n, 1:2], func=AF.Sqrt, bias=tmp["eps"][0:n, :], scale=1.0), reads=[mvb], writes=[rsb])
    P.emit("dve", lambda h: h.reciprocal(out=rs[0:n, :], in_=rs[0:n, :]), reads=[rsb], writes=[rsb])
    P.emit("dve", lambda h: h.tensor_scalar(out=out[0:n, :], in0=z[0:n, :], scalar1=mv[0:n, 0:1], scalar2=rs[0:n, 0:1],
                                            op0=ALU.subtract, op1=ALU.mult), reads=[zb, mvb, rsb], writes=[outb])
    P.emit("pool", lambda h: h.tensor_tensor(out=out[0:n, :], in0=out[0:n, :], in1=g_ap[0:n, :], op=ALU.mult), reads=[outb, gbB], writes=[outb])
    P.emit("pool", lambda h: h.tensor_tensor(out=out[0:n, :], in0=out[0:n, :], in1=b_ap[0:n, :], op=ALU.add), reads=[outb, gbB], writes=[outb])


def alloc_ln_tmp(nc, st, P):
    t = {}
    t["stats"] = st.enter_context(nc.sbuf_tensor("ln_stats", [128, 2, 6], F32))
    t["statsb"] = Buf("ln_stats")
    t["mv"] = st.enter_context(nc.sbuf_tensor("ln_mv", [128, 2], F32))
    t["mvb"] = Buf("ln_mv")
    t["rs"] = st.enter_context(nc.sbuf_tensor("ln_rs", [128, 1], F32))
    t["rsb"] = Buf("ln_rs")
    t["eps"] = st.enter_context(nc.sbuf_tensor("ln_eps", [128, 1], F32))
    P.emit("pool", lambda h: h.memset(t["eps"][:], EPS), writes=[Buf()])
    return t


def make_ident(P, nc, st):
    idf = st.enter_context(nc.sbuf_tensor("identf", [128, 128], F32))
    idb = st.enter_context(nc.sbuf_tensor("identb", [128, 128], BF16))
    B = Buf("ident")
    P.emit("pool", lambda h: h.memset(idf[:], 1.0), writes=[B])
    P.emit("pool", lambda h: h.affine_select(out=idf[:], in_=idf[:], pattern=[[-1, 128]], base=0, channel_multiplier=1,
                                              compare_op=ALU.is_equal, fill=0.0), reads=[B], writes=[B])
    P.emit("pool", lambda h: h.tensor_copy(out=idb[:], in_=idf[:]), reads=[B], writes=[B])
    return idf, idb, B


def load_tokens_T(P, C, nc, hT, hTb, xt, xtb, xsem, idf, idB, pp, loader, tiles):
    for (c0, n, args) in tiles:
        loader(xt, xtb, xsem, n, *args)
        for half in range(2):
            pt, pb = pp.get()
            for j in range(4):
                kc = half * 4 + j
                P.emit("pe", lambda h, kc=kc, j=j, pt=pt, n=n: h.transpose(out=pt[:, j * 128:j * 128 + n], in_=xt[0:n, kc * 128:(kc + 1) * 128],
                                                                         identity=idf[0:n, 0:n]), reads=[xtb, idB], writes=[pb])
            e = "act" if half == 0 else "dve"
            if e == "act":
                P.emit("act", lambda h, half=half, pt=pt, n=n, c0=c0: h.activation(
                    out=hT[:, half * 4:half * 4 + 4, c0:c0 + n], in_=pt[:, :].rearrange("p (j t) -> p j t", j=4)[:, :, 0:n], func=AF.Copy),
                    reads=[pb], writes=[hTb])
            else:
                P.emit("dve", lambda h, half=half, pt=pt, n=n, c0=c0: h.tensor_copy(
                    out=hT[:, half * 4:half * 4 + 4, c0:c0 + n], in_=pt[:, :].rearrange("p (j t) -> p j t", j=4)[:, :, 0:n]),
                    reads=[pb], writes=[hTb])


def phase_rglru(P, C, nc, nseq):
    with contextlib.ExitStack() as st:
        sb = lambda name, shape, dt: st.enter_context(nc.sbuf_tensor("a_" + name, shape, dt))
        idf, idb, idB = make_ident(P, nc, st)
        lnt = alloc_ln_tmp(nc, st, P)
        pp = PsumPool(nc, st, 8, [128, 512], F32, "psA")
        pf = sb("pf", [128, PF_COLS], F32)
        pfB = Buf("pf")
        sem_c = P.dsem("semc_a")
        P.emit("sp", lambda h: h.dma_start(out=pf[:], in_=C.pf), writes=[pfB], dsem=sem_c)
        wout = sb("wout", [128, 8, D], BF16)
        woutB = Buf("wout")
        sem_wo = P.dsem("sem_wo")
        for kc in range(8):
            P.emit("pool", lambda h, kc=kc: h.dma_start(out=wout[:, kc, :], in_=C.w_out[kc * 128:(kc + 1) * 128, :]), writes=[woutB], dsem=sem_wo)
        wga = sb("wga", [128, 16, 128], BF16)
        wgx = sb("wgx", [128, 16, 128], BF16)
        wgB = Buf("wg")
        sem_wg = P.dsem("sem_wg")
        P.emit("pool", lambda h: h.dma_start(out=wga[:], in_=C.w_a.rearrange("r n c d -> c (r n) d")), writes=[wgB], dsem=sem_wg)
        P.emit("pool", lambda h: h.dma_start(out=wgx[:], in_=C.w_x.rearrange("r n c d -> c (r n) d")), writes=[wgB], dsem=sem_wg)
        gt = sb("lng", [128, 2, D], F32)
        gtB = Buf("lng")
        sem_g = P.dsem("sem_lng")
        P.emit("sp", lambda h: h.dma_start(out=gt[:, 0, :], in_=C.pt[PT["mix_g0"]:PT["mix_g0"] + 1, :].partition_broadcast(128)), writes=[gtB], dsem=sem_g)
        P.emit("sp", lambda h: h.dma_start(out=gt[:, 1, :], in_=C.pt[PT["mix_b0"]:PT["mix_b0"] + 1, :].partition_broadcast(128)), writes=[gtB], dsem=sem_g)
        cl = sb("cl", [128, 16], F32)
        clB = Buf("cl")
        lam = pf[:, PF["lam"]:PF["lam"] + 16]
        P.emit("act", lambda h: h.activation(out=cl[:], in_=lam, func=AF.Exp, scale=-1.0), reads=[pfB], writes=[clB])
        P.emit("act", lambda h: h.activation(out=cl[:], in_=cl[:], func=AF.Ln, bias=1.0, scale=1.0), reads=[clB], writes=[clB])
        P.emit("dve", lambda h: h.tensor_scalar(out=cl[:], in0=cl[:], scalar1=-8.0, scalar2=None, op0=ALU.mult), reads=[clB], writes=[clB])

        hT = sb("hT", [128, 8, L], BF16)
        hTB = Buf("hT")
        Y = sb("Y", [128, 8, L], BF16)
        YB = Buf("Y")
        xt = sb("xt", [128, D], F32)
        xtB = Buf("xt")
        sem_x = P.dsem("sem_x")
        win = [sb("win%d" % i, [128, 8, 256], BF16) for i in range(2)]
        winB = [Buf("win%d" % i) for i in range(2)]
        sem_win = [P.dsem("sem_win%d" % i) for i in range(2)]
        gg = sb("gg", [128, L], F32); ggB = Buf("gg")
        upad = sb("upad", [128, L + 3], F32); upB = Buf("upad")
        uc = sb("uc", [128, L], F32); ucB = Buf("uc")
        ucb = sb("ucb", [128, L], BF16); ucbB = Buf("ucb")
        ab = sb("ab", [128, L], F32); abB = Buf("ab")
        bb = sb("bb", [128, L], F32); bbB = Buf("bb")
        tm = sb("tm", [128, L], F32); tmB = Buf("tm")
        hf = sb("hf", [128, L], F32); hfB = Buf("hf")
        z = sb("z", [128, D], F32); zB = Buf("z")
        zo = sb("zo", [128, D], F32); zoB = Buf("zo")
        sem_o = P.dsem("sem_oa")
        xt2 = sb("xt2", [128, D], F32); xt2B = Buf("xt2"); sem_x2 = P.dsem("sem_x2")
        z2 = sb("z2", [128, D], F32); z2B = Buf("z2")
        zo2 = sb("zo2", [128, D], F32); zo2B = Buf("zo2")
        sem_o2 = P.dsem("sem_oa2")
        lnt2 = alloc_ln_tmp_named(nc, st, P, "a2_")
        LNB = [(xt, xtB, sem_x, z, zB, zo, zoB, sem_o, lnt), (xt2, xt2B, sem_x2, z2, z2B, zo2, zo2B, sem_o2, lnt2)]
        P.emit("pool", lambda h: h.memset(upad[:], 0.0), writes=[upB])
        pcs = pieces(L)
        wcount = 0
        for s in range(nseq):
            def loader(xt_, xtb_, sem_, n, p0, s=s):
                load_seq_tile(P, C, "sp", xt_, xtb_, sem_, s, p0, n)
            load_tokens_T(P, C, nc, hT, hTB, xt, xtB, sem_x, idf, idB, pp, loader, [(p0, n, (p0,)) for (p0, n) in pos_tiles()])
            for c in range(8):
                wi = wcount % 2
                wcount += 1
                w = win[wi]
                P.emit("pool", lambda h, w=w, c=c: h.dma_start(out=w[:, :, 0:128], in_=C.w_in[:, c * 128:(c + 1) * 128].rearrange("(kc p) n -> p kc n", p=128)),
                       writes=[winB[wi]], dsem=sem_win[wi])
                P.emit("pool", lambda h, w=w, c=c: h.dma_start(out=w[:, :, 128:256], in_=C.w_in[:, D + c * 128:D + (c + 1) * 128].rearrange("(kc p) n -> p kc n", p=128)),
                       writes=[winB[wi]], dsem=sem_win[wi])
                for (t0, tn) in pcs:
                    pt, pb = pp.get()
                    for kc in range(8):
                        P.emit("pe", lambda h, pt=pt, kc=kc, t0=t0, tn=tn, w=w: h.matmul(pt[:, 0:tn], lhsT=w[:, kc, 0:128], rhs=hT[:, kc, t0:t0 + tn],
                                                                                      start=(kc == 0), stop=(kc == 7)), reads=[winB[wi], hTB], writes=[pb])
                    P.emit("act", lambda h, pt=pt, t0=t0, tn=tn: h.activation(out=gg[:, t0:t0 + tn], in_=pt[:, 0:tn], func=AF.Gelu_apprx_tanh),
                           reads=[pb], writes=[ggB])
                for (t0, tn) in pcs:
                    pt, pb = pp.get()
                    for kc in range(8):
                        P.emit("pe", lambda h, pt=pt, kc=kc, t0=t0, tn=tn, w=w: h.matmul(pt[:, 0:tn], lhsT=w[:, kc, 128:256], rhs=hT[:, kc, t0:t0 + tn],
                                                                                      start=(kc == 0), stop=(kc == 7)), reads=[winB[wi], hTB], writes=[pb])
                    P.emit("act", lambda h, pt=pt, t0=t0, tn=tn: h.activation(out=upad[:, 2 + t0:2 + t0 + tn], in_=pt[:, 0:tn], func=AF.Copy),
                           reads=[pb], writes=[upB])
                cw = PF["conv_w"]
                P.emit("dve", lambda h, c=c: h.tensor_scalar(out=uc[:], in0=upad[:, 0:L], scalar1=pf[:, cw + c:cw + c + 1],
                                                             scalar2=pf[:, PF["conv_b"] + c:PF["conv_b"] + c + 1], op0=ALU.mult, op1=ALU.add),
                       reads=[upB, pfB], writes=[ucB])
                for k in range(1, 4):
                    P.emit("dve", lambda h, c=c, k=k: h.scalar_tensor_tensor(out=uc[:], in0=upad[:, k:k + L], scalar=pf[:, cw + k * 8 + c:cw + k * 8 + c + 1],
                                                                            in1=uc[:], op0=ALU.mult, op1=ALU.add), reads=[upB, pfB, ucB], writes=[ucB])
                P.emit("pool", lambda h: h.tensor_copy(out=ucb[:], in_=uc[:]), reads=[ucB], writes=[ucbB])
                for r in range(2):
                    gi = r * 8 + c
                    for (t0, tn) in pcs:
                        pt, pb = pp.get()
                        P.emit("pe", lambda h, pt=pt, t0=t0, tn=tn, gi=gi: h.matmul(pt[:, 0:tn], lhsT=wga[:, gi, :], rhs=ucb[:, t0:t0 + tn], start=True, stop=True),
                               reads=[wgB, ucbB], writes=[pb])
                        P.emit("act", lambda h, pt=pt, t0=t0, tn=tn, gi=gi: h.activation(out=ab[:, t0:t0 + tn], in_=pt[:, 0:tn], func=AF.Sigmoid,
                                                                                       bias=pf[:, PF["b_a"] + gi:PF["b_a"] + gi + 1], scale=1.0), reads=[pb, pfB], writes=[abB])
                    for (t0, tn) in pcs:
                        pt, pb = pp.get()
                        P.emit("pe", lambda h, pt=pt, t0=t0, tn=tn, gi=gi: h.matmul(pt[:, 0:tn], lhsT=wgx[:, gi, :], rhs=ucb[:, t0:t0 + tn], start=True, stop=True),
                               reads=[wgB, ucbB], writes=[pb])
                        P.emit("act", lambda h, pt=pt, t0=t0, tn=tn, gi=gi: h.activation(out=bb[:, t0:t0 + tn], in_=pt[:, 0:tn], func=AF.Sigmoid,
                                                                                       bias=pf[:, PF["b_x"] + gi:PF["b_x"] + gi + 1], scale=1.0), reads=[pb, pfB], writes=[bbB])
                    P.emit("act", lambda h, gi=gi: h.activation(out=ab[:], in_=ab[:], func=AF.Exp, scale=cl[:, gi:gi + 1]), reads=[abB, clB], writes=[abB])
                    P.emit("pool", lambda h: h.tensor_tensor(out=bb[:], in0=bb[:], in1=uc[:], op=ALU.mult), reads=[bbB, ucB], writes=[bbB])
                    P.emit("act", lambda h: h.activation(out=tm[:], in_=ab[:], func=AF.Square), reads=[abB], writes=[tmB])
                    P.emit("act", lambda h: h.activation(out=tm[:], in_=tm[:], func=AF.Sqrt, bias=1.0, scale=-1.0), reads=[tmB], writes=[tmB])
                    P.emit("pool", lambda h: h.tensor_tensor(out=bb[:], in0=bb[:], in1=tm[:], op=ALU.mult), reads=[bbB, tmB], writes=[bbB])
                    if r == 0:
                        P.emit("dve", lambda h: h.tensor_tensor_scan(out=hf[:], data0=ab[:], data1=bb[:], initial=0.0, op0=ALU.mult, op1=ALU.add),
                               reads=[abB, bbB], writes=[hfB])
                    else:
                        P.emit("dve", lambda h: h.tensor_tensor_scan(out=tm[:, ::-1], data0=ab[:, ::-1], data1=bb[:, ::-1], initial=0.0, op0=ALU.mult, op1=ALU.add),
                               reads=[abB, bbB], writes=[tmB])
                P.emit("pool", lambda h: h.tensor_tensor(out=hf[:], in0=hf[:], in1=tm[:], op=ALU.add), reads=[hfB, tmB], writes=[hfB])
                P.emit("dve", lambda h, c=c: h.tensor_tensor(out=Y[:, c, :], in0=hf[:], in1=gg[:], op=ALU.mult), reads=[hfB, ggB], writes=[YB])
            for ti, (p0, n) in enumerate(pos_tiles()):
                (xt_, xtB_, sem_x_, z_, zB_, zo_, zoB_, sem_o_, lnt_) = LNB[ti % 2]
                loader(xt_, xtB_, sem_x_, n, p0)
                for half in range(2):
                    pt, pb = pp.get()
                    for kc in range(8):
                        P.emit("pe", lambda h: h.matmul(pt[0:n, :], lhsT=Y[:, kc, p0:p0 + n], rhs=wout[:, kc, half * 512:(half + 1) * 512],
                                                        start=(kc == 0), stop=(kc == 7)), reads=[YB, woutB], writes=[pb])
                    P.emit("dve", lambda h: h.scalar_tensor_tensor(out=z_[0:n, half * 512:(half + 1) * 512], in0=xt_[0:n, half * 512:(half + 1) * 512],
                                                                   scalar=ALPHA, in1=pt[0:n, :], op0=ALU.mult, op1=ALU.add),
                           reads=[xtB_, pb], writes=[zB_])
                layer_norm_tile(P, C, gtB, z_, zB_, n, gt[:, 0, :], gt[:, 1, :], zo_, zoB_, lnt_)
                r0 = s * L + p0
                P.emit("sp", lambda h: h.dma_start(out=C.H1[r0:r0 + n, :], in_=zo_[0:n, :]), reads=[zoB_], writes=[C.H1B], dsem=sem_o_)
        P.wait_all("sp", [(sem_o, sem_o.count), (sem_o2, sem_o2.count)])
        P.flush_block([C.H1B])


def build_program(nseq, stop_after=None):
    nc = bass.Bass("TRN2", target_bir_lowering=False)
    C = Ctx()
    T = nseq * L
    di = lambda name, shape: nc.dram_tensor(name, shape, F32, kind="ExternalInput").ap()
    C.x = di("x", [nseq, SEQ, D])
    C.meta = di("meta", [NMETA, D])
    C.w_in = di("w_in", [D, 2 * D])
    C.w_a = di("w_a", [2, 8, 128, 128])
    C.w_x = di("w_x", [2, 8, 128, 128])
    C.w_out = di("w_out", [D, D])
    C.pw1 = di("pw1", [D, 2 * D])
    C.pw2 = di("pw2", [D, D])
    C.wq = di("wq", [2, D, 2 * D])
    C.kt = di("kt", [2, 2, 128, 128])
    C.ut = di("ut", [2, D, NEXP])
    C.v = di("v", [2, NEXP, D])
    C.pf = di("pf", [128, PF_COLS])
    C.pt = di("pt", [len(PT), D])
    dbg = stop_after is not None
    mk = lambda name, shape, out: nc.dram_tensor(name, shape, F32, kind=("ExternalOutput" if out else "Internal")).ap()
    C.H1 = mk("H1", [T, D], dbg and stop_after == 1)
    C.H1B = Buf("H1")
    C.H2 = mk("H2", [T, D], dbg and stop_after == 2)
    C.H2B = Buf("H2")
    C.H3 = mk("H3", [T, D], dbg and stop_after == 3)
    C.H3B = Buf("H3")
    C.out = nc.dram_tensor("out", [nseq, SEQ, D], F32, kind="ExternalOutput").ap() if (stop_after is None or stop_after == 4) else None
    C.outB = Buf("out")
    with contextlib.ExitStack() as st:
        P = Prog(nc, st)
        phase_rglru(P, C, nc, nseq)
        if stop_after == 1:
            return nc
        phase_peer(P, C, nc, T, 0, C.H1, C.H1B, C.H2, C.H2B, False, "b_")
        if stop_after == 2:
            return nc
        phase_conf(P, C, nc, nseq, C.H2, C.H2B, C.H3, C.H3B, "c_")
        if stop_after == 3:
            return nc
        phase_peer(P, C, nc, T, 1, C.H3, C.H3B, C.out, C.outB, True, "d_")
    return nc


def pack_fm(v):
    v = np.asarray(v, np.float32).reshape(-1, 128)
    return np.ascontiguousarray(v.T)


def make_shared_inputs(inp):
    f = lambda a: np.ascontiguousarray(np.asarray(a, np.float32))
    pfm = np.zeros((128, PF_COLS), np.float32)
    cw = inp["lru_conv_w"][0]
    for k in range(4):
        pfm[:, PF["conv_w"] + k * 8:PF["conv_w"] + k * 8 + 8] = pack_fm(cw[k])
    pfm[:, PF["conv_b"]:PF["conv_b"] + 8] = pack_fm(inp["lru_conv_b"][0])
    pfm[:, PF["b_a"]:PF["b_a"] + 16] = pack_fm(inp["lru_b_a"][0].reshape(-1))
    pfm[:, PF["b_x"]:PF["b_x"] + 16] = pack_fm(inp["lru_b_x"][0].reshape(-1))
    pfm[:, PF["lam"]:PF["lam"] + 16] = pack_fm(inp["lru_lambda"][0].reshape(-1))
    pfm[:, PF["b_pw1"]:PF["b_pw1"] + 16] = pack_fm(inp["conf_b_pw1"][0])
    dw = inp["conf_dw_w"][0]
    for k in range(31):
        pfm[:, PF["dw_w"] + k * 8:PF["dw_w"] + k * 8 + 8] = pack_fm(dw[k])
    pfm[:, PF["dw_b"]:PF["dw_b"] + 8] = pack_fm(inp["conf_dw_b"][0])
    pfm[:, PF["cln_g"]:PF["cln_g"] + 8] = pack_fm(inp["conf_ln_g"][0])
    pfm[:, PF["cln_b"]:PF["cln_b"] + 8] = pack_fm(inp["conf_ln_b"][0])
    ptm = np.zeros((len(PT), D), np.float32)
    for i in range(2):
        ptm[PT["mix_g%d" % i]] = inp["ln_mix_g"][i]
        ptm[PT["mix_b%d" % i]] = inp["ln_mix_b"][i]
        ptm[PT["ffn_g%d" % i]] = inp["ln_ffn_g"][i]
        ptm[PT["ffn_b%d" % i]] = inp["ln_ffn_b"][i]
    ptm[PT["b_pw2"]] = inp["conf_b_pw2"][0]
    sh = {
        "meta": f(inp["meta_tokens"]),
        "w_in": f(inp["lru_w_in"][0]),
        "w_a": f(inp["lru_w_a"][0]),
        "w_x": f(inp["lru_w_x"][0]),
        "w_out": f(inp["lru_w_out"][0]),
        "pw1": f(inp["conf_w_pw1"][0]),
        "pw2": f(inp["conf_w_pw2"][0]),
        "wq": f(inp["peer_w_query"]),
        "kt": f(np.transpose(np.asarray(inp["peer_sub_keys"], np.float32), (0, 1, 3, 2))),
        "ut": f(np.transpose(np.asarray(inp["peer_u"], np.float32), (0, 2, 1))),
        "v": f(inp["peer_v"]),
        "pf": pfm,
        "pt": ptm,
    }
    return sh


def kernel(**inputs):
    x = np.asarray(inputs["x"], np.float32)
    nseq = x.shape[0] // NCORES
    sh = make_shared_inputs(inputs)
    nc = build_program(nseq)
    in_maps = []
    for c in range(NCORES):
        m = dict(sh)
        m["x"] = np.ascontiguousarray(x[c * nseq:(c + 1) * nseq])
        in_maps.append(m)
    res = run_bass_kernel_spmd(nc, in_maps, core_ids=list(range(NCORES)))
    return np.concatenate([r["out"] for r in res.results], axis=0)


def out_segments(r0, n):
    segs = []
    r = r0
    while r < r0 + n:
        s, pos = divmod(r, L)
        if pos < NMETA:
            r += min(NMETA - pos, r0 + n - r)
            continue
        cnt = min(L - pos, r0 + n - r)
        segs.append((r - r0, cnt, s, pos - NMETA))
        r += cnt
    return segs


def phase_peer(P, C, nc, T, layer, Hin, HinB, Hout, HoutB, final, pfx):
    GN = 4
    NG = NKEY // GN
    GE = GN * NKEY
    with contextlib.ExitStack() as st:
        sb = lambda name, shape, dt: st.enter_context(nc.sbuf_tensor(pfx + name, shape, dt))
        idf, idb, idB = make_ident_named(P, nc, st, pfx)
        lnt = alloc_ln_tmp_named(nc, st, P, pfx)
        ppA = PsumPool(nc, st, 2, [128, 512], F32, pfx + "psA")
        _ptt = st.enter_context(nc.psum_tensor(pfx + "psT", [128, 2, 4, 128], BF16))
        ppT = PsumPool.__new__(PsumPool); ppT.t = [_ptt[:, 0], _ptt[:, 1]]; _pTB = Buf("psT"); ppT.b = [_pTB, _pTB]; ppT.i = 0
        ppO = PsumPool(nc, st, 2, [128, 1024], F32, pfx + "psO")
        s1g = st.enter_context(nc.psum_tensor(pfx + "s1g", [128, 2, 8, GN], F32)); _s1gB = Buf("s1g"); s1gB = [_s1gB, _s1gB]
        wq = sb("wq", [128, 8, 2 * D], BF16); wqB = Buf("wq")
        sem_wq = P.dsem(pfx + "sem_wq")
        for kc in range(8):
            P.emit("pool", lambda h, kc=kc: h.dma_start(out=wq[:, kc, :], in_=C.wq[layer, kc * 128:(kc + 1) * 128, :], max_dma_last_dim=4096), writes=[wqB], dsem=sem_wq)
        kt = sb("kt", [128, 2, 128], BF16); ktB = Buf("kt")
        sem_kt = P.dsem(pfx + "sem_kt")
        P.emit("pool", lambda h: h.dma_start(out=kt[:], in_=C.kt[layer].rearrange("p k n -> k p n")), writes=[ktB], dsem=sem_kt)
        gt = sb("lng", [128, 2, D], F32); gtB = Buf("lng")
        sem_g = P.dsem(pfx + "sem_lng")
        gi, bi = PT["ffn_g%d" % layer], PT["ffn_b%d" % layer]
        P.emit("sp", lambda h: h.dma_start(out=gt[:, 0, :], in_=C.pt[gi:gi + 1, :].partition_broadcast(128)), writes=[gtB], dsem=sem_g)
        P.emit("sp", lambda h: h.dma_start(out=gt[:, 1, :], in_=C.pt[bi:bi + 1, :].partition_broadcast(128)), writes=[gtB], dsem=sem_g)
        xt = sb("xt", [128, D], F32); xtB = Buf("xt"); sem_x = P.dsem(pfx + "sem_x")
        xtf = sb("xtf", [128, D], F32); xtfB = Buf("xtf"); sem_xf = P.dsem(pfx + "sem_xf")
        P.emit("pool", lambda h: h.memset(xtf[:], 0.0), writes=[xtfB])
        hT = sb("hT", [128, 8, 512], BF16); hTB = Buf("hT")
        P.emit("pool", lambda h: h.memset(hT[:], 0.0), writes=[hTB])
        P.emit("pool", lambda h: h.memset(xt[:], 0.0), writes=[xtB])
        qTs = [sb("qT%d" % i, [128, 512], BF16) for i in range(2)]; qTB = [Buf("qT%d" % i) for i in range(2)]
        S = sb("S", [128, 4, 16, 128], F32); SB = [[Buf("S%d_%d" % (i, j)) for j in range(16)] for i in range(4)]
        TAU = sb("TAU", [128, 4, 8], F32); TAUB = [Buf("TAU%d" % i) for i in range(4)]
        O = sb("O", [128, 4, D], F32); OB = [Buf("O%d" % i) for i in range(4)]
        T16 = sb("T16", [128, 4, 16, 16], F32); T16B = [Buf("T16_%d" % i) for i in range(4)]
        tmpS = sb("tmpS", [128, 2, 128], F32); tmpSB = [Buf("tmpS0"), Buf("tmpS1")]
        cand2 = sb("cand2", [128, 256], F32); cand2B = Buf("cand2")
        c24 = sb("c24", [128, 8, 24], F32); c24B = Buf("c24")
        e16 = sb("e16", [128, 8, 16], F32); e16B = Buf("e16")
        zs = sb("zs", [128, 8], F32); zsB = Buf("zs")
        mb = sb("mb", [128, 8], F32); mbB = Buf("mb")
        UT = [sb("UT%d" % i, [128, 8, GE], BF16) for i in range(2)]; UTB = [Buf("UT%d" % i) for i in range(2)]
        VG = [sb("VG%d" % i, [128, GN, D], BF16) for i in range(2)]; VGB = [Buf("VG%d" % i) for i in range(2)]
        sem_u = [P.dsem(pfx + "sem_u%d" % i) for i in range(2)]
        sem_v = [P.dsem(pfx + "sem_v%d" % i) for i in range(2)]
        G = [sb("G%d" % i, [128, 8, GN, 128], F32) for i in range(2)]; GB = [[Buf("G%d_%d" % (i, j)) for j in range(9)] for i in range(2)]
        NPR = 2
        cand = G[0][:, 0:4].rearrange("p h a n -> p (h a n)").rearrange("p (h c) -> p h c", h=8); candB = GB[0][0]
        E = [sb("E%d" % i, [128, 8, GN, 128], BF16) for i in range(2)]
        EB = [[Buf("E%d_%d" % (i, j)) for j in range(8)] for i in range(2)]
        A = [sb("A%d" % i, [128, GE], BF16) for i in range(2)]; AB = [Buf("A%d" % i) for i in range(2)]
        WA = [sb("WA%d" % i, [128, GE], BF16) for i in range(2)]; WAB = [Buf("WA%d" % i) for i in range(2)]
        WT = [sb("WT%d" % i, [128, GN, 128], BF16) for i in range(2)]; WTB = [Buf("WT%d" % i) for i in range(2)]
        z = sb("z", [128, D], F32); zB = Buf("z")
        zo, zoB = z, zB
        sem_o = P.dsem(pfx + "sem_o")

        tiles = [(r0, min(128, T - r0)) for r0 in range(0, T, 128)]
        cnt = 0
        for s0 in range(0, len(tiles), 4):
            tl = tiles[s0:s0 + 4]
            ntl = len(tl)
            ncol = ntl * 128

            def loader(xt_, xtb_, sem_, n, r0, eng="sp"):
                P.emit(eng, lambda h: h.dma_start(out=xt_[0:n, :], in_=Hin[r0:r0 + n, :]), reads=[HinB], writes=[xtb_], dsem=sem_)
            for i, (r0, n) in enumerate(tl):
                loader(xtf, xtfB, sem_xf, n, r0, "pool")
                for half in range(2):
                    pt, pb = ppA.get()
                    for j in range(4):
                        kc = half * 4 + j
                        P.emit("pe", lambda h, kc=kc, j=j, pt=pt: h.transpose(out=pt[:, j * 128:(j + 1) * 128], in_=xtf[:, kc * 128:(kc + 1) * 128], identity=idf[:]),
                               reads=[xtfB, idB], writes=[pb])
                    if half == 0:
                        P.emit("act", lambda h, pt=pt, i=i: h.activation(out=hT[:, 0:4, i * 128:(i + 1) * 128], in_=pt[:, :].rearrange("p (j t) -> p j t", j=4), func=AF.Copy),
                               reads=[pb], writes=[hTB])
                    else:
                        P.emit("dve", lambda h, pt=pt, i=i: h.tensor_copy(out=hT[:, 4:8, i * 128:(i + 1) * 128], in_=pt[:, :].rearrange("p (j t) -> p j t", j=4)),
                               reads=[pb], writes=[hTB])
            for j in range(16):
                pt, pb = ppA.get()
                for kc in range(8):
                    P.emit("pe", lambda h, pt=pt, kc=kc, j=j: h.matmul(pt[:, 0:ncol], lhsT=wq[:, kc, j * 128:(j + 1) * 128], rhs=hT[:, kc, 0:ncol],
                                                                     start=(kc == 0), stop=(kc == 7)), reads=[wqB, hTB], writes=[pb])
                q = qTs[j % 2]; qb = qTB[j % 2]
                if j % 2 == 0:
                    P.emit("act", lambda h, pt=pt, q=q: h.activation(out=q[:, 0:ncol], in_=pt[:, 0:ncol], func=AF.Copy), reads=[pb], writes=[qb])
                else:
                    P.emit("dve", lambda h, pt=pt, q=q: h.tensor_copy(out=q[:, 0:ncol], in_=pt[:, 0:ncol]), reads=[pb], writes=[qb])
                po, pob = ppO.get()
                for i in range(ntl):
                    P.emit("pe", lambda h, po=po, i=i, q=q, j=j: h.matmul(po[:, i * 128:(i + 1) * 128], lhsT=q[:, i * 128:(i + 1) * 128], rhs=kt[:, j % 2, :], start=True, stop=True),
                           reads=[qb, ktB], writes=[pob])
                eng = "act" if j % 2 == 1 else "dve"
                if eng == "act":
                    P.emit("act", lambda h, po=po, j=j: h.activation(out=S[:, 0:ntl, j, :], in_=po[:, 0:ncol].rearrange("p (t n) -> p t n", n=128), func=AF.Copy),
                           reads=[pob], writes=[SB[i][j] for i in range(ntl)])
                else:
                    P.emit("dve", lambda h, po=po, j=j: h.tensor_copy(out=S[:, 0:ntl, j, :], in_=po[:, 0:ncol].rearrange("p (t n) -> p t n", n=128)),
                           reads=[pob], writes=[SB[i][j] for i in range(ntl)])
                for i in range(ntl):
                    tb = (j * 4 + i) % 2
                    P.emit("dve", lambda h: h.max(out=T16[:, i, j, 0:8], in_=S[:, i, j, :]), reads=[SB[i][j]], writes=[T16B[i]])
                    P.emit("dve", lambda h: h.match_replace(out=tmpS[:, tb], in_to_replace=T16[:, i, j, 0:8], in_values=S[:, i, j, :], imm_value=-1e30),
                           reads=[SB[i][j], T16B[i]], writes=[tmpSB[tb]])
                    P.emit("dve", lambda h: h.max(out=T16[:, i, j, 8:16], in_=tmpS[:, tb]), reads=[tmpSB[tb]], writes=[T16B[i]])
            for i in range(ntl):
                P.emit("dve", lambda h, i=i: h.tensor_tensor(out=cand[:].rearrange("p h (a b) -> p h a b", a=16),
                                                        in0=T16[:, i, 0::2, :].unsqueeze(3).to_broadcast([128, 8, 16, 16]),
                                                        in1=T16[:, i, 1::2, :].unsqueeze(2).to_broadcast([128, 8, 16, 16]), op=ALU.add), reads=[T16B[i]], writes=[candB])
                for hh in range(8):
                    P.emit("dve", lambda h, hh=hh: h.max(out=c24[:, hh, 0:8], in_=cand[:, hh, :]), reads=[candB], writes=[c24B])
                    P.emit("dve", lambda h, hh=hh: h.match_replace(out=cand2[:], in_to_replace=c24[:, hh, 0:8], in_values=cand[:, hh, :], imm_value=-1e30),
                           reads=[candB, c24B], writes=[cand2B])
                    P.emit("dve", lambda h, hh=hh: h.max(out=c24[:, hh, 8:16], in_=cand2[:]), reads=[cand2B], writes=[c24B])
                    P.emit("dve", lambda h, hh=hh: h.match_replace(out=cand2[:], in_to_replace=c24[:, hh, 8:16], in_values=cand2[:], imm_value=-1e30),
                           reads=[cand2B, c24B], writes=[cand2B])
                    P.emit("dve", lambda h, hh=hh: h.max(out=c24[:, hh, 16:24], in_=cand2[:]), reads=[cand2B], writes=[c24B])
                P.emit("dve", lambda h: h.tensor_tensor(out=e16[:], in0=c24[:, :, 0:16], in1=c24[:, :, 0:1].to_broadcast([128, 8, 16]), op=ALU.subtract),
                       reads=[c24B], writes=[e16B])
                P.emit("act", lambda h: h.activation(out=e16[:], in_=e16[:], func=AF.Exp), reads=[e16B], writes=[e16B])
                P.emit("dve", lambda h: h.tensor_reduce(out=zs[:], in_=e16[:], axis=AX.X, op=ALU.add), reads=[e16B], writes=[zsB])
                P.emit("act", lambda h: h.activation(out=zs[:], in_=zs[:], func=AF.Ln), reads=[zsB], writes=[zsB])
                P.emit("dve", lambda h: h.tensor_tensor(out=mb[:], in0=zs[:], in1=c24[:, :, 0], op=ALU.add), reads=[zsB, c24B], writes=[mbB])
                P.emit("dve", lambda h: h.tensor_tensor(out=zs[:], in0=c24[:, :, 15], in1=c24[:, :, 16], op=ALU.add), reads=[c24B, zsB], writes=[zsB])
                P.emit("dve", lambda h, i=i: h.scalar_tensor_tensor(out=TAU[:, i, :], in0=zs[:], scalar=0.5, in1=mb[:], op0=ALU.mult, op1=ALU.subtract),
                       reads=[zsB, mbB], writes=[TAUB[i]])
                P.emit("dve", lambda h, i=i: h.tensor_tensor(out=mb[:], in0=mb[:], in1=TAU[:, i, :], op=ALU.add), reads=[mbB, TAUB[i]], writes=[mbB])
                P.emit("dve", lambda h, i=i: h.tensor_tensor(out=S[:, i, 0::2, :], in0=S[:, i, 0::2, :], in1=mb[:].unsqueeze(2).to_broadcast([128, 8, 128]), op=ALU.subtract),
                       reads=SB[i][0::2] + [mbB], writes=SB[i][0::2])

            def load_group(g):
                b = g % 2
                P.emit("pool", lambda h: h.dma_start(out=UT[b][:], in_=C.ut[layer, :, g * GE:(g + 1) * GE].rearrange("(kc p) e -> p kc e", p=128)),
                       writes=[UTB[b]], dsem=sem_u[b])
                P.emit("pool", lambda h: h.dma_start(out=VG[b][:], in_=C.v[layer, g * GE:(g + 1) * GE, :].rearrange("(ec p) d -> p ec d", p=128)),
                       writes=[VGB[b]], dsem=sem_v[b])

            items = [(g, i) for g in range(NG) for i in range(ntl)]
            po_of = {}
            ptt_of = {}

            def st_tr(k):
                eb = k % 2
                ptt, pttb = ppT.get()
                for ec in range(GN):
                    P.emit("pe", lambda h: h.transpose(out=ptt[:, ec, :], in_=WA[eb][:, ec * 128:(ec + 1) * 128], identity=idb[:]),
                           reads=[WAB[eb], idB], writes=[pttb])
                P.emit("act", lambda h: h.activation(out=WT[eb][:], in_=ptt[:], func=AF.Copy), reads=[pttb], writes=[WTB[eb]])

            def st_s1g(k):
                g, i = items[k]
                eb = k % 2
                P.emit("act", lambda h: h.activation(out=s1g[:, eb], in_=S[:, i, 0::2, g * GN:(g + 1) * GN], func=AF.Copy), reads=SB[i][0::2], writes=[s1gB[eb]])

            def st_front(k):
                g, i = items[k]
                b = g % 2
                eb = k % 2
                Ei, EiB = E[eb], EB[eb]
                Gk, GkB = G[eb], GB[eb]
                pa, pab = ppA.get()
                P.emit("dve", lambda h: h.tensor_tensor(out=Gk[:], in0=S[:, i, 1::2, :].unsqueeze(2).to_broadcast([128, 8, GN, 128]),
                                                        in1=s1g[:, eb].unsqueeze(3).to_broadcast([128, 8, GN, 128]), op=ALU.add),
                       reads=SB[i][1::2] + [s1gB[eb]], writes=GkB[0:9])
                if k + 1 < nit:
                    st_s1g(k + 1)
                for hh in range(8):
                    if hh >= 8 - NPR:
                        P.emit("act", lambda h: h.activation(out=Gk[:, hh], in_=Gk[:, hh], func=AF.Prelu, alpha=1.0e7), reads=[GkB[0]], writes=[GkB[1 + hh]])
                        P.emit("act", lambda h: h.activation(out=Ei[:, hh], in_=Gk[:, hh], func=AF.Exp, bias=TAU[:, i, hh:hh + 1], scale=1.0),
                               reads=[GkB[1 + hh], TAUB[i]], writes=[EiB[hh]])
                    else:
                        P.emit("act", lambda h: h.activation(out=Ei[:, hh], in_=Gk[:, hh], func=AF.Exp, bias=TAU[:, i, hh:hh + 1], scale=1.0),
                               reads=[GkB[0], TAUB[i]], writes=[EiB[hh]])
                for kc in range(8):
                    P.emit("pe", lambda h: h.matmul(pa[:, :], lhsT=hT[:, kc, i * 128:(i + 1) * 128], rhs=UT[b][:, kc, :], start=(kc == 0), stop=(kc == 7)),
                           reads=[hTB, UTB[b]], writes=[pab])
                P.emit("act", lambda h: h.activation(out=A[eb][:], in_=pa[:, :], func=AF.Gelu_apprx_tanh), reads=[pab], writes=[AB[eb]])

            def st_mid(k):
                g, i = items[k]
                eb = k % 2
                Ei, EiB = E[eb], EB[eb]
                Gk, GkB = G[eb], GB[eb]
                nm = 8 - NPR
                P.emit("dve", lambda h: h.scalar_tensor_tensor(out=Ei[:, 0:nm].rearrange("p h a n -> p (h a n)"), in0=Gk[:, 0:nm].rearrange("p h a n -> p (h a n)"), scalar=0.0,
                                                               in1=Ei[:, 0:nm].rearrange("p h a n -> p (h a n)"), op0=ALU.is_ge, op1=ALU.mult),
                       reads=[GkB[0]] + EiB[0:nm], writes=EiB[0:nm])
                P.emit("dve", lambda h: h.tensor_tensor(out=Ei[:, 0:4], in0=Ei[:, 0:4], in1=Ei[:, 4:8], op=ALU.add), reads=EiB[0:8], writes=EiB[0:4])
                P.emit("dve", lambda h: h.tensor_tensor(out=Ei[:, 0:2], in0=Ei[:, 0:2], in1=Ei[:, 2:4], op=ALU.add), reads=EiB[0:4], writes=EiB[0:2])
                P.emit("dve", lambda h: h.tensor_tensor(out=Ei[:, 0], in0=Ei[:, 0], in1=Ei[:, 1], op=ALU.add), reads=EiB[0:2], writes=EiB[0:1])
                P.emit("dve", lambda h: h.tensor_tensor(out=WA[eb][:], in0=A[eb][:], in1=Ei[:, 0].rearrange("p a n -> p (a n)"), op=ALU.mult),
                       reads=[AB[eb], EiB[0]], writes=[WAB[eb]])

            def st_vmm(k):
                g, i = items[k]
                b = g % 2
                eb = k % 2
                po, pob = ppO.get()
                for dh in range(2):
                    if g > 0:
                        P.emit("pe", lambda h: h.matmul(po[:, dh * 512:(dh + 1) * 512], lhsT=idf[:], rhs=O[:, i, dh * 512:(dh + 1) * 512], start=True, stop=False),
                               reads=[idB, OB[i]], writes=[pob])
                    for ec in range(GN):
                        P.emit("pe", lambda h: h.matmul(po[:, dh * 512:(dh + 1) * 512], lhsT=WT[eb][:, ec, :], rhs=VG[b][:, ec, dh * 512:(dh + 1) * 512],
                                                        start=(ec == 0 and g == 0), stop=(ec == GN - 1)),
                               reads=[WTB[eb], VGB[b]], writes=[pob])
                po_of[k] = (po, pob)

            def st_acc(k):
                g, i = items[k]
                po, pob = po_of.pop(k)
                P.emit("act", lambda h: h.activation(out=O[:, i, :], in_=po[:, :], func=AF.Copy), reads=[pob], writes=[OB[i]])

            load_group(0)
            if NG > 1:
                load_group(1)
            nit = len(items)
            st_s1g(0)
            for k in range(nit + 3):
                if k < nit:
                    st_front(k)
                if 0 <= k - 2 < nit:
                    st_acc(k - 2)
                if 0 <= k - 1 < nit:
                    st_mid(k - 1)
                    st_tr(k - 1)
                    st_vmm(k - 1)
                    gk, ik = items[k - 1]
                    if ik == ntl - 1 and gk + 2 < NG:
                        load_group(gk + 2)
            for i, (r0, n) in enumerate(tl):
                loader(xt, xtB, sem_x, n, r0)
                P.emit("dve", lambda h, i=i: h.scalar_tensor_tensor(out=z[:], in0=xt[:], scalar=ALPHA, in1=O[:, i, :], op0=ALU.mult, op1=ALU.add),
                       reads=[xtB, OB[i]], writes=[zB])
                layer_norm_tile(P, C, gtB, z, zB, 128, gt[:, 0, :], gt[:, 1, :], zo, zoB, lnt)
                if not final:
                    P.emit("sp", lambda h, r0=r0, n=n: h.dma_start(out=Hout[r0:r0 + n, :], in_=zo[0:n, :]), reads=[zoB], writes=[HoutB], dsem=sem_o)
                else:
                    for (ro, c, sq, ps) in out_segments(r0, n):
                        P.emit("sp", lambda h, ro=ro, c=c, sq=sq, ps=ps: h.dma_start(out=Hout[sq, ps:ps + c, :], in_=zo[ro:ro + c, :]), reads=[zoB], writes=[HoutB], dsem=sem_o)
        P.wait_all("sp", [(sem_o, sem_o.count)])
        P.flush_block([HinB, HoutB])


def make_ident_named(P, nc, st, pfx):
    idf = st.enter_context(nc.sbuf_tensor(pfx + "identf", [128, 128], F32))
    idb = st.enter_context(nc.sbuf_tensor(pfx + "identb", [128, 128], BF16))
    B = Buf("ident")
    P.emit("pool", lambda h: h.memset(idf[:], 1.0), writes=[B])
    P.emit("pool", lambda h: h.affine_select(out=idf[:], in_=idf[:], pattern=[[-1, 128]], base=0, channel_multiplier=1,
                                              compare_op=ALU.is_equal, fill=0.0), reads=[B], writes=[B])
    P.emit("pool", lambda h: h.tensor_copy(out=idb[:], in_=idf[:]), reads=[B], writes=[B])
    return idf, idb, B


def alloc_ln_tmp_named(nc, st, P, pfx):
    t = {}
    t["stats"] = st.enter_context(nc.sbuf_tensor(pfx + "ln_stats", [128, 2, 6], F32))
    t["statsb"] = Buf("ln_stats")
    t["mv"] = st.enter_context(nc.sbuf_tensor(pfx + "ln_mv", [128, 2], F32))
    t["mvb"] = Buf("ln_mv")
    t["rs"] = st.enter_context(nc.sbuf_tensor(pfx + "ln_rs", [128, 1], F32))
    t["rsb"] = Buf("ln_rs")
    t["eps"] = st.enter_context(nc.sbuf_tensor(pfx + "ln_eps", [128, 1], F32))
    P.emit("pool", lambda h: h.memset(t["eps"][:], EPS), writes=[Buf()])
    return t


def phase_conf(P, C, nc, nseq, Hin, HinB, Hout, HoutB, pfx):
    KW = 31
    PADW = KW // 2
    with contextlib.ExitStack() as st:
        sb = lambda name, shape, dt: st.enter_context(nc.sbuf_tensor(pfx + name, shape, dt))
        idf, idb, idB = make_ident_named(P, nc, st, pfx)
        lnt = alloc_ln_tmp_named(nc, st, P, pfx)
        pp = PsumPool(nc, st, 8, [128, 512], F32, pfx + "ps")
        pf = sb("pf", [128, PF_COLS], F32); pfB = Buf("pf")
        sem_c = P.dsem(pfx + "semc")
        P.emit("sp", lambda h: h.dma_start(out=pf[:], in_=C.pf), writes=[pfB], dsem=sem_c)
        ones = sb("ones", [128, 128], F32); onesB = Buf("ones")
        P.emit("pool", lambda h: h.memset(ones[:], 1.0 / D), writes=[onesB])
        pw2 = sb("pw2", [128, 8, D], BF16); pw2B = Buf("pw2")
        sem_p2 = P.dsem(pfx + "sem_p2")
        for kc in range(8):
            P.emit("pool", lambda h, kc=kc: h.dma_start(out=pw2[:, kc, :], in_=C.pw2[kc * 128:(kc + 1) * 128, :]), writes=[pw2B], dsem=sem_p2)
        gt = sb("lng", [128, 3, D], F32); gtB = Buf("lng")
        sem_g = P.dsem(pfx + "sem_lng")
        for k, nm in enumerate(("mix_g1", "mix_b1", "b_pw2")):
            P.emit("sp", lambda h, k=k, nm=nm: h.dma_start(out=gt[:, k, :], in_=C.pt[PT[nm]:PT[nm] + 1, :].partition_broadcast(128)), writes=[gtB], dsem=sem_g)
        hT = sb("hT", [128, 8, L], BF16); hTB = Buf("hT")
        CV = sb("CV", [128, 8, L], F32); CVB = [Buf("CV%d" % i) for i in range(8)]
        xt = sb("xt", [128, D], F32); xtB = Buf("xt"); sem_x = P.dsem(pfx + "sem_x")
        win = [sb("win%d" % i, [128, 8, 256], BF16) for i in range(2)]
        winB = [Buf("win%d" % i) for i in range(2)]
        sem_win = [P.dsem(pfx + "sem_win%d" % i) for i in range(2)]
        sig = sb("sig", [128, L], F32); sigB = Buf("sig")
        gpad = sb("gpad", [128, L + 2 * PADW], F32); gpB = Buf("gpad")
        P.emit("pool", lambda h: h.memset(gpad[:], 0.0), writes=[gpB])
        sq = [sb("sq%d" % i, [128, 512], F32) for i in range(2)]; sqB = [Buf("sq%d" % i) for i in range(2)]
        mean = sb("mean", [128, 512], F32); meanB = Buf("mean")
        rstd = sb("rstd", [128, 512], F32); rstdB = Buf("rstd")
        tq = [sb("tq%d" % i, [128, 512], F32) for i in range(2)]; tqB = [Buf("tq%d" % i) for i in range(2)]
        z = sb("z", [128, D], F32); zB = Buf("z")
        zo = sb("zo", [128, D], F32); zoB = Buf("zo")
        sem_o = P.dsem(pfx + "sem_o")
        xt2 = sb("xt2", [128, D], F32); xt2B = Buf("xt2"); sem_x2 = P.dsem(pfx + "sem_x2")
        z2 = sb("z2", [128, D], F32); z2B = Buf("z2")
        zo2 = sb("zo2", [128, D], F32); zo2B = Buf("zo2")
        sem_o2 = P.dsem(pfx + "sem_o2")
        lnt2 = alloc_ln_tmp_named(nc, st, P, pfx + "2_")
        LNB = [(xt, xtB, sem_x, z, zB, zo, zoB, sem_o, lnt), (xt2, xt2B, sem_x2, z2, z2B, zo2, zo2B, sem_o2, lnt2)]
        pcs = pieces(L)
        wcount = 0
        sqc = 0
        for s in range(nseq):
            def loader(xt_, xtb_, sem_, n, p0, s=s):
                r0 = s * L + p0
                P.emit("sp", lambda h: h.dma_start(out=xt_[0:n, :], in_=Hin[r0:r0 + n, :]), reads=[HinB], writes=[xtb_], dsem=sem_)
            load_tokens_T(P, C, nc, hT, hTB, xt, xtB, sem_x, idf, idB, pp, loader, [(p0, n, (p0,)) for (p0, n) in pos_tiles()])
            for c in range(8):
                wi = wcount % 2
                wcount += 1
                w = win[wi]
                P.emit("pool", lambda h: h.dma_start(out=w[:, :, 0:128], in_=C.pw1[:, c * 128:(c + 1) * 128].rearrange("(kc p) n -> p kc n", p=128)),
                       writes=[winB[wi]], dsem=sem_win[wi])
                P.emit("pool", lambda h: h.dma_start(out=w[:, :, 128:256], in_=C.pw1[:, D + c * 128:D + (c + 1) * 128].rearrange("(kc p) n -> p kc n", p=128)),
                       writes=[winB[wi]], dsem=sem_win[wi])
                for (t0, tn) in pcs:
                    pa, pab = pp.get()
                    pg, pgb = pp.get()
                    for kc in range(8):
                        P.emit("pe", lambda h: h.matmul(pg[:, 0:tn], lhsT=w[:, kc, 128:256], rhs=hT[:, kc, t0:t0 + tn], start=(kc == 0), stop=(kc == 7)),
                               reads=[winB[wi], hTB], writes=[pgb])
                    for kc in range(8):
                        P.emit("pe", lambda h: h.matmul(pa[:, 0:tn], lhsT=w[:, kc, 0:128], rhs=hT[:, kc, t0:t0 + tn], start=(kc == 0), stop=(kc == 7)),
                               reads=[winB[wi], hTB], writes=[pab])
                    bg = PF["b_pw1"] + 8 + c
                    ba = PF["b_pw1"] + c
                    P.emit("act", lambda h: h.activation(out=sig[:, t0:t0 + tn], in_=pg[:, 0:tn], func=AF.Sigmoid, bias=pf[:, bg:bg + 1], scale=1.0),
                           reads=[pgb, pfB], writes=[sigB])
                    P.emit("dve", lambda h: h.scalar_tensor_tensor(out=gpad[:, PADW + t0:PADW + t0 + tn], in0=pa[:, 0:tn], scalar=pf[:, ba:ba + 1], in1=sig[:, t0:t0 + tn],
                                                                   op0=ALU.add, op1=ALU.mult), reads=[pab, pfB, sigB], writes=[gpB])
                dw = PF["dw_w"]
                db = PF["dw_b"] + c
                P.emit("dve", lambda h: h.tensor_scalar(out=CV[:, c, :], in0=gpad[:, 0:L], scalar1=pf[:, dw + c:dw + c + 1], scalar2=pf[:, db:db + 1],
                                                        op0=ALU.mult, op1=ALU.add), reads=[gpB, pfB], writes=[CVB[c]])
                for k in range(1, KW):
                    P.emit("dve", lambda h: h.scalar_tensor_tensor(out=CV[:, c, :], in0=gpad[:, k:k + L], scalar=pf[:, dw + k * 8 + c:dw + k * 8 + c + 1],
                                                                   in1=CV[:, c, :], op0=ALU.mult, op1=ALU.add), reads=[gpB, pfB, CVB[c]], writes=[CVB[c]])
            Yc, YcB = hT, hTB
            for (t0, tn) in pcs:
                pm, pmb = pp.get()
                pq, pqb = pp.get()
                for c in range(8):
                    P.emit("pe", lambda h: h.matmul(pm[:, 0:tn], lhsT=ones[:], rhs=CV[:, c, t0:t0 + tn], start=(c == 0), stop=(c == 7)),
                           reads=[onesB, CVB[c]], writes=[pmb])
                for c in range(8):
                    si = sqc % 2
                    sqc += 1
                    P.emit("act", lambda h: h.activation(out=sq[si][:, 0:tn], in_=CV[:, c, t0:t0 + tn], func=AF.Square), reads=[CVB[c]], writes=[sqB[si]])
                    P.emit("pe", lambda h: h.matmul(pq[:, 0:tn], lhsT=ones[:], rhs=sq[si][:, 0:tn], start=(c == 0), stop=(c == 7)),
                           reads=[onesB, sqB[si]], writes=[pqb])
                P.emit("act", lambda h: h.activation(out=mean[:, 0:tn], in_=pm[:, 0:tn], func=AF.Copy), reads=[pmb], writes=[meanB])
                P.emit("dve", lambda h: h.tensor_tensor(out=rstd[:, 0:tn], in0=mean[:, 0:tn], in1=mean[:, 0:tn], op=ALU.mult), reads=[meanB], writes=[rstdB])
                P.emit("dve", lambda h: h.tensor_tensor(out=rstd[:, 0:tn], in0=pq[:, 0:tn], in1=rstd[:, 0:tn], op=ALU.subtract), reads=[pqb, rstdB], writes=[rstdB])
                P.emit("act", lambda h: h.activation(out=rstd[:, 0:tn], in_=rstd[:, 0:tn], func=AF.Sqrt, bias=lnt["eps"][:, :], scale=1.0), reads=[rstdB], writes=[rstdB])
                P.emit("dve", lambda h: h.reciprocal(out=rstd[:, 0:tn], in_=rstd[:, 0:tn]), reads=[rstdB], writes=[rstdB])
                for c in range(8):
                    ti = c % 2
                    P.emit("dve", lambda h: h.tensor_tensor(out=tq[ti][:, 0:tn], in0=CV[:, c, t0:t0 + tn], in1=mean[:, 0:tn], op=ALU.subtract),
                           reads=[CVB[c], meanB], writes=[tqB[ti]])
                    P.emit("pool", lambda h: h.tensor_tensor(out=tq[ti][:, 0:tn], in0=tq[ti][:, 0:tn], in1=rstd[:, 0:tn], op=ALU.mult),
                           reads=[tqB[ti], rstdB], writes=[tqB[ti]])
                    gcol = PF["cln_g"] + c
                    bcol = PF["cln_b"] + c
                    P.emit("act", lambda h: h.activation(out=Yc[:, c, t0:t0 + tn], in_=tq[ti][:, 0:tn], func=AF.Silu, bias=pf[:, bcol:bcol + 1], scale=pf[:, gcol:gcol + 1]),
                           reads=[tqB[ti], pfB], writes=[YcB])
            for ti, (p0, n) in enumerate(pos_tiles()):
                (xt_, xtB_, sem_x_, z_, zB_, zo_, zoB_, sem_o_, lnt_) = LNB[ti % 2]
                loader(xt_, xtB_, sem_x_, n, p0)
                for half in range(2):
                    pt, pb = pp.get()
                    for kc in range(8):
                        P.emit("pe", lambda h: h.matmul(pt[0:n, :], lhsT=Yc[:, kc, p0:p0 + n], rhs=pw2[:, kc, half * 512:(half + 1) * 512], start=(kc == 0), stop=(kc == 7)),
                               reads=[YcB, pw2B], writes=[pb])
                    P.emit("dve", lambda h: h.scalar_tensor_tensor(out=z_[0:n, half * 512:(half + 1) * 512], in0=xt_[0:n, half * 512:(half + 1) * 512],
                                                                   scalar=ALPHA, in1=pt[0:n, :], op0=ALU.mult, op1=ALU.add), reads=[xtB_, pb], writes=[zB_])
                P.emit("pool", lambda h: h.tensor_tensor(out=z_[0:n, :], in0=z_[0:n, :], in1=gt[0:n, 2, :], op=ALU.add), reads=[zB_, gtB], writes=[zB_])
                layer_norm_tile(P, C, gtB, z_, zB_, n, gt[:, 0, :], gt[:, 1, :], zo_, zoB_, lnt_)
                r0 = s * L + p0
                P.emit("sp", lambda h: h.dma_start(out=Hout[r0:r0 + n, :], in_=zo_[0:n, :]), reads=[zoB_], writes=[HoutB], dsem=sem_o_)
        P.wait_all("sp", [(sem_o, sem_o.count), (sem_o2, sem_o2.count)])
        P.flush_block([HinB, HoutB])
```
